# Optimizing a Trainium2 kernel written in Bass

```python
import jax, jax.numpy as jnp
from jax import lax
import numpy as np

D_MODEL = 2048
BATCH = 4
SEQ = 2048
DEPTH = 2

BRANCH_WIDTH = 1024
N_BRANCH = 3
CONV_A_WIDTH = 3
POOL_WINDOWS = (2, 4, 8, 16)
N_POOL_GROUPS = len(POOL_WINDOWS)
POOL_GROUP_WIDTH = BRANCH_WIDTH // N_POOL_GROUPS
CONV_C_WIDTH = 31
PLE_DIM = 256
RMS_EPS = 1e-6
LN_EPS = 1e-5
N_BRANCH_SLICES = 9
IN_COLS = N_BRANCH_SLICES * BRANCH_WIDTH + N_BRANCH * D_MODEL

kernel_name = "hybrid_conv_pool_conformer_gated_merge"


def rms_norm(x, g):
    xf = x.astype(jnp.float32)
    y = xf * lax.rsqrt(jnp.mean(xf * xf, axis=-1, keepdims=True) + RMS_EPS)
    return (y * g.astype(jnp.float32)).astype(x.dtype)


def layer_norm(x, g, b):
    xf = x.astype(jnp.float32)
    mu = jnp.mean(xf, axis=-1, keepdims=True)
    var = jnp.mean(jnp.square(xf - mu), axis=-1, keepdims=True)
    y = (xf - mu) * lax.rsqrt(var + LN_EPS)
    return (y * g.astype(jnp.float32) + b.astype(jnp.float32)).astype(x.dtype)


def causal_dwconv(x, w):
    k = w.shape[0]
    return lax.conv_general_dilated(
        x, w.astype(x.dtype)[:, None, :], window_strides=(1,),
        padding=[(k - 1, 0)], dimension_numbers=("NWC", "WIO", "NWC"),
        feature_group_count=x.shape[-1])


def causal_multiscale_pool(u):
    s = u.shape[1]
    uf = u.astype(jnp.float32)
    cs = jnp.cumsum(uf, axis=1)
    t = jnp.arange(1, s + 1, dtype=jnp.float32)
    outs = []
    for g, w in enumerate(POOL_WINDOWS):
        c = cs[:, :, g]
        shifted = jnp.pad(c, ((0, 0), (w, 0), (0, 0)))[:, :s]
        cnt = jnp.minimum(t, jnp.float32(w))[None, :, None]
        outs.append((c - shifted) / cnt)
    pooled = jnp.stack(outs, axis=2)
    return (pooled - uf).astype(u.dtype)


def hybrid_layer(h, p_i, norm_g, w_in, conv_a_w, pool_w, pool_scale, conv_c_w, conv_c_b,
                 ln_c_g, ln_c_b, w_branch_out, w_o, w_ple_gate, w_ple_proj):
    b, s, d = h.shape
    e = BRANCH_WIDTH
    xn = rms_norm(h, norm_g)
    proj = jnp.einsum("bsd,dc->bsc", xn, w_in)
    split_pts = [e * i for i in range(1, N_BRANCH_SLICES + 1)]
    a_in, a_b, a_c, a_z, b_in, b_z, c_val, c_gate, c_z, gates = jnp.split(proj, split_pts, axis=-1)

    y_a = a_b * causal_dwconv(a_c * a_in, conv_a_w) * jax.nn.silu(a_z)

    pooled = causal_multiscale_pool(b_in.reshape(b, s, N_POOL_GROUPS, POOL_GROUP_WIDTH))
    y_b = jnp.einsum("bsgi,gio->bsgo", pooled, pool_w).reshape(b, s, e)
    y_b = y_b * pool_scale * jax.nn.silu(b_z)

    v = c_val * jax.nn.sigmoid(c_gate)
    v = causal_dwconv(v, conv_c_w) + conv_c_b
    v = layer_norm(v, ln_c_g, ln_c_b)
    y_c = jax.nn.silu(v) * jax.nn.silu(c_z)

    branches = jnp.stack([y_a, y_b, y_c], axis=2)
    up = jnp.einsum("bsne,ned->bsnd", branches, w_branch_out)
    g = jax.nn.sigmoid(gates.reshape(b, s, N_BRANCH, d))
    merged = jnp.sum(g * up, axis=2)
    h = h + jnp.einsum("bsd,de->bse", merged, w_o)

    ple = jnp.einsum("bsp,pd->bsd", p_i, w_ple_proj)
    h = h + jax.nn.sigmoid(jnp.einsum("bsd,de->bse", h, w_ple_gate)) * ple
    return h


def setup_inputs(seed: int = 0) -> dict:
    key = jax.random.key(seed)
    ks = jax.random.split(key, 16)
    f32 = jnp.float32
    d, e = D_MODEL, BRANCH_WIDTH
    nrm = lambda k, shape, scale: jax.random.normal(k, shape, f32) * scale
    return {
        "x": nrm(ks[0], (BATCH, SEQ, d), 1.0),
        "p": nrm(ks[1], (DEPTH, BATCH, SEQ, PLE_DIM), 1.0),
        "norm_g": 1.0 + nrm(ks[2], (DEPTH, d), 0.05),
        "w_in": nrm(ks[3], (DEPTH, d, IN_COLS), d ** -0.5),
        "conv_a_w": nrm(ks[4], (DEPTH, CONV_A_WIDTH, e), CONV_A_WIDTH ** -0.5),
        "pool_w": nrm(ks[5], (DEPTH, N_POOL_GROUPS, POOL_GROUP_WIDTH, POOL_GROUP_WIDTH), POOL_GROUP_WIDTH ** -0.5),
        "pool_scale": 1.0 + nrm(ks[6], (DEPTH, e), 0.1),
        "conv_c_w": nrm(ks[7], (DEPTH, CONV_C_WIDTH, e), CONV_C_WIDTH ** -0.5),
        "conv_c_b": nrm(ks[8], (DEPTH, e), 0.02),
        "ln_c_g": 1.0 + nrm(ks[9], (DEPTH, e), 0.05),
        "ln_c_b": nrm(ks[10], (DEPTH, e), 0.02),
        "w_branch_out": nrm(ks[11], (DEPTH, N_BRANCH, e, d), e ** -0.5),
        "w_o": nrm(ks[12], (DEPTH, d, d), d ** -0.5),
        "w_ple_gate": nrm(ks[13], (DEPTH, d, d), d ** -0.5),
        "w_ple_proj": nrm(ks[14], (DEPTH, PLE_DIM, d), PLE_DIM ** -0.5),
        "final_norm_g": 1.0 + nrm(ks[15], (d,), 0.05),
    }


def reference(x, p, norm_g, w_in, conv_a_w, pool_w, pool_scale, conv_c_w, conv_c_b,
              ln_c_g, ln_c_b, w_branch_out, w_o, w_ple_gate, w_ple_proj, final_norm_g):
    h = x
    for i in range(DEPTH):
        h = hybrid_layer(h, p[i], norm_g[i], w_in[i], conv_a_w[i], pool_w[i], pool_scale[i],
                         conv_c_w[i], conv_c_b[i], ln_c_g[i], ln_c_b[i], w_branch_out[i],
                         w_o[i], w_ple_gate[i], w_ple_proj[i])
    return rms_norm(h, final_norm_g)
```

```python
from contextlib import ExitStack

import numpy as np
import concourse.bass as bass
import concourse.mybir as mybir
from concourse.bass_utils import run_bass_kernel_spmd

F32 = mybir.dt.float32
BF16 = mybir.dt.bfloat16
AF = mybir.ActivationFunctionType
ALU = mybir.AluOpType

D = 2048
E = 1024
SEQ = 2048
BATCH = 4
DEPTH = 2
PLE = 256
IN_COLS = 15360
HALO = 62
TOK = 1024
TL = TOK + HALO
NCORES = 8
RMS_EPS = 1e-6
LN_EPS = 1e-5
NPAR = 48 + 2 * 304
ENGS = ("pe", "act", "dve", "pool", "sp")


class Res:
    __slots__ = ("name", "writer", "readers")

    def __init__(self, name):
        self.name = name
        self.writer = None
        self.readers = {}


class Op:
    __slots__ = ("eng", "meth", "kw", "deps", "is_dma", "dsem", "dcount", "signal", "sigval")

    def __init__(self, eng, meth, kw):
        self.eng = eng
        self.meth = meth
        self.kw = kw
        self.deps = []
        self.is_dma = False
        self.dsem = None
        self.dcount = 0
        self.signal = False
        self.sigval = 0


class Sched:
    def __init__(self, nc):
        self.nc = nc
        self.ops = {e: [] for e in ENGS}
        self.dma_counts = {}

    def _add_dep(self, op, dep):
        if dep is None or dep is op:
            return
        if (not dep.is_dma) and (not op.is_dma) and dep.eng == "pe" and op.eng == "pe":
            return
        op.deps.append(dep)
        if not dep.is_dma:
            dep.signal = True

    def op(self, eng, meth, kw=None, reads=(), writes=(), dma_sem=None):
        o = Op(eng, meth, kw)
        if dma_sem is not None:
            o.is_dma = True
            o.dsem = dma_sem
            self.dma_counts[dma_sem] = self.dma_counts.get(dma_sem, 0) + 16
            o.dcount = self.dma_counts[dma_sem]
        for r in reads:
            self._add_dep(o, r.writer)
        for w in writes:
            self._add_dep(o, w.writer)
            for rd in w.readers.values():
                self._add_dep(o, rd)
        for w in writes:
            w.writer = o
            w.readers = {}
        for r in reads:
            r.readers[id(o) if o.is_dma else o.eng] = o
        self.ops[eng].append(o)
        return o

    def emit(self, sems, dma_sems):
        nc = self.nc
        for e in ENGS:
            c = 0
            for o in self.ops[e]:
                if (not o.is_dma) and o.signal:
                    c += 1
                    o.sigval = c
        engobj = {"pe": "tensor", "act": "scalar", "dve": "vector", "pool": "gpsimd", "sp": "sync"}
        with nc.Block() as block:
            for e in ENGS:
                ops = self.ops[e]
                if not ops:
                    continue

                def body(eng, ops=ops, e=e):
                    waited = {}
                    for o in ops:
                        need = {}
                        for d in o.deps:
                            if d.is_dma:
                                key, val = ("d", d.dsem), d.dcount
                            else:
                                key, val = ("e", d.eng), d.sigval
                            if val > need.get(key, 0):
                                need[key] = val
                        todo = []
                        for key, val in need.items():
                            if waited.get(key, 0) >= val:
                                continue
                            waited[key] = val
                            todo.append((key, val))
                        emb = None
                        if o.meth is not None and not o.is_dma:
                            for t in todo:
                                if t[0][0] == "e":
                                    emb = t
                            if emb is not None:
                                todo.remove(emb)
                        for key, val in todo:
                            sem = dma_sems[key[1]] if key[0] == "d" else sems[key[1]]
                            eng.wait_ge(sem, val)
                        if o.meth is None:
                            continue
                        ins = getattr(eng, o.meth)(**o.kw)
                        if emb is not None:
                            ins._wait_ge(sems[emb[0][1]], emb[1])
                        if o.is_dma:
                            ins.then_inc(dma_sems[o.dsem], 16)
                        elif o.signal:
                            ins.then_inc(sems[e], 1)

                getattr(block, engobj[e])(body)


class Rng:
    def __init__(self, lo, n, nt):
        assert lo + n * nt == TL and n <= 512
        self.lo, self.n, self.nt = lo, n, nt


R_L = [
    (Rng(0, 362, 3), Rng(30, 352, 3)),
    (Rng(30, 352, 3), Rng(62, 512, 2)),
]
R_FINAL = Rng(62, 512, 2)


def _par_base(l):
    return 48 + l * 304


def build_program(depth_run=DEPTH):
    nc = bass.Bass("TRN2", target_bir_lowering=False)
    xT = nc.dram_tensor("xT", [D, TL], F32, kind="ExternalInput").ap()
    pT = nc.dram_tensor("pT", [DEPTH, PLE, TL], F32, kind="ExternalInput").ap()
    par_d = nc.dram_tensor("par", [128, NPAR], F32, kind="ExternalInput").ap()
    icnt_d = nc.dram_tensor("icnt", [128, 64], F32, kind="ExternalInput").ap()
    ident_d = nc.dram_tensor("ident", [128, 128], F32, kind="ExternalInput").ap()
    w_in = nc.dram_tensor("w_in", [DEPTH, D, IN_COLS], F32, kind="ExternalInput").ap()
    pool_w = nc.dram_tensor("pool_w", [DEPTH, 4, 256, 256], F32, kind="ExternalInput").ap()
    w_bo = nc.dram_tensor("w_branch_out", [DEPTH, 3, E, D], F32, kind="ExternalInput").ap()
    w_o = nc.dram_tensor("w_o", [DEPTH, D, D], F32, kind="ExternalInput").ap()
    w_pg = nc.dram_tensor("w_ple_gate", [DEPTH, D, D], F32, kind="ExternalInput").ap()
    w_pp = nc.dram_tensor("w_ple_proj", [DEPTH, PLE, D], F32, kind="ExternalInput").ap()
    outT = nc.dram_tensor("outT", [D, TOK], F32, kind="ExternalOutput").ap()

    S = Sched(nc)
    dma_names = ["x0", "x1", "x2", "x3", "par", "ident", "p", "ring0", "ring1", "side0", "side1", "out0", "out1", "out2", "out3"]

    def ACT(out, in_, func, reads, writes, **kw):
        S.op("act", "activation", dict(out=out, in_=in_, func=func, **kw), reads, writes)

    def TT(out, in0, in1, op, reads, writes):
        S.op("dve", "tensor_tensor", dict(out=out, in0=in0, in1=in1, op=op), reads, writes)

    def TS(out, in0, s1, s2, op0, op1, reads, writes):
        kw = dict(out=out, in0=in0, scalar1=s1, scalar2=s2, op0=op0)
        if op1 is not None:
            kw["op1"] = op1
        S.op("dve", "tensor_scalar", kw, reads, writes)

    def STT(out, in0, scalar, in1, op0, op1, reads, writes):
        S.op("dve", "scalar_tensor_tensor", dict(out=out, in0=in0, scalar=scalar, in1=in1, op0=op0, op1=op1), reads, writes)

    def RECIP(out, in_, reads, writes):
        S.op("dve", "reciprocal", dict(out=out, in_=in_), reads, writes)

    def MM(out, lhsT, rhs, start, stop, reads, writes):
        S.op("pe", "matmul", dict(out=out, lhsT=lhsT, rhs=rhs, start=start, stop=stop), reads, writes)

    def DMA(eng, out, in_, reads, writes, sem):
        S.op(eng, "dma_start", dict(out=out, in_=in_), reads, writes, dma_sem=sem)

    with ExitStack() as es:
        ec = es.enter_context
        h = ec(nc.sbuf_tensor("h", [128, 16, TL], F32))
        xn = ec(nc.sbuf_tensor("xn", [128, 16, TL], BF16))
        acc = ec(nc.sbuf_tensor("acc", [128, 16, TL], BF16))
        yb = ec(nc.sbuf_tensor("yb", [128, 8, TL], BF16))
        ring = ec(nc.sbuf_tensor("ring", [128, 2, 16, 256], BF16))
        side = ec(nc.sbuf_tensor("side", [128, 2, 2, 256], BF16))
        dgb = ec(nc.sbuf_tensor("dgb", [128, 31 * 128], BF16))
        par = ec(nc.sbuf_tensor("par_s", [128, NPAR], F32))
        icnt = ec(nc.sbuf_tensor("icnt_s", [128, 64], F32))
        ident = ec(nc.sbuf_tensor("ident_s", [128, 128], BF16))
        ones = ec(nc.sbuf_tensor("ones_s", [128, 128], BF16))
        tmp16 = ec(nc.sbuf_tensor("tmp16", [128, 16], F32))
        Tt = ec(nc.sbuf_tensor("Tt", [128, 6, TL], F32))
        psa = ec(nc.psum_tensor("psa", [128, 8, 512], F32))
        s_pe, s_act, s_dve, s_pool, s_sp = [ec(nc.semaphore(n)) for n in ("s_pe", "s_act", "s_dve", "s_pool", "s_sp")]
        dlist = [ec(nc.semaphore(f"d{i}")) for i in range(len(dma_names))]
        dsems = dict(zip(dma_names, dlist))
        r_bank = [Res(f"bank{i}") for i in range(8)]
        r_h = [Res(f"h{k}") for k in range(16)]
        r_xn = [Res(f"xn{k}") for k in range(16)]
        r_acc = [Res(f"acc{k}") for k in range(16)]
        r_yb = [Res(f"yb{k}") for k in range(8)]
        r_ring = [Res("ring0"), Res("ring1")]
        r_side = [Res("side0"), Res("side1")]
        r_dgk = [Res(f"dg{k}") for k in range(31)]
        r_par = Res("par")
        r_ident = Res("ident")
        r_ones = Res("ones")
        r_tmp16 = Res("tmp16")
        r_Ta = [Res(f"T{i}a") for i in range(6)]
        r_Tb = [Res(f"T{i}b") for i in range(6)]
        r_out = [Res(f"out{k}") for k in range(16)]

        dg = dgb[:, :].rearrange("p (k c) -> p k c", c=128)
        pTs = dgb[:, 0:2 * TL].rearrange("p (k t) -> p k t", t=TL)

        def TF(i):
            return Tt[:, i, :], [r_Ta[i], r_Tb[i]]

        def TB(i, half):
            v = Tt[:, i, :].bitcast(BF16)
            return v[:, half * TL:(half + 1) * TL], [r_Ta[i] if half == 0 else r_Tb[i]]

        def v3(ap2d, R):
            return ap2d[:, R.lo:TL].rearrange("p (j n) -> p j n", n=R.n)

        class PGrp:
            def __init__(self, base, nt):
                self.base, self.nt = base, nt
                self.res = r_bank[base:base + nt]

        def pv(g, R):
            return psa[:, g.base:g.base + R.nt, 0:R.n]

        def pcol(c):
            return par[:, c:c + 1]

        state = {"pg": 0, "ring": 0, "side": 0}

        def next_pg(nt):
            b = state["pg"]
            if b + nt > 8:
                b = 0
            state["pg"] = b + nt
            return PGrp(b, nt)

        def load_main(src2d, kch):
            s = state["ring"]
            state["ring"] ^= 1
            DMA("pool", ring[:, s, 0:kch, :], src2d.rearrange("(k p) c -> p k c", p=128), [], [r_ring[s]], f"ring{s}")
            return s

        def load_main512(src2d):
            s = state["ring"]
            state["ring"] ^= 1
            dst = ring[:, s, :, :].rearrange("p (k a) c -> p k (a c)", a=2)
            DMA("pool", dst, src2d.rearrange("(k p) c -> p k c", p=128), [], [r_ring[s]], f"ring{s}")
            return s

        def load_side(src2d):
            s = state["side"]
            state["side"] ^= 1
            DMA("pool", side[:, s, :, :], src2d.rearrange("(k p) c -> p k c", p=128), [], [r_side[s]], f"side{s}")
            return s

        def mm_group(steps, R):
            g = next_pg(R.nt)
            ns = len(steps)
            for i, (lhsT, rhs2d, shift, reads) in enumerate(steps):
                for tt in range(R.nt):
                    lo = R.lo + tt * R.n + shift
                    MM(psa[:, g.base + tt, 0:R.n], lhsT, rhs2d[:, lo:lo + R.n], (i == 0), (i == ns - 1), reads, g.res)
            return g

        def proj_steps(slot, coff, src, r_src):
            return [(ring[:, slot, k, coff:coff + 128], src[:, k, :], 0, [r_ring[slot], r_src[k]]) for k in range(16)]

        DMA("sp", par[:, :], par_d, [], [r_par], "par")
        DMA("sp", icnt[:, :], icnt_d, [], [r_par], "par")
        for i4 in range(4):
            DMA("sp", h[:, 4 * i4:4 * i4 + 4, :], xT[512 * i4:512 * i4 + 512, :].rearrange("(k p) t -> p k t", p=128), [],
                r_h[4 * i4:4 * i4 + 4], f"x{i4}")
        DMA("pool", ident[:, :], ident_d, [], [r_ident], "ident")
        S.op("dve", "memset", dict(ap=ones[:, :], constant=1.0), [], [r_ones])

        def rms_stats(R):
            g = next_pg(R.nt)
            for k in range(16):
                sq, r_sq = TB(5, k % 2)
                ACT(sq[:, R.lo:TL], h[:, k, R.lo:TL], AF.Square, [r_h[k]], r_sq)
                for tt in range(R.nt):
                    lo = R.lo + tt * R.n
                    MM(psa[:, g.base + tt, 0:R.n], ones[:, :], sq[:, lo:lo + R.n], (k == 0), (k == 15), [r_ones] + r_sq, g.res)
            rs, r_rs = TF(0)
            TS(v3(rs, R), pv(g, R), 1.0 / D, RMS_EPS, ALU.mult, ALU.add, g.res, r_rs)
            ACT(rs[:, R.lo:TL], rs[:, R.lo:TL], AF.Ln, r_rs, r_rs)
            ACT(rs[:, R.lo:TL], rs[:, R.lo:TL], AF.Exp, r_rs, r_rs, scale=-0.5)
            return rs, r_rs

        for l in range(depth_run):
            R_in, R_out = R_L[l]
            PB = _par_base(l)
            wl = w_in[l]
            LO = R_out.lo

            rs, r_rs = rms_stats(R_in)
            for k in range(16):
                STT(xn[:, k, R_in.lo:TL], h[:, k, R_in.lo:TL], pcol(l * 16 + k), rs[:, R_in.lo:TL], ALU.mult, ALU.mult,
                    [r_h[k], r_par] + r_rs, [r_xn[k]])

            def merge_branch(b):
                for q in range(4):
                    sgs = []
                    for hpair in range(2):
                        d0c = q * 4 + hpair * 2
                        c0 = 9216 + b * 2048 + d0c * 128
                        slot = load_main(wl[:, c0:c0 + 256], 16)
                        for jj in range(2):
                            g = mm_group(proj_steps(slot, jj * 128, xn, r_xn), R_out)
                            sg, r_sg = TF(hpair * 2 + jj)
                            ACT(v3(sg, R_out), pv(g, R_out), AF.Sigmoid, g.res, r_sg)
                            sgs.append((sg, r_sg))
                    slot = load_main512(w_bo[l, b][:, q * 512:(q + 1) * 512])
                    wv = ring[:, slot, :, :].rearrange("p (k a) c -> p k (a c)", a=2)
                    for jj in range(4):
                        d = q * 4 + jj
                        steps = [(wv[:, k, jj * 128:(jj + 1) * 128], yb[:, k, :], 0, [r_ring[slot], r_yb[k]]) for k in range(8)]
                        g = mm_group(steps, R_out)
                        sg, r_sg = sgs[jj]
                        if b == 0:
                            TT(v3(acc[:, d, :], R_out), pv(g, R_out), v3(sg, R_out), ALU.mult, g.res + r_sg, [r_acc[d]])
                        else:
                            t, r_t = TF(4 + (jj % 2))
                            TT(v3(t, R_out), pv(g, R_out), v3(sg, R_out), ALU.mult, g.res + r_sg, r_t)
                            TT(acc[:, d, LO:TL], acc[:, d, LO:TL], t[:, LO:TL], ALU.add, [r_acc[d]] + r_t, [r_acc[d]])

            for j in range(4):
                u = [2 * j, 2 * j + 1]
                tA = [TF(0), TF(1)]
                cv = [TF(2), TF(3)]
                slot = load_main(wl[:, 0 * E + u[0] * 128: 0 * E + u[0] * 128 + 256], 16)
                for i in range(2):
                    g = mm_group(proj_steps(slot, i * 128, xn, r_xn), R_in)
                    ACT(v3(tA[i][0], R_in), pv(g, R_in), AF.Copy, g.res, tA[i][1])
                slot = load_main(wl[:, 2 * E + u[0] * 128: 2 * E + u[0] * 128 + 256], 16)
                for i in range(2):
                    g = mm_group(proj_steps(slot, i * 128, xn, r_xn), R_in)
                    cx, r_cx = tA[i]
                    co, r_co = cv[i]
                    TT(v3(cx, R_in), pv(g, R_in), v3(cx, R_in), ALU.mult, g.res + r_cx, r_cx)
                    wc = PB + u[i] * 3
                    TS(co[:, LO:TL], cx[:, LO:TL], pcol(wc + 2), None, ALU.mult, None, r_cx + [r_par], r_co)
                    for sh in (1, 2):
                        STT(co[:, LO:TL], cx[:, LO - sh:TL - sh], pcol(wc + 2 - sh), co[:, LO:TL], ALU.mult, ALU.add,
                            r_cx + r_co + [r_par], r_co)
                slot = load_main(wl[:, 1 * E + u[0] * 128: 1 * E + u[0] * 128 + 256], 16)
                for i in range(2):
                    g = mm_group(proj_steps(slot, i * 128, xn, r_xn), R_out)
                    co, r_co = cv[i]
                    TT(v3(co, R_out), pv(g, R_out), v3(co, R_out), ALU.mult, g.res + r_co, r_co)
                slot = load_main(wl[:, 3 * E + u[0] * 128: 3 * E + u[0] * 128 + 256], 16)
                for i in range(2):
                    g = mm_group(proj_steps(slot, i * 128, xn, r_xn), R_out)
                    sz, r_sz = tA[i]
                    co, r_co = cv[i]
                    ACT(v3(sz, R_out), pv(g, R_out), AF.Silu, g.res, r_sz)
                    TT(yb[:, u[i], LO:TL], co[:, LO:TL], sz[:, LO:TL], ALU.mult, r_co + r_sz, [r_yb[u[i]]])
            merge_branch(0)

            for j in range(4):
                u = [2 * j, 2 * j + 1]
                wlen = 2 ** (j + 1)
                U = [TF(0), TF(1)]
                XY = [TF(2), TF(3)]
                P = [TB(4, 0), TB(4, 1)]
                slot = load_main(wl[:, 4 * E + u[0] * 128: 4 * E + u[0] * 128 + 256], 16)
                for i in range(2):
                    g = mm_group(proj_steps(slot, i * 128, xn, r_xn), R_in)
                    Ui, r_Ui = U[i]
                    ACT(v3(Ui, R_in), pv(g, R_in), AF.Copy, g.res, r_Ui)
                    src, r_src = Ui, r_Ui
                    lo_valid = R_in.lo
                    sh = 1
                    st = 0
                    while sh < wlen:
                        dst, r_dst = XY[st % 2]
                        lo_new = lo_valid + sh
                        TT(dst[:, lo_new:TL], src[:, lo_new:TL], src[:, lo_new - sh:TL - sh], ALU.add, r_src, r_dst)
                        src, r_src = dst, r_dst
                        lo_valid = lo_new
                        sh *= 2
                        st += 1
                    assert lo_valid <= LO
                    Pi, r_Pi = P[i]
                    STT(Pi[:, LO:TL], src[:, LO:TL], 1.0 / wlen, Ui[:, LO:TL], ALU.mult, ALU.subtract, r_src + r_Ui, r_Pi)
                    TT(tmp16[:, :], src[:, HALO:HALO + 16], icnt[:, j * 16:(j + 1) * 16], ALU.mult, r_src + [r_par], [r_tmp16])
                    TT(Pi[:, HALO:HALO + 16], tmp16[:, :], Ui[:, HALO:HALO + 16], ALU.subtract, [r_tmp16] + r_Ui, r_Pi)
                slot = load_main(wl[:, 5 * E + u[0] * 128: 5 * E + u[0] * 128 + 256], 16)
                for i in range(2):
                    g = mm_group(proj_steps(slot, i * 128, xn, r_xn), R_out)
                    ACT(v3(U[i][0], R_out), pv(g, R_out), AF.Silu, g.res, U[i][1])
                ss = load_side(pool_w[l, j])
                for o in range(2):
                    steps = [(side[:, ss, i, o * 128:(o + 1) * 128], P[i][0], 0, [r_side[ss]] + P[i][1]) for i in range(2)]
                    g = mm_group(steps, R_out)
                    STT(v3(yb[:, u[o], :], R_out), pv(g, R_out), pcol(PB + 24 + u[o]), v3(U[o][0], R_out), ALU.mult, ALU.mult,
                        g.res + [r_par] + U[o][1], [r_yb[u[o]]])
            merge_branch(1)

            for j in range(4):
                u = [2 * j, 2 * j + 1]
                SG = [TF(0), TF(1)]
                V = [TB(2, 0), TB(2, 1)]

                def build_diag(ui):
                    wc = PB + 56 + ui * 31
                    for k in range(31):
                        TS(dg[:, k, :], ident[:, :], pcol(wc + k), None, ALU.mult, None, [r_ident, r_par], [r_dgk[k]])

                build_diag(u[0])
                slot = load_main(wl[:, 7 * E + u[0] * 128: 7 * E + u[0] * 128 + 256], 16)
                for i in range(2):
                    g = mm_group(proj_steps(slot, i * 128, xn, r_xn), R_in)
                    ACT(v3(SG[i][0], R_in), pv(g, R_in), AF.Sigmoid, g.res, SG[i][1])
                slot = load_main(wl[:, 6 * E + u[0] * 128: 6 * E + u[0] * 128 + 256], 16)
                for i in range(2):
                    g = mm_group(proj_steps(slot, i * 128, xn, r_xn), R_in)
                    TT(v3(V[i][0], R_in), pv(g, R_in), v3(SG[i][0], R_in), ALU.mult, g.res + SG[i][1], V[i][1])
                for i in range(2):
                    if i == 1:
                        build_diag(u[1])
                    Vi, r_Vi = V[i]
                    steps = [(dg[:, k, :], Vi, k - 30, [r_dgk[k]] + r_Vi) for k in range(31)]
                    g = mm_group(steps, R_out)
                    ACT(v3(yb[:, u[i], :], R_out), pv(g, R_out), AF.Identity, g.res + [r_par], [r_yb[u[i]]], bias=pcol(PB + 32 + u[i]))
            steps = [(ones[:, :], yb[:, k, :], 0, [r_ones, r_yb[k]]) for k in range(8)]
            g1 = mm_group(steps, R_out)
            mu, r_mu = TF(3)
            TS(v3(mu, R_out), pv(g1, R_out), 1.0 / E, None, ALU.mult, None, g1.res, r_mu)
            g2 = next_pg(R_out.nt)
            for k in range(8):
                sq, r_sq = TB(5, k % 2)
                ACT(sq[:, LO:TL], yb[:, k, LO:TL], AF.Square, [r_yb[k]], r_sq)
                for tt in range(R_out.nt):
                    lo = LO + tt * R_out.n
                    MM(psa[:, g2.base + tt, 0:R_out.n], ones[:, :], sq[:, lo:lo + R_out.n], (k == 0), (k == 7), [r_ones] + r_sq, g2.res)
            rstd, r_rstd = TF(4)
            TS(v3(rstd, R_out), pv(g2, R_out), 1.0 / E, None, ALU.mult, None, g2.res, r_rstd)
            msq, r_msq = TF(5)
            TT(msq[:, LO:TL], mu[:, LO:TL], mu[:, LO:TL], ALU.mult, r_mu, r_msq)
            TT(rstd[:, LO:TL], rstd[:, LO:TL], msq[:, LO:TL], ALU.subtract, r_rstd + r_msq, r_rstd)
            TS(rstd[:, LO:TL], rstd[:, LO:TL], 0.0, LN_EPS, ALU.max, ALU.add, r_rstd, r_rstd)
            ACT(rstd[:, LO:TL], rstd[:, LO:TL], AF.Ln, r_rstd, r_rstd)
            ACT(rstd[:, LO:TL], rstd[:, LO:TL], AF.Exp, r_rstd, r_rstd, scale=-0.5)
            for j in range(4):
                u = [2 * j, 2 * j + 1]
                slot = load_main(wl[:, 8 * E + u[0] * 128: 8 * E + u[0] * 128 + 256], 16)
                for i in range(2):
                    g = mm_group(proj_steps(slot, i * 128, xn, r_xn), R_out)
                    Z, r_Z = TF(i)
                    ACT(v3(Z, R_out), pv(g, R_out), AF.Silu, g.res, r_Z)
                    a, r_a = TF(2)
                    s, r_s = TF(5)
                    ui = u[i]
                    TT(a[:, LO:TL], yb[:, ui, LO:TL], mu[:, LO:TL], ALU.subtract, [r_yb[ui]] + r_mu, r_a)
                    TT(a[:, LO:TL], a[:, LO:TL], rstd[:, LO:TL], ALU.mult, r_a + r_rstd, r_a)
                    ACT(s[:, LO:TL], a[:, LO:TL], AF.Silu, r_a + [r_par], r_s, bias=pcol(PB + 48 + ui), scale=pcol(PB + 40 + ui))
                    TT(yb[:, ui, LO:TL], s[:, LO:TL], Z[:, LO:TL], ALU.mult, r_s + r_Z, [r_yb[ui]])
            merge_branch(2)

            for q in range(8):
                slot = load_main(w_o[l][:, q * 256:(q + 1) * 256], 16)
                for i in range(2):
                    oc = q * 2 + i
                    g = mm_group(proj_steps(slot, i * 128, acc, r_acc), R_out)
                    TT(v3(h[:, oc, :], R_out), pv(g, R_out), v3(h[:, oc, :], R_out), ALU.add, g.res + [r_h[oc]], [r_h[oc]])
                    ACT(xn[:, oc, LO:TL], h[:, oc, LO:TL], AF.Copy, [r_h[oc]], [r_xn[oc]])

            DMA("pool", pTs, pT[l].rearrange("(k p) t -> p k t", p=128), [], r_dgk, "p")
            for q in range(8):
                slot = load_main(w_pg[l][:, q * 256:(q + 1) * 256], 16)
                ss = load_side(w_pp[l][:, q * 256:(q + 1) * 256])
                sgs = []
                for i in range(2):
                    g = mm_group(proj_steps(slot, i * 128, xn, r_xn), R_out)
                    sg, r_sg = TF(i)
                    ACT(v3(sg, R_out), pv(g, R_out), AF.Sigmoid, g.res, r_sg)
                    sgs.append((sg, r_sg))
                for i in range(2):
                    oc = q * 2 + i
                    steps = [(side[:, ss, k, i * 128:(i + 1) * 128], pTs[:, k, :], 0, [r_side[ss]] + r_dgk[0:17]) for k in range(2)]
                    g = mm_group(steps, R_out)
                    sg, r_sg = sgs[i]
                    t, r_t = TF(2 + i)
                    TT(v3(t, R_out), pv(g, R_out), v3(sg, R_out), ALU.mult, g.res + r_sg, r_t)
                    TT(h[:, oc, LO:TL], h[:, oc, LO:TL], t[:, LO:TL], ALU.add, [r_h[oc]] + r_t, [r_h[oc]])

        Rf = R_L[depth_run - 1][1] if depth_run < DEPTH else R_FINAL
        rs, r_rs = rms_stats(Rf)
        for k in range(16):
            ob, r_ob = TF(1 + (k % 4))
            STT(ob[:, HALO:TL], h[:, k, HALO:TL], pcol(32 + k), rs[:, HALO:TL], ALU.mult, ALU.mult, [r_h[k], r_par] + r_rs, r_ob)
            DMA("sp", outT[k * 128:(k + 1) * 128, :], ob[:, HALO:TL], r_ob, [r_out[k]], f"out{k % 4}")
        S.op("sp", None, None, r_out, [])

        S.emit({"pe": s_pe, "act": s_act, "dve": s_dve, "pool": s_pool, "sp": s_sp}, dsems)
    return nc


def _pack_params(norm_g, conv_a_w, pool_scale, conv_c_w, conv_c_b, ln_c_g, ln_c_b, final_norm_g):
    par = np.zeros((128, NPAR), dtype=np.float32)
    for l in range(DEPTH):
        par[:, l * 16:(l + 1) * 16] = norm_g[l].reshape(16, 128).T
        B = _par_base(l)
        par[:, B:B + 24] = conv_a_w[l].reshape(3, 8, 128).transpose(2, 1, 0).reshape(128, 24)
        par[:, B + 24:B + 32] = pool_scale[l].reshape(8, 128).T
        par[:, B + 32:B + 40] = conv_c_b[l].reshape(8, 128).T
        par[:, B + 40:B + 48] = ln_c_g[l].reshape(8, 128).T
        par[:, B + 48:B + 56] = ln_c_b[l].reshape(8, 128).T
        par[:, B + 56:B + 304] = conv_c_w[l].reshape(31, 8, 128).transpose(2, 1, 0).reshape(128, 248)
    par[:, 32:48] = final_norm_g.reshape(16, 128).T
    return par


def _icnt_table(half):
    t = np.zeros((128, 64), dtype=np.float32)
    for g, w in enumerate((2, 4, 8, 16)):
        for i in range(16):
            pos = half * TOK + i
            t[:, g * 16 + i] = np.float32(1.0) / np.float32(min(pos + 1, w))
    return t


_NC_CACHE = {}


def kernel(x, p, norm_g, w_in, conv_a_w, pool_w, pool_scale, conv_c_w, conv_c_b,
           ln_c_g, ln_c_b, w_branch_out, w_o, w_ple_gate, w_ple_proj, final_norm_g):
    f = lambda a: np.ascontiguousarray(np.asarray(a, dtype=np.float32))
    x, p = f(x), f(p)
    w_in, pool_w, w_branch_out, w_o, w_ple_gate, w_ple_proj = map(f, (w_in, pool_w, w_branch_out, w_o, w_ple_gate, w_ple_proj))
    par = _pack_params(*map(f, (norm_g, conv_a_w, pool_scale, conv_c_w, conv_c_b, ln_c_g, ln_c_b, final_norm_g)))
    ident = np.eye(128, dtype=np.float32)
    if "nc" not in _NC_CACHE:
        _NC_CACHE["nc"] = build_program()
    nc = _NC_CACHE["nc"]
    in_maps = []
    for c in range(NCORES):
        b, half = c // 2, c % 2
        xT = np.zeros((D, TL), dtype=np.float32)
        pT = np.zeros((DEPTH, PLE, TL), dtype=np.float32)
        g0 = half * TOK - HALO
        s0 = max(g0, 0)
        xT[:, s0 - g0:] = x[b, s0:half * TOK + TOK, :].T
        for l in range(DEPTH):
            pT[l][:, s0 - g0:] = p[l, b, s0:half * TOK + TOK, :].T
        in_maps.append({
            "xT": xT, "pT": pT, "par": par, "icnt": _icnt_table(half), "ident": ident,
            "w_in": w_in, "pool_w": pool_w, "w_branch_out": w_branch_out, "w_o": w_o,
            "w_ple_gate": w_ple_gate, "w_ple_proj": w_ple_proj,
        })
    res = run_bass_kernel_spmd(nc, in_maps, core_ids=list(range(NCORES)))
    out = np.empty((BATCH, SEQ, D), dtype=np.float32)
    for c in range(NCORES):
        b, half = c // 2, c % 2
        out[b, half * TOK:(half + 1) * TOK, :] = res.results[c]["outT"].T
    return out
```

```python
from contextlib import ExitStack

import numpy as np
import concourse.bass as bass
import concourse.mybir as mybir
from concourse.bass_utils import run_bass_kernel_spmd

F32 = mybir.dt.float32
BF16 = mybir.dt.bfloat16
AF = mybir.ActivationFunctionType
ALU = mybir.AluOpType

D = 2048
E = 1024
SEQ = 2048
BATCH = 4
DEPTH = 2
PLE = 256
IN_COLS = 15360
HALO = 62
TOK = 1024
TL = TOK + HALO
NCORES = 8
RMS_EPS = 1e-6
LN_EPS = 1e-5
NPAR = 48 + 2 * 304
ENGS = ("pe", "act", "dve", "pool", "sp")


class Res:
    __slots__ = ("name", "writer", "readers")

    def __init__(self, name):
        self.name = name
        self.writer = None
        self.readers = {}


class Op:
    __slots__ = ("eng", "meth", "kw", "deps", "is_dma", "dsem", "dcount", "signal", "sigval")

    def __init__(self, eng, meth, kw):
        self.eng = eng
        self.meth = meth
        self.kw = kw
        self.deps = []
        self.is_dma = False
        self.dsem = None
        self.dcount = 0
        self.signal = False
        self.sigval = 0


class Sched:
    def __init__(self, nc):
        self.nc = nc
        self.ops = {e: [] for e in ENGS}
        self.dma_counts = {}

    def _add_dep(self, op, dep):
        if dep is None or dep is op:
            return
        if (not dep.is_dma) and (not op.is_dma) and dep.eng == "pe" and op.eng == "pe":
            return
        op.deps.append(dep)
        if not dep.is_dma:
            dep.signal = True

    def op(self, eng, meth, kw=None, reads=(), writes=(), dma_sem=None):
        o = Op(eng, meth, kw)
        if dma_sem is not None:
            o.is_dma = True
            o.dsem = dma_sem
            self.dma_counts[dma_sem] = self.dma_counts.get(dma_sem, 0) + 16
            o.dcount = self.dma_counts[dma_sem]
        for r in reads:
            self._add_dep(o, r.writer)
        for w in writes:
            self._add_dep(o, w.writer)
            for rd in w.readers.values():
                self._add_dep(o, rd)
        for w in writes:
            w.writer = o
            w.readers = {}
        for r in reads:
            r.readers[id(o) if o.is_dma else o.eng] = o
        self.ops[eng].append(o)
        return o

    def emit(self, sems, dma_sems):
        nc = self.nc
        for e in ENGS:
            c = 0
            for o in self.ops[e]:
                if (not o.is_dma) and o.signal:
                    c += 1
                    o.sigval = c
        engobj = {"pe": "tensor", "act": "scalar", "dve": "vector", "pool": "gpsimd", "sp": "sync"}
        with nc.Block() as block:
            for e in ENGS:
                ops = self.ops[e]
                if not ops:
                    continue

                def body(eng, ops=ops, e=e):
                    waited = {}
                    for o in ops:
                        need = {}
                        for d in o.deps:
                            if d.is_dma:
                                key, val = ("d", d.dsem), d.dcount
                            else:
                                key, val = ("e", d.eng), d.sigval
                            if val > need.get(key, 0):
                                need[key] = val
                        todo = []
                        for key, val in need.items():
                            if waited.get(key, 0) >= val:
                                continue
                            waited[key] = val
                            todo.append((key, val))
                        emb = None
                        if o.meth is not None and not o.is_dma:
                            for t in todo:
                                if t[0][0] == "e":
                                    emb = t
                            if emb is not None:
                                todo.remove(emb)
                        for key, val in todo:
                            sem = dma_sems[key[1]] if key[0] == "d" else sems[key[1]]
                            eng.wait_ge(sem, val)
                        if o.meth is None:
                            continue
                        ins = getattr(eng, o.meth)(**o.kw)
                        if emb is not None:
                            ins._wait_ge(sems[emb[0][1]], emb[1])
                        if o.is_dma:
                            ins.then_inc(dma_sems[o.dsem], 16)
                        elif o.signal:
                            ins.then_inc(sems[e], 1)

                getattr(block, engobj[e])(body)


class Rng:
    def __init__(self, lo, n, nt):
        assert lo + n * nt == TL and n <= 512
        self.lo, self.n, self.nt = lo, n, nt


R_L = [
    (Rng(0, 362, 3), Rng(30, 352, 3)),
    (Rng(30, 352, 3), Rng(62, 512, 2)),
]
R_FINAL = Rng(62, 512, 2)


def _par_base(l):
    return 48 + l * 304


def build_program(depth_run=DEPTH):
    nc = bass.Bass("TRN2", target_bir_lowering=False)
    xT = nc.dram_tensor("xT", [D, TL], F32, kind="ExternalInput").ap()
    pT = nc.dram_tensor("pT", [DEPTH, PLE, TL], F32, kind="ExternalInput").ap()
    par_d = nc.dram_tensor("par", [128, NPAR], F32, kind="ExternalInput").ap()
    icnt_d = nc.dram_tensor("icnt", [128, 64], F32, kind="ExternalInput").ap()
    ident_d = nc.dram_tensor("ident", [128, 128], F32, kind="ExternalInput").ap()
    w_in = nc.dram_tensor("w_in", [DEPTH, D, IN_COLS], F32, kind="ExternalInput").ap()
    pool_w = nc.dram_tensor("pool_w", [DEPTH, 4, 256, 256], F32, kind="ExternalInput").ap()
    w_bo = nc.dram_tensor("w_branch_out", [DEPTH, 3, E, D], F32, kind="ExternalInput").ap()
    w_o = nc.dram_tensor("w_o", [DEPTH, D, D], F32, kind="ExternalInput").ap()
    w_pg = nc.dram_tensor("w_ple_gate", [DEPTH, D, D], F32, kind="ExternalInput").ap()
    w_pp = nc.dram_tensor("w_ple_proj", [DEPTH, PLE, D], F32, kind="ExternalInput").ap()
    outT = nc.dram_tensor("outT", [D, TOK], F32, kind="ExternalOutput").ap()

    S = Sched(nc)
    dma_names = ["x0", "x1", "x2", "x3", "par", "ident", "p", "ring0", "ring1", "side0", "side1", "out0", "out1", "out2", "out3"]

    def ACT(out, in_, func, reads, writes, **kw):
        S.op("act", "activation", dict(out=out, in_=in_, func=func, **kw), reads, writes)

    def TT(out, in0, in1, op, reads, writes):
        S.op("dve", "tensor_tensor", dict(out=out, in0=in0, in1=in1, op=op), reads, writes)

    def TS(out, in0, s1, s2, op0, op1, reads, writes):
        kw = dict(out=out, in0=in0, scalar1=s1, scalar2=s2, op0=op0)
        if op1 is not None:
            kw["op1"] = op1
        S.op("dve", "tensor_scalar", kw, reads, writes)

    def STT(out, in0, scalar, in1, op0, op1, reads, writes):
        S.op("dve", "scalar_tensor_tensor", dict(out=out, in0=in0, scalar=scalar, in1=in1, op0=op0, op1=op1), reads, writes)

    def RECIP(out, in_, reads, writes):
        S.op("dve", "reciprocal", dict(out=out, in_=in_), reads, writes)

    def MM(out, lhsT, rhs, start, stop, reads, writes):
        S.op("pe", "matmul", dict(out=out, lhsT=lhsT, rhs=rhs, start=start, stop=stop), reads, writes)

    def DMA(eng, out, in_, reads, writes, sem):
        S.op(eng, "dma_start", dict(out=out, in_=in_), reads, writes, dma_sem=sem)

    with ExitStack() as es:
        ec = es.enter_context
        h = ec(nc.sbuf_tensor("h", [128, 16, TL], F32))
        xn = ec(nc.sbuf_tensor("xn", [128, 16, TL], BF16))
        acc = ec(nc.sbuf_tensor("acc", [128, 16, TL], BF16))
        yb = ec(nc.sbuf_tensor("yb", [128, 8, TL], BF16))
        ring = ec(nc.sbuf_tensor("ring", [128, 2, 16, 256], BF16))
        side = ec(nc.sbuf_tensor("side", [128, 2, 2, 256], BF16))
        dgb = ec(nc.sbuf_tensor("dgb", [128, 31 * 128], BF16))
        par = ec(nc.sbuf_tensor("par_s", [128, NPAR], F32))
        icnt = ec(nc.sbuf_tensor("icnt_s", [128, 64], F32))
        ident = ec(nc.sbuf_tensor("ident_s", [128, 128], BF16))
        ones = ec(nc.sbuf_tensor("ones_s", [128, 128], BF16))
        tmp16 = ec(nc.sbuf_tensor("tmp16", [128, 16], F32))
        Tt = ec(nc.sbuf_tensor("Tt", [128, 6, TL], F32))
        psa = ec(nc.psum_tensor("psa", [128, 8, 512], F32))
        s_pe, s_act, s_dve, s_pool, s_sp = [ec(nc.semaphore(n)) for n in ("s_pe", "s_act", "s_dve", "s_pool", "s_sp")]
        dlist = [ec(nc.semaphore(f"d{i}")) for i in range(len(dma_names))]
        dsems = dict(zip(dma_names, dlist))
        r_bank = [Res(f"bank{i}") for i in range(8)]
        r_h = [Res(f"h{k}") for k in range(16)]
        r_xn = [Res(f"xn{k}") for k in range(16)]
        r_acc = [Res(f"acc{k}") for k in range(16)]
        r_yb = [Res(f"yb{k}") for k in range(8)]
        r_ring = [Res("ring0"), Res("ring1")]
        r_side = [Res("side0"), Res("side1")]
        r_dgk = [Res(f"dg{k}") for k in range(31)]
        r_par = Res("par")
        r_ident = Res("ident")
        r_ones = Res("ones")
        r_tmp16 = Res("tmp16")
        r_Ta = [Res(f"T{i}a") for i in range(6)]
        r_Tb = [Res(f"T{i}b") for i in range(6)]
        r_out = [Res(f"out{k}") for k in range(16)]

        dg = dgb[:, :].rearrange("p (k c) -> p k c", c=128)
        pTs = dgb[:, 0:2 * TL].rearrange("p (k t) -> p k t", t=TL)

        def TF(i):
            return Tt[:, i, :], [r_Ta[i], r_Tb[i]]

        def TB(i, half):
            v = Tt[:, i, :].bitcast(BF16)
            return v[:, half * TL:(half + 1) * TL], [r_Ta[i] if half == 0 else r_Tb[i]]

        def v3(ap2d, R):
            return ap2d[:, R.lo:TL].rearrange("p (j n) -> p j n", n=R.n)

        class PGrp:
            def __init__(self, base, nt):
                self.base, self.nt = base, nt
                self.res = r_bank[base:base + nt]

        def pv(g, R):
            return psa[:, g.base:g.base + R.nt, 0:R.n]

        def pcol(c):
            return par[:, c:c + 1]

        state = {"pg": 0, "ring": 0, "side": 0}

        def next_pg(nt):
            b = state["pg"]
            if b + nt > 8:
                b = 0
            state["pg"] = b + nt
            return PGrp(b, nt)

        def load_main(src2d, kch):
            s = state["ring"]
            state["ring"] ^= 1
            DMA("pool", ring[:, s, 0:kch, :], src2d.rearrange("(k p) c -> p k c", p=128), [], [r_ring[s]], f"ring{s}")
            return s

        def load_main512(src2d):
            s = state["ring"]
            state["ring"] ^= 1
            dst = ring[:, s, :, :].rearrange("p (k a) c -> p k (a c)", a=2)
            DMA("pool", dst, src2d.rearrange("(k p) c -> p k c", p=128), [], [r_ring[s]], f"ring{s}")
            return s

        def load_side(src2d):
            s = state["side"]
            state["side"] ^= 1
            DMA("pool", side[:, s, :, :], src2d.rearrange("(k p) c -> p k c", p=128), [], [r_side[s]], f"side{s}")
            return s

        def mm_group(steps, R):
            g = next_pg(R.nt)
            ns = len(steps)
            for i, (lhsT, rhs2d, shift, reads) in enumerate(steps):
                for tt in range(R.nt):
                    lo = R.lo + tt * R.n + shift
                    MM(psa[:, g.base + tt, 0:R.n], lhsT, rhs2d[:, lo:lo + R.n], (i == 0), (i == ns - 1), reads, g.res)
            return g

        def proj_steps(slot, coff, src, r_src):
            return [(ring[:, slot, k, coff:coff + 128], src[:, k, :], 0, [r_ring[slot], r_src[k]]) for k in range(16)]

        DMA("sp", par[:, :], par_d, [], [r_par], "par")
        DMA("sp", icnt[:, :], icnt_d, [], [r_par], "par")
        for i4 in range(4):
            DMA("sp", h[:, 4 * i4:4 * i4 + 4, :], xT[512 * i4:512 * i4 + 512, :].rearrange("(k p) t -> p k t", p=128), [],
                r_h[4 * i4:4 * i4 + 4], f"x{i4}")
        DMA("pool", ident[:, :], ident_d, [], [r_ident], "ident")
        S.op("dve", "memset", dict(ap=ones[:, :], constant=1.0), [], [r_ones])

        def rms_stats(R):
            g = next_pg(R.nt)
            for k in range(16):
                if k % 2 == 0:
                    sq, r_sq = TB(5, (k // 2) % 2)
                    ACT(sq[:, R.lo:TL], h[:, k, R.lo:TL], AF.Square, [r_h[k]], r_sq)
                else:
                    sq, r_sq = TB(4, (k // 2) % 2)
                    TT(sq[:, R.lo:TL], h[:, k, R.lo:TL], h[:, k, R.lo:TL], ALU.mult, [r_h[k]], r_sq)
                for tt in range(R.nt):
                    lo = R.lo + tt * R.n
                    MM(psa[:, g.base + tt, 0:R.n], ones[:, :], sq[:, lo:lo + R.n], (k == 0), (k == 15), [r_ones] + r_sq, g.res)
            rs, r_rs = TF(0)
            TS(v3(rs, R), pv(g, R), 1.0 / D, RMS_EPS, ALU.mult, ALU.add, g.res, r_rs)
            ACT(rs[:, R.lo:TL], rs[:, R.lo:TL], AF.Ln, r_rs, r_rs)
            ACT(rs[:, R.lo:TL], rs[:, R.lo:TL], AF.Exp, r_rs, r_rs, scale=-0.5)
            return rs, r_rs

        for l in range(depth_run):
            R_in, R_out = R_L[l]
            PB = _par_base(l)
            wl = w_in[l]
            LO = R_out.lo

            rs, r_rs = rms_stats(R_in)
            for k in range(16):
                STT(xn[:, k, R_in.lo:TL], h[:, k, R_in.lo:TL], pcol(l * 16 + k), rs[:, R_in.lo:TL], ALU.mult, ALU.mult,
                    [r_h[k], r_par] + r_rs, [r_xn[k]])

            def merge_branch(b):
                for q in range(4):
                    sgs = []
                    for hpair in range(2):
                        d0c = q * 4 + hpair * 2
                        c0 = 9216 + b * 2048 + d0c * 128
                        slot = load_main(wl[:, c0:c0 + 256], 16)
                        for jj in range(2):
                            g = mm_group(proj_steps(slot, jj * 128, xn, r_xn), R_out)
                            sg, r_sg = TF(hpair * 2 + jj)
                            ACT(v3(sg, R_out), pv(g, R_out), AF.Sigmoid, g.res, r_sg)
                            sgs.append((sg, r_sg))
                    slot = load_main512(w_bo[l, b][:, q * 512:(q + 1) * 512])
                    wv = ring[:, slot, :, :].rearrange("p (k a) c -> p k (a c)", a=2)
                    for jj in range(4):
                        d = q * 4 + jj
                        steps = [(wv[:, k, jj * 128:(jj + 1) * 128], yb[:, k, :], 0, [r_ring[slot], r_yb[k]]) for k in range(8)]
                        g = mm_group(steps, R_out)
                        sg, r_sg = sgs[jj]
                        if b == 0:
                            TT(v3(acc[:, d, :], R_out), pv(g, R_out), v3(sg, R_out), ALU.mult, g.res + r_sg, [r_acc[d]])
                        else:
                            t, r_t = TF(4 + (jj % 2))
                            TT(v3(t, R_out), pv(g, R_out), v3(sg, R_out), ALU.mult, g.res + r_sg, r_t)
                            TT(acc[:, d, LO:TL], acc[:, d, LO:TL], t[:, LO:TL], ALU.add, [r_acc[d]] + r_t, [r_acc[d]])

            for j in range(4):
                u = [2 * j, 2 * j + 1]
                tA = [TF(0), TF(1)]
                cv = [TF(2), TF(3)]
                slot = load_main(wl[:, 0 * E + u[0] * 128: 0 * E + u[0] * 128 + 256], 16)
                for i in range(2):
                    g = mm_group(proj_steps(slot, i * 128, xn, r_xn), R_in)
                    ACT(v3(tA[i][0], R_in), pv(g, R_in), AF.Copy, g.res, tA[i][1])
                slot = load_main(wl[:, 2 * E + u[0] * 128: 2 * E + u[0] * 128 + 256], 16)
                for i in range(2):
                    g = mm_group(proj_steps(slot, i * 128, xn, r_xn), R_in)
                    cx, r_cx = tA[i]
                    co, r_co = cv[i]
                    TT(v3(cx, R_in), pv(g, R_in), v3(cx, R_in), ALU.mult, g.res + r_cx, r_cx)
                    wc = PB + u[i] * 3
                    TS(co[:, LO:TL], cx[:, LO:TL], pcol(wc + 2), None, ALU.mult, None, r_cx + [r_par], r_co)
                    for sh in (1, 2):
                        STT(co[:, LO:TL], cx[:, LO - sh:TL - sh], pcol(wc + 2 - sh), co[:, LO:TL], ALU.mult, ALU.add,
                            r_cx + r_co + [r_par], r_co)
                slot = load_main(wl[:, 1 * E + u[0] * 128: 1 * E + u[0] * 128 + 256], 16)
                for i in range(2):
                    g = mm_group(proj_steps(slot, i * 128, xn, r_xn), R_out)
                    co, r_co = cv[i]
                    TT(v3(co, R_out), pv(g, R_out), v3(co, R_out), ALU.mult, g.res + r_co, r_co)
                slot = load_main(wl[:, 3 * E + u[0] * 128: 3 * E + u[0] * 128 + 256], 16)
                for i in range(2):
                    g = mm_group(proj_steps(slot, i * 128, xn, r_xn), R_out)
                    sz, r_sz = tA[i]
                    co, r_co = cv[i]
                    ACT(v3(sz, R_out), pv(g, R_out), AF.Silu, g.res, r_sz)
                    TT(yb[:, u[i], LO:TL], co[:, LO:TL], sz[:, LO:TL], ALU.mult, r_co + r_sz, [r_yb[u[i]]])
            merge_branch(0)

            for j in range(4):
                u = [2 * j, 2 * j + 1]
                wlen = 2 ** (j + 1)
                U = [TF(0), TF(1)]
                XY = [TF(2), TF(3)]
                P = [TB(4, 0), TB(4, 1)]
                slot = load_main(wl[:, 4 * E + u[0] * 128: 4 * E + u[0] * 128 + 256], 16)
                for i in range(2):
                    g = mm_group(proj_steps(slot, i * 128, xn, r_xn), R_in)
                    Ui, r_Ui = U[i]
                    ACT(v3(Ui, R_in), pv(g, R_in), AF.Copy, g.res, r_Ui)
                    src, r_src = Ui, r_Ui
                    lo_valid = R_in.lo
                    sh = 1
                    st = 0
                    while sh < wlen:
                        dst, r_dst = XY[st % 2]
                        lo_new = lo_valid + sh
                        TT(dst[:, lo_new:TL], src[:, lo_new:TL], src[:, lo_new - sh:TL - sh], ALU.add, r_src, r_dst)
                        src, r_src = dst, r_dst
                        lo_valid = lo_new
                        sh *= 2
                        st += 1
                    assert lo_valid <= LO
                    Pi, r_Pi = P[i]
                    STT(Pi[:, LO:TL], src[:, LO:TL], 1.0 / wlen, Ui[:, LO:TL], ALU.mult, ALU.subtract, r_src + r_Ui, r_Pi)
                    TT(tmp16[:, :], src[:, HALO:HALO + 16], icnt[:, j * 16:(j + 1) * 16], ALU.mult, r_src + [r_par], [r_tmp16])
                    TT(Pi[:, HALO:HALO + 16], tmp16[:, :], Ui[:, HALO:HALO + 16], ALU.subtract, [r_tmp16] + r_Ui, r_Pi)
                slot = load_main(wl[:, 5 * E + u[0] * 128: 5 * E + u[0] * 128 + 256], 16)
                for i in range(2):
                    g = mm_group(proj_steps(slot, i * 128, xn, r_xn), R_out)
                    ACT(v3(U[i][0], R_out), pv(g, R_out), AF.Silu, g.res, U[i][1])
                ss = load_side(pool_w[l, j])
                for o in range(2):
                    steps = [(side[:, ss, i, o * 128:(o + 1) * 128], P[i][0], 0, [r_side[ss]] + P[i][1]) for i in range(2)]
                    g = mm_group(steps, R_out)
                    STT(v3(yb[:, u[o], :], R_out), pv(g, R_out), pcol(PB + 24 + u[o]), v3(U[o][0], R_out), ALU.mult, ALU.mult,
                        g.res + [r_par] + U[o][1], [r_yb[u[o]]])
            merge_branch(1)

            for j in range(4):
                u = [2 * j, 2 * j + 1]
                SG = [TF(0), TF(1)]
                V = [TB(2, 0), TB(2, 1)]

                def build_diag(ui):
                    wc = PB + 56 + ui * 31
                    for k in range(31):
                        TS(dg[:, k, :], ident[:, :], pcol(wc + k), None, ALU.mult, None, [r_ident, r_par], [r_dgk[k]])

                build_diag(u[0])
                slot = load_main(wl[:, 7 * E + u[0] * 128: 7 * E + u[0] * 128 + 256], 16)
                for i in range(2):
                    g = mm_group(proj_steps(slot, i * 128, xn, r_xn), R_in)
                    ACT(v3(SG[i][0], R_in), pv(g, R_in), AF.Sigmoid, g.res, SG[i][1])
                slot = load_main(wl[:, 6 * E + u[0] * 128: 6 * E + u[0] * 128 + 256], 16)
                for i in range(2):
                    g = mm_group(proj_steps(slot, i * 128, xn, r_xn), R_in)
                    TT(v3(V[i][0], R_in), pv(g, R_in), v3(SG[i][0], R_in), ALU.mult, g.res + SG[i][1], V[i][1])
                for i in range(2):
                    if i == 1:
                        build_diag(u[1])
                    Vi, r_Vi = V[i]
                    steps = [(dg[:, k, :], Vi, k - 30, [r_dgk[k]] + r_Vi) for k in range(31)]
                    g = mm_group(steps, R_out)
                    ACT(v3(yb[:, u[i], :], R_out), pv(g, R_out), AF.Identity, g.res + [r_par], [r_yb[u[i]]], bias=pcol(PB + 32 + u[i]))
            steps = [(ones[:, :], yb[:, k, :], 0, [r_ones, r_yb[k]]) for k in range(8)]
            g1 = mm_group(steps, R_out)
            mu, r_mu = TF(3)
            TS(v3(mu, R_out), pv(g1, R_out), 1.0 / E, None, ALU.mult, None, g1.res, r_mu)
            g2 = next_pg(R_out.nt)
            for k in range(8):
                if k % 2 == 0:
                    sq, r_sq = TB(5, (k // 2) % 2)
                    ACT(sq[:, LO:TL], yb[:, k, LO:TL], AF.Square, [r_yb[k]], r_sq)
                else:
                    sq, r_sq = TB(4, (k // 2) % 2)
                    TT(sq[:, LO:TL], yb[:, k, LO:TL], yb[:, k, LO:TL], ALU.mult, [r_yb[k]], r_sq)
                for tt in range(R_out.nt):
                    lo = LO + tt * R_out.n
                    MM(psa[:, g2.base + tt, 0:R_out.n], ones[:, :], sq[:, lo:lo + R_out.n], (k == 0), (k == 7), [r_ones] + r_sq, g2.res)
            rstd, r_rstd = TF(4)
            TS(v3(rstd, R_out), pv(g2, R_out), 1.0 / E, None, ALU.mult, None, g2.res, r_rstd)
            msq, r_msq = TF(5)
            TT(msq[:, LO:TL], mu[:, LO:TL], mu[:, LO:TL], ALU.mult, r_mu, r_msq)
            TT(rstd[:, LO:TL], rstd[:, LO:TL], msq[:, LO:TL], ALU.subtract, r_rstd + r_msq, r_rstd)
            TS(rstd[:, LO:TL], rstd[:, LO:TL], 0.0, LN_EPS, ALU.max, ALU.add, r_rstd, r_rstd)
            ACT(rstd[:, LO:TL], rstd[:, LO:TL], AF.Ln, r_rstd, r_rstd)
            ACT(rstd[:, LO:TL], rstd[:, LO:TL], AF.Exp, r_rstd, r_rstd, scale=-0.5)
            for j in range(4):
                u = [2 * j, 2 * j + 1]
                slot = load_main(wl[:, 8 * E + u[0] * 128: 8 * E + u[0] * 128 + 256], 16)
                for i in range(2):
                    g = mm_group(proj_steps(slot, i * 128, xn, r_xn), R_out)
                    Z, r_Z = TF(i)
                    ACT(v3(Z, R_out), pv(g, R_out), AF.Silu, g.res, r_Z)
                    a, r_a = TF(2)
                    s, r_s = TF(5)
                    ui = u[i]
                    TT(a[:, LO:TL], yb[:, ui, LO:TL], mu[:, LO:TL], ALU.subtract, [r_yb[ui]] + r_mu, r_a)
                    TT(a[:, LO:TL], a[:, LO:TL], rstd[:, LO:TL], ALU.mult, r_a + r_rstd, r_a)
                    ACT(s[:, LO:TL], a[:, LO:TL], AF.Silu, r_a + [r_par], r_s, bias=pcol(PB + 48 + ui), scale=pcol(PB + 40 + ui))
                    TT(yb[:, ui, LO:TL], s[:, LO:TL], Z[:, LO:TL], ALU.mult, r_s + r_Z, [r_yb[ui]])
            merge_branch(2)

            for q in range(8):
                slot = load_main(w_o[l][:, q * 256:(q + 1) * 256], 16)
                for i in range(2):
                    oc = q * 2 + i
                    g = mm_group(proj_steps(slot, i * 128, acc, r_acc), R_out)
                    TT(v3(h[:, oc, :], R_out), pv(g, R_out), v3(h[:, oc, :], R_out), ALU.add, g.res + [r_h[oc]], [r_h[oc]])
                    ACT(xn[:, oc, LO:TL], h[:, oc, LO:TL], AF.Copy, [r_h[oc]], [r_xn[oc]])

            DMA("pool", pTs, pT[l].rearrange("(k p) t -> p k t", p=128), [], r_dgk, "p")
            for q in range(8):
                slot = load_main(w_pg[l][:, q * 256:(q + 1) * 256], 16)
                ss = load_side(w_pp[l][:, q * 256:(q + 1) * 256])
                sgs = []
                for i in range(2):
                    g = mm_group(proj_steps(slot, i * 128, xn, r_xn), R_out)
                    sg, r_sg = TF(i)
                    ACT(v3(sg, R_out), pv(g, R_out), AF.Sigmoid, g.res, r_sg)
                    sgs.append((sg, r_sg))
                for i in range(2):
                    oc = q * 2 + i
                    steps = [(side[:, ss, k, i * 128:(i + 1) * 128], pTs[:, k, :], 0, [r_side[ss]] + r_dgk[0:17]) for k in range(2)]
                    g = mm_group(steps, R_out)
                    sg, r_sg = sgs[i]
                    t, r_t = TF(2 + i)
                    TT(v3(t, R_out), pv(g, R_out), v3(sg, R_out), ALU.mult, g.res + r_sg, r_t)
                    TT(h[:, oc, LO:TL], h[:, oc, LO:TL], t[:, LO:TL], ALU.add, [r_h[oc]] + r_t, [r_h[oc]])

        Rf = R_L[depth_run - 1][1] if depth_run < DEPTH else R_FINAL
        rs, r_rs = rms_stats(Rf)
        for k in range(16):
            ob, r_ob = TF(1 + (k % 4))
            STT(ob[:, HALO:TL], h[:, k, HALO:TL], pcol(32 + k), rs[:, HALO:TL], ALU.mult, ALU.mult, [r_h[k], r_par] + r_rs, r_ob)
            DMA("sp", outT[k * 128:(k + 1) * 128, :], ob[:, HALO:TL], r_ob, [r_out[k]], f"out{k % 4}")
        S.op("sp", None, None, r_out, [])

        S.emit({"pe": s_pe, "act": s_act, "dve": s_dve, "pool": s_pool, "sp": s_sp}, dsems)
    return nc


def _pack_params(norm_g, conv_a_w, pool_scale, conv_c_w, conv_c_b, ln_c_g, ln_c_b, final_norm_g):
    par = np.zeros((128, NPAR), dtype=np.float32)
    for l in range(DEPTH):
        par[:, l * 16:(l + 1) * 16] = norm_g[l].reshape(16, 128).T
        B = _par_base(l)
        par[:, B:B + 24] = conv_a_w[l].reshape(3, 8, 128).transpose(2, 1, 0).reshape(128, 24)
        par[:, B + 24:B + 32] = pool_scale[l].reshape(8, 128).T
        par[:, B + 32:B + 40] = conv_c_b[l].reshape(8, 128).T
        par[:, B + 40:B + 48] = ln_c_g[l].reshape(8, 128).T
        par[:, B + 48:B + 56] = ln_c_b[l].reshape(8, 128).T
        par[:, B + 56:B + 304] = conv_c_w[l].reshape(31, 8, 128).transpose(2, 1, 0).reshape(128, 248)
    par[:, 32:48] = final_norm_g.reshape(16, 128).T
    return par


def _icnt_table(half):
    t = np.zeros((128, 64), dtype=np.float32)
    for g, w in enumerate((2, 4, 8, 16)):
        for i in range(16):
            pos = half * TOK + i
            t[:, g * 16 + i] = np.float32(1.0) / np.float32(min(pos + 1, w))
    return t


_NC_CACHE = {}


def kernel(x, p, norm_g, w_in, conv_a_w, pool_w, pool_scale, conv_c_w, conv_c_b,
           ln_c_g, ln_c_b, w_branch_out, w_o, w_ple_gate, w_ple_proj, final_norm_g):
    f = lambda a: np.ascontiguousarray(np.asarray(a, dtype=np.float32))
    x, p = f(x), f(p)
    w_in, pool_w, w_branch_out, w_o, w_ple_gate, w_ple_proj = map(f, (w_in, pool_w, w_branch_out, w_o, w_ple_gate, w_ple_proj))
    par = _pack_params(*map(f, (norm_g, conv_a_w, pool_scale, conv_c_w, conv_c_b, ln_c_g, ln_c_b, final_norm_g)))
    ident = np.eye(128, dtype=np.float32)
    if "nc" not in _NC_CACHE:
        _NC_CACHE["nc"] = build_program()
    nc = _NC_CACHE["nc"]
    in_maps = []
    for c in range(NCORES):
        b, half = c // 2, c % 2
        xT = np.zeros((D, TL), dtype=np.float32)
        pT = np.zeros((DEPTH, PLE, TL), dtype=np.float32)
        g0 = half * TOK - HALO
        s0 = max(g0, 0)
        xT[:, s0 - g0:] = x[b, s0:half * TOK + TOK, :].T
        for l in range(DEPTH):
            pT[l][:, s0 - g0:] = p[l, b, s0:half * TOK + TOK, :].T
        in_maps.append({
            "xT": xT, "pT": pT, "par": par, "icnt": _icnt_table(half), "ident": ident,
            "w_in": w_in, "pool_w": pool_w, "w_branch_out": w_branch_out, "w_o": w_o,
            "w_ple_gate": w_ple_gate, "w_ple_proj": w_ple_proj,
        })
    res = run_bass_kernel_spmd(nc, in_maps, core_ids=list(range(NCORES)))
    out = np.empty((BATCH, SEQ, D), dtype=np.float32)
    for c in range(NCORES):
        b, half = c // 2, c % 2
        out[b, half * TOK:(half + 1) * TOK, :] = res.results[c]["outT"].T
    return out
```

```python
from contextlib import ExitStack

import numpy as np
import concourse.bass as bass
import concourse.mybir as mybir
from concourse.bass_utils import run_bass_kernel_spmd

F32 = mybir.dt.float32
BF16 = mybir.dt.bfloat16
AF = mybir.ActivationFunctionType
ALU = mybir.AluOpType

D = 2048
E = 1024
SEQ = 2048
BATCH = 4
DEPTH = 2
PLE = 256
IN_COLS = 15360
HALO = 62
TOK = 1024
TL = TOK + HALO
NCORES = 8
RMS_EPS = 1e-6
LN_EPS = 1e-5
NPAR = 48 + 2 * 304 + 2
C_RMS_EPS = 48 + 2 * 304
ENGS = ("pe", "act", "dve", "pool", "sp")


class Res:
    __slots__ = ("name", "writer", "readers")

    def __init__(self, name):
        self.name = name
        self.writer = None
        self.readers = {}


class Op:
    __slots__ = ("eng", "meth", "kw", "deps", "is_dma", "dsem", "dcount", "signal", "sigval")

    def __init__(self, eng, meth, kw):
        self.eng = eng
        self.meth = meth
        self.kw = kw
        self.deps = []
        self.is_dma = False
        self.dsem = None
        self.dcount = 0
        self.signal = False
        self.sigval = 0


class Sched:
    def __init__(self, nc):
        self.nc = nc
        self.ops = {e: [] for e in ENGS}
        self.dma_counts = {}

    def _add_dep(self, op, dep):
        if dep is None or dep is op:
            return
        if (not dep.is_dma) and (not op.is_dma) and dep.eng == "pe" and op.eng == "pe":
            return
        op.deps.append(dep)
        if not dep.is_dma:
            dep.signal = True

    def op(self, eng, meth, kw=None, reads=(), writes=(), dma_sem=None):
        o = Op(eng, meth, kw)
        if dma_sem is not None:
            o.is_dma = True
            o.dsem = dma_sem
            self.dma_counts[dma_sem] = self.dma_counts.get(dma_sem, 0) + 16
            o.dcount = self.dma_counts[dma_sem]
        for r in reads:
            self._add_dep(o, r.writer)
        for w in writes:
            self._add_dep(o, w.writer)
            for rd in w.readers.values():
                self._add_dep(o, rd)
        for w in writes:
            w.writer = o
            w.readers = {}
        for r in reads:
            r.readers[id(o) if o.is_dma else o.eng] = o
        self.ops[eng].append(o)
        return o

    def emit(self, sems, dma_sems):
        nc = self.nc
        for e in ENGS:
            c = 0
            for o in self.ops[e]:
                if (not o.is_dma) and o.signal:
                    c += 1
                    o.sigval = c
        engobj = {"pe": "tensor", "act": "scalar", "dve": "vector", "pool": "gpsimd", "sp": "sync"}
        with nc.Block() as block:
            for e in ENGS:
                ops = self.ops[e]
                if not ops:
                    continue

                def body(eng, ops=ops, e=e):
                    waited = {}
                    for o in ops:
                        need = {}
                        for d in o.deps:
                            if d.is_dma:
                                key, val = ("d", d.dsem), d.dcount
                            else:
                                key, val = ("e", d.eng), d.sigval
                            if val > need.get(key, 0):
                                need[key] = val
                        todo = []
                        for key, val in need.items():
                            if waited.get(key, 0) >= val:
                                continue
                            waited[key] = val
                            todo.append((key, val))
                        emb = None
                        if o.meth is not None and not o.is_dma:
                            for t in todo:
                                if t[0][0] == "e":
                                    emb = t
                            if emb is not None:
                                todo.remove(emb)
                        for key, val in todo:
                            sem = dma_sems[key[1]] if key[0] == "d" else sems[key[1]]
                            eng.wait_ge(sem, val)
                        if o.meth is None:
                            continue
                        ins = getattr(eng, o.meth)(**o.kw)
                        if emb is not None:
                            ins._wait_ge(sems[emb[0][1]], emb[1])
                        if o.is_dma:
                            ins.then_inc(dma_sems[o.dsem], 16)
                        elif o.signal:
                            ins.then_inc(sems[e], 1)

                getattr(block, engobj[e])(body)


class Rng:
    def __init__(self, lo, n, nt):
        assert lo + n * nt == TL and n <= 512
        self.lo, self.n, self.nt = lo, n, nt


R_L = [
    (Rng(0, 362, 3), Rng(30, 352, 3)),
    (Rng(30, 352, 3), Rng(62, 512, 2)),
]
R_FINAL = Rng(62, 512, 2)


def _par_base(l):
    return 48 + l * 304


def build_program(depth_run=DEPTH):
    nc = bass.Bass("TRN2", target_bir_lowering=False)
    xT = nc.dram_tensor("xT", [D, TL], F32, kind="ExternalInput").ap()
    pT = nc.dram_tensor("pT", [DEPTH, PLE, TL], F32, kind="ExternalInput").ap()
    par_d = nc.dram_tensor("par", [128, NPAR], F32, kind="ExternalInput").ap()
    icnt_d = nc.dram_tensor("icnt", [128, 64], F32, kind="ExternalInput").ap()
    ident_d = nc.dram_tensor("ident", [128, 128], F32, kind="ExternalInput").ap()
    w_in = nc.dram_tensor("w_in", [DEPTH, D, IN_COLS], F32, kind="ExternalInput").ap()
    pool_w = nc.dram_tensor("pool_w", [DEPTH, 4, 256, 256], F32, kind="ExternalInput").ap()
    w_bo = nc.dram_tensor("w_branch_out", [DEPTH, 3, E, D], F32, kind="ExternalInput").ap()
    w_o = nc.dram_tensor("w_o", [DEPTH, D, D], F32, kind="ExternalInput").ap()
    w_pg = nc.dram_tensor("w_ple_gate", [DEPTH, D, D], F32, kind="ExternalInput").ap()
    w_pp = nc.dram_tensor("w_ple_proj", [DEPTH, PLE, D], F32, kind="ExternalInput").ap()
    outT = nc.dram_tensor("outT", [D, TOK], F32, kind="ExternalOutput").ap()

    S = Sched(nc)
    dma_names = ["x0", "x1", "x2", "x3", "par", "ident", "p", "ring0", "ring1", "side0", "side1", "out0", "out1", "out2", "out3"]

    def ACT(out, in_, func, reads, writes, **kw):
        S.op("act", "activation", dict(out=out, in_=in_, func=func, **kw), reads, writes)

    def TT(out, in0, in1, op, reads, writes):
        S.op("dve", "tensor_tensor", dict(out=out, in0=in0, in1=in1, op=op), reads, writes)

    def TS(out, in0, s1, s2, op0, op1, reads, writes):
        kw = dict(out=out, in0=in0, scalar1=s1, scalar2=s2, op0=op0)
        if op1 is not None:
            kw["op1"] = op1
        S.op("dve", "tensor_scalar", kw, reads, writes)

    def STT(out, in0, scalar, in1, op0, op1, reads, writes):
        S.op("dve", "scalar_tensor_tensor", dict(out=out, in0=in0, scalar=scalar, in1=in1, op0=op0, op1=op1), reads, writes)

    def RECIP(out, in_, reads, writes):
        S.op("dve", "reciprocal", dict(out=out, in_=in_), reads, writes)

    def MM(out, lhsT, rhs, start, stop, reads, writes):
        S.op("pe", "matmul", dict(out=out, lhsT=lhsT, rhs=rhs, start=start, stop=stop), reads, writes)

    def DMA(eng, out, in_, reads, writes, sem):
        S.op(eng, "dma_start", dict(out=out, in_=in_), reads, writes, dma_sem=sem)

    with ExitStack() as es:
        ec = es.enter_context
        h = ec(nc.sbuf_tensor("h", [128, 16, TL], F32))
        xn = ec(nc.sbuf_tensor("xn", [128, 16, TL], BF16))
        acc = ec(nc.sbuf_tensor("acc", [128, 16, TL], BF16))
        yb = ec(nc.sbuf_tensor("yb", [128, 8, TL], BF16))
        ring = ec(nc.sbuf_tensor("ring", [128, 2, 16, 256], BF16))
        side = ec(nc.sbuf_tensor("side", [128, 2, 2, 256], BF16))
        dgb = ec(nc.sbuf_tensor("dgb", [128, 31 * 128], BF16))
        par = ec(nc.sbuf_tensor("par_s", [128, NPAR], F32))
        icnt = ec(nc.sbuf_tensor("icnt_s", [128, 64], F32))
        ident = ec(nc.sbuf_tensor("ident_s", [128, 128], BF16))
        ones = ec(nc.sbuf_tensor("ones_s", [128, 128], BF16))
        tmp16 = ec(nc.sbuf_tensor("tmp16", [128, 16], F32))
        Tt = ec(nc.sbuf_tensor("Tt", [128, 6, TL], F32))
        psa = ec(nc.psum_tensor("psa", [128, 8, 512], F32))
        s_pe, s_act, s_dve, s_pool, s_sp = [ec(nc.semaphore(n)) for n in ("s_pe", "s_act", "s_dve", "s_pool", "s_sp")]
        dlist = [ec(nc.semaphore(f"d{i}")) for i in range(len(dma_names))]
        dsems = dict(zip(dma_names, dlist))
        r_bank = [Res(f"bank{i}") for i in range(8)]
        r_h = [Res(f"h{k}") for k in range(16)]
        r_xn = [Res(f"xn{k}") for k in range(16)]
        r_acc = [Res(f"acc{k}") for k in range(16)]
        r_yb = [Res(f"yb{k}") for k in range(8)]
        r_ring = [Res("ring0"), Res("ring1")]
        r_side = [Res("side0"), Res("side1")]
        r_dgk = [Res(f"dg{k}") for k in range(31)]
        r_par = Res("par")
        r_ident = Res("ident")
        r_ones = Res("ones")
        r_tmp16 = Res("tmp16")
        r_Ta = [Res(f"T{i}a") for i in range(6)]
        r_Tb = [Res(f"T{i}b") for i in range(6)]
        r_out = [Res(f"out{k}") for k in range(16)]

        dg = dgb[:, :].rearrange("p (k c) -> p k c", c=128)
        pTs = dgb[:, 0:2 * TL].rearrange("p (k t) -> p k t", t=TL)

        def TF(i):
            return Tt[:, i, :], [r_Ta[i], r_Tb[i]]

        def TB(i, half):
            v = Tt[:, i, :].bitcast(BF16)
            return v[:, half * TL:(half + 1) * TL], [r_Ta[i] if half == 0 else r_Tb[i]]

        def v3(ap2d, R):
            return ap2d[:, R.lo:TL].rearrange("p (j n) -> p j n", n=R.n)

        class PGrp:
            def __init__(self, base, nt):
                self.base, self.nt = base, nt
                self.res = r_bank[base:base + nt]

        def pv(g, R):
            return psa[:, g.base:g.base + R.nt, 0:R.n]

        def pcol(c):
            return par[:, c:c + 1]

        state = {"pg": 0, "ring": 0, "side": 0, "reserved": set()}

        def next_pg(nt):
            b = state["pg"]
            for _ in range(8):
                if b + nt > 8:
                    b = 0
                if not (set(range(b, b + nt)) & state["reserved"]):
                    break
                b += nt
            assert b + nt <= 8 and not (set(range(b, b + nt)) & state["reserved"])
            state["pg"] = b + nt
            return PGrp(b, nt)

        def load_main(src2d, kch):
            s = state["ring"]
            state["ring"] ^= 1
            hold = [r_h[11]] if state.setdefault("n_main", 0) < 2 else []
            state["n_main"] += 1
            DMA("pool", ring[:, s, 0:kch, :], src2d.rearrange("(k p) c -> p k c", p=128), hold, [r_ring[s]], f"ring{s}")
            return s

        def load_main512(src2d):
            s = state["ring"]
            state["ring"] ^= 1
            dst = ring[:, s, :, :].rearrange("p (k a) c -> p k (a c)", a=2)
            DMA("pool", dst, src2d.rearrange("(k p) c -> p k c", p=128), [], [r_ring[s]], f"ring{s}")
            return s

        def load_side(src2d):
            s = state["side"]
            state["side"] ^= 1
            DMA("pool", side[:, s, :, :], src2d.rearrange("(k p) c -> p k c", p=128), [], [r_side[s]], f"side{s}")
            return s

        def mm_group(steps, R):
            g = next_pg(R.nt)
            ns = len(steps)
            for i, (lhsT, rhs2d, shift, reads) in enumerate(steps):
                for tt in range(R.nt):
                    lo = R.lo + tt * R.n + shift
                    MM(psa[:, g.base + tt, 0:R.n], lhsT, rhs2d[:, lo:lo + R.n], (i == 0), (i == ns - 1), reads, g.res)
            return g

        def proj_steps(slot, coff, src, r_src):
            return [(ring[:, slot, k, coff:coff + 128], src[:, k, :], 0, [r_ring[slot], r_src[k]]) for k in range(16)]

        for i4 in range(4):
            DMA("sp", h[:, 4 * i4:4 * i4 + 4, :], xT[512 * i4:512 * i4 + 512, :].rearrange("(k p) t -> p k t", p=128), [],
                r_h[4 * i4:4 * i4 + 4], f"x{i4}")
        DMA("sp", par[:, :], par_d, [], [r_par], "par")
        DMA("sp", icnt[:, :], icnt_d, [], [r_par], "par")
        DMA("pool", ident[:, :], ident_d, [r_h[11]], [r_ident], "ident")
        S.op("dve", "memset", dict(ap=ones[:, :], constant=1.0), [], [r_ones])

        def rms_sq(k, R):
            if k % 2 == 0:
                sq, r_sq = TB(5, (k // 2) % 2)
                ACT(sq[:, R.lo:TL], h[:, k, R.lo:TL], AF.Square, [r_h[k]], r_sq)
            else:
                sq, r_sq = TB(4, (k // 2) % 2)
                TT(sq[:, R.lo:TL], h[:, k, R.lo:TL], h[:, k, R.lo:TL], ALU.mult, [r_h[k]], r_sq)
            return sq, r_sq

        def rms_mm(g, k, R, sq, r_sq):
            for tt in range(R.nt):
                lo = R.lo + tt * R.n
                MM(psa[:, g.base + tt, 0:R.n], ones[:, :], sq[:, lo:lo + R.n], (k == 0), (k == 15), [r_ones] + r_sq, g.res)

        def rms_step(g, k, R):
            sq, r_sq = rms_sq(k, R)
            rms_mm(g, k, R, sq, r_sq)

        def rms_finish(g, R):
            rs, r_rs = TF(0)
            ACT(v3(rs, R), pv(g, R), AF.Ln, g.res + [r_par], r_rs, scale=1.0 / D, bias=pcol(C_RMS_EPS))
            ACT(rs[:, R.lo:TL], rs[:, R.lo:TL], AF.Exp, r_rs, r_rs, scale=-0.5)
            return rs, r_rs

        def rms_stats(R):
            g = next_pg(R.nt)
            for k in range(16):
                rms_step(g, k, R)
            return rms_finish(g, R)

        for l in range(depth_run):
            R_in, R_out = R_L[l]
            PB = _par_base(l)
            wl = w_in[l]
            LO = R_out.lo

            rs, r_rs = rms_stats(R_in)
            for k in range(16):
                STT(xn[:, k, R_in.lo:TL], h[:, k, R_in.lo:TL], pcol(l * 16 + k), rs[:, R_in.lo:TL], ALU.mult, ALU.mult,
                    [r_h[k], r_par] + r_rs, [r_xn[k]])

            def merge_branch(b):
                for q in range(4):
                    sgs = []
                    for hpair in range(2):
                        d0c = q * 4 + hpair * 2
                        c0 = 9216 + b * 2048 + d0c * 128
                        slot = load_main(wl[:, c0:c0 + 256], 16)
                        for jj in range(2):
                            g = mm_group(proj_steps(slot, jj * 128, xn, r_xn), R_out)
                            sg, r_sg = TF(hpair * 2 + jj)
                            ACT(v3(sg, R_out), pv(g, R_out), AF.Sigmoid, g.res, r_sg)
                            sgs.append((sg, r_sg))
                    slot = load_main512(w_bo[l, b][:, q * 512:(q + 1) * 512])
                    wv = ring[:, slot, :, :].rearrange("p (k a) c -> p k (a c)", a=2)
                    for jj in range(4):
                        d = q * 4 + jj
                        steps = [(wv[:, k, jj * 128:(jj + 1) * 128], yb[:, k, :], 0, [r_ring[slot], r_yb[k]]) for k in range(8)]
                        g = mm_group(steps, R_out)
                        sg, r_sg = sgs[jj]
                        if b == 0:
                            TT(v3(acc[:, d, :], R_out), pv(g, R_out), v3(sg, R_out), ALU.mult, g.res + r_sg, [r_acc[d]])
                        else:
                            t, r_t = TF(4 + (jj % 2))
                            TT(v3(t, R_out), pv(g, R_out), v3(sg, R_out), ALU.mult, g.res + r_sg, r_t)
                            TT(acc[:, d, LO:TL], acc[:, d, LO:TL], t[:, LO:TL], ALU.add, [r_acc[d]] + r_t, [r_acc[d]])

            for j in range(4):
                u = [2 * j, 2 * j + 1]
                tA = [TF(0), TF(1)]
                cv = [TF(2), TF(3)]
                slot = load_main(wl[:, 0 * E + u[0] * 128: 0 * E + u[0] * 128 + 256], 16)
                for i in range(2):
                    g = mm_group(proj_steps(slot, i * 128, xn, r_xn), R_in)
                    ACT(v3(tA[i][0], R_in), pv(g, R_in), AF.Copy, g.res, tA[i][1])
                slot = load_main(wl[:, 2 * E + u[0] * 128: 2 * E + u[0] * 128 + 256], 16)
                for i in range(2):
                    g = mm_group(proj_steps(slot, i * 128, xn, r_xn), R_in)
                    cx, r_cx = tA[i]
                    co, r_co = cv[i]
                    TT(v3(cx, R_in), pv(g, R_in), v3(cx, R_in), ALU.mult, g.res + r_cx, r_cx)
                    wc = PB + u[i] * 3
                    TS(co[:, LO:TL], cx[:, LO:TL], pcol(wc + 2), None, ALU.mult, None, r_cx + [r_par], r_co)
                    for sh in (1, 2):
                        STT(co[:, LO:TL], cx[:, LO - sh:TL - sh], pcol(wc + 2 - sh), co[:, LO:TL], ALU.mult, ALU.add,
                            r_cx + r_co + [r_par], r_co)
                slot = load_main(wl[:, 1 * E + u[0] * 128: 1 * E + u[0] * 128 + 256], 16)
                for i in range(2):
                    g = mm_group(proj_steps(slot, i * 128, xn, r_xn), R_out)
                    co, r_co = cv[i]
                    TT(v3(co, R_out), pv(g, R_out), v3(co, R_out), ALU.mult, g.res + r_co, r_co)
                slot = load_main(wl[:, 3 * E + u[0] * 128: 3 * E + u[0] * 128 + 256], 16)
                for i in range(2):
                    g = mm_group(proj_steps(slot, i * 128, xn, r_xn), R_out)
                    sz, r_sz = tA[i]
                    co, r_co = cv[i]
                    ACT(v3(sz, R_out), pv(g, R_out), AF.Silu, g.res, r_sz)
                    TT(yb[:, u[i], LO:TL], co[:, LO:TL], sz[:, LO:TL], ALU.mult, r_co + r_sz, [r_yb[u[i]]])
            merge_branch(0)

            for j in range(4):
                u = [2 * j, 2 * j + 1]
                wlen = 2 ** (j + 1)
                U = [TF(0), TF(1)]
                XY = [TF(2), TF(3)]
                P = [TB(4, 0), TB(4, 1)]
                slot = load_main(wl[:, 4 * E + u[0] * 128: 4 * E + u[0] * 128 + 256], 16)
                for i in range(2):
                    g = mm_group(proj_steps(slot, i * 128, xn, r_xn), R_in)
                    Ui, r_Ui = U[i]
                    ACT(v3(Ui, R_in), pv(g, R_in), AF.Copy, g.res, r_Ui)
                    src, r_src = Ui, r_Ui
                    lo_valid = R_in.lo
                    sh = 1
                    st = 0
                    while sh < wlen:
                        dst, r_dst = XY[st % 2]
                        lo_new = lo_valid + sh
                        TT(dst[:, lo_new:TL], src[:, lo_new:TL], src[:, lo_new - sh:TL - sh], ALU.add, r_src, r_dst)
                        src, r_src = dst, r_dst
                        lo_valid = lo_new
                        sh *= 2
                        st += 1
                    assert lo_valid <= LO
                    Pi, r_Pi = P[i]
                    STT(Pi[:, LO:TL], src[:, LO:TL], 1.0 / wlen, Ui[:, LO:TL], ALU.mult, ALU.subtract, r_src + r_Ui, r_Pi)
                    TT(tmp16[:, :], src[:, HALO:HALO + 16], icnt[:, j * 16:(j + 1) * 16], ALU.mult, r_src + [r_par], [r_tmp16])
                    TT(Pi[:, HALO:HALO + 16], tmp16[:, :], Ui[:, HALO:HALO + 16], ALU.subtract, [r_tmp16] + r_Ui, r_Pi)
                slot = load_main(wl[:, 5 * E + u[0] * 128: 5 * E + u[0] * 128 + 256], 16)
                for i in range(2):
                    g = mm_group(proj_steps(slot, i * 128, xn, r_xn), R_out)
                    ACT(v3(U[i][0], R_out), pv(g, R_out), AF.Silu, g.res, U[i][1])
                ss = load_side(pool_w[l, j])
                for o in range(2):
                    steps = [(side[:, ss, i, o * 128:(o + 1) * 128], P[i][0], 0, [r_side[ss]] + P[i][1]) for i in range(2)]
                    g = mm_group(steps, R_out)
                    STT(v3(yb[:, u[o], :], R_out), pv(g, R_out), pcol(PB + 24 + u[o]), v3(U[o][0], R_out), ALU.mult, ALU.mult,
                        g.res + [r_par] + U[o][1], [r_yb[u[o]]])
            merge_branch(1)

            for j in range(4):
                u = [2 * j, 2 * j + 1]
                SG = [TF(0), TF(1)]
                V = [TB(2, 0), TB(2, 1)]

                def build_diag(ui):
                    wc = PB + 56 + ui * 31
                    for k in range(31):
                        TS(dg[:, k, :], ident[:, :], pcol(wc + k), None, ALU.mult, None, [r_ident, r_par], [r_dgk[k]])

                build_diag(u[0])
                slot = load_main(wl[:, 7 * E + u[0] * 128: 7 * E + u[0] * 128 + 256], 16)
                for i in range(2):
                    g = mm_group(proj_steps(slot, i * 128, xn, r_xn), R_in)
                    ACT(v3(SG[i][0], R_in), pv(g, R_in), AF.Sigmoid, g.res, SG[i][1])
                slot = load_main(wl[:, 6 * E + u[0] * 128: 6 * E + u[0] * 128 + 256], 16)
                for i in range(2):
                    g = mm_group(proj_steps(slot, i * 128, xn, r_xn), R_in)
                    TT(v3(V[i][0], R_in), pv(g, R_in), v3(SG[i][0], R_in), ALU.mult, g.res + SG[i][1], V[i][1])
                for i in range(2):
                    if i == 1:
                        build_diag(u[1])
                    Vi, r_Vi = V[i]
                    steps = [(dg[:, k, :], Vi, k - 30, [r_dgk[k]] + r_Vi) for k in range(31)]
                    g = mm_group(steps, R_out)
                    ACT(v3(yb[:, u[i], :], R_out), pv(g, R_out), AF.Identity, g.res + [r_par], [r_yb[u[i]]], bias=pcol(PB + 32 + u[i]))
            steps = [(ones[:, :], yb[:, k, :], 0, [r_ones, r_yb[k]]) for k in range(8)]
            g1 = mm_group(steps, R_out)
            mu, r_mu = TF(3)
            TS(v3(mu, R_out), pv(g1, R_out), 1.0 / E, None, ALU.mult, None, g1.res, r_mu)
            g2 = next_pg(R_out.nt)
            for k in range(8):
                if k % 2 == 0:
                    sq, r_sq = TB(5, (k // 2) % 2)
                    ACT(sq[:, LO:TL], yb[:, k, LO:TL], AF.Square, [r_yb[k]], r_sq)
                else:
                    sq, r_sq = TB(4, (k // 2) % 2)
                    TT(sq[:, LO:TL], yb[:, k, LO:TL], yb[:, k, LO:TL], ALU.mult, [r_yb[k]], r_sq)
                for tt in range(R_out.nt):
                    lo = LO + tt * R_out.n
                    MM(psa[:, g2.base + tt, 0:R_out.n], ones[:, :], sq[:, lo:lo + R_out.n], (k == 0), (k == 7), [r_ones] + r_sq, g2.res)
            rstd, r_rstd = TF(4)
            TS(v3(rstd, R_out), pv(g2, R_out), 1.0 / E, None, ALU.mult, None, g2.res, r_rstd)
            msq, r_msq = TF(5)
            TT(msq[:, LO:TL], mu[:, LO:TL], mu[:, LO:TL], ALU.mult, r_mu, r_msq)
            TT(rstd[:, LO:TL], rstd[:, LO:TL], msq[:, LO:TL], ALU.subtract, r_rstd + r_msq, r_rstd)
            TS(rstd[:, LO:TL], rstd[:, LO:TL], 0.0, LN_EPS, ALU.max, ALU.add, r_rstd, r_rstd)
            ACT(rstd[:, LO:TL], rstd[:, LO:TL], AF.Ln, r_rstd, r_rstd)
            ACT(rstd[:, LO:TL], rstd[:, LO:TL], AF.Exp, r_rstd, r_rstd, scale=-0.5)
            for j in range(4):
                u = [2 * j, 2 * j + 1]
                slot = load_main(wl[:, 8 * E + u[0] * 128: 8 * E + u[0] * 128 + 256], 16)
                for i in range(2):
                    g = mm_group(proj_steps(slot, i * 128, xn, r_xn), R_out)
                    Z, r_Z = TF(i)
                    ACT(v3(Z, R_out), pv(g, R_out), AF.Silu, g.res, r_Z)
                    a, r_a = TF(2)
                    s, r_s = TF(5)
                    ui = u[i]
                    TT(a[:, LO:TL], yb[:, ui, LO:TL], mu[:, LO:TL], ALU.subtract, [r_yb[ui]] + r_mu, r_a)
                    TT(a[:, LO:TL], a[:, LO:TL], rstd[:, LO:TL], ALU.mult, r_a + r_rstd, r_a)
                    ACT(s[:, LO:TL], a[:, LO:TL], AF.Silu, r_a + [r_par], r_s, bias=pcol(PB + 48 + ui), scale=pcol(PB + 40 + ui))
                    TT(yb[:, ui, LO:TL], s[:, LO:TL], Z[:, LO:TL], ALU.mult, r_s + r_Z, [r_yb[ui]])
            merge_branch(2)

            for q in range(8):
                slot = load_main(w_o[l][:, q * 256:(q + 1) * 256], 16)
                for i in range(2):
                    oc = q * 2 + i
                    g = mm_group(proj_steps(slot, i * 128, acc, r_acc), R_out)
                    TT(v3(h[:, oc, :], R_out), pv(g, R_out), v3(h[:, oc, :], R_out), ALU.add, g.res + [r_h[oc]], [r_h[oc]])
                    ACT(xn[:, oc, LO:TL], h[:, oc, LO:TL], AF.Copy, [r_h[oc]], [r_xn[oc]])

            DMA("pool", pTs, pT[l].rearrange("(k p) t -> p k t", p=128), [], r_dgk, "p")
            last = (l == depth_run - 1)
            if last:
                if R_out.nt == 2:
                    state["pg"] = 6
                gF = next_pg(R_out.nt)
                state["reserved"] = set(range(gF.base, gF.base + R_out.nt))
            for q in range(8):
                pend = []
                if last and q >= 1:
                    for kk in (2 * q - 2, 2 * q - 1):
                        pend.append((kk,) + rms_sq(kk, R_out))
                slot = load_main(w_pg[l][:, q * 256:(q + 1) * 256], 16)
                ss = load_side(w_pp[l][:, q * 256:(q + 1) * 256])
                sgs = []
                for i in range(2):
                    g = mm_group(proj_steps(slot, i * 128, xn, r_xn), R_out)
                    sg, r_sg = TF(i)
                    ACT(v3(sg, R_out), pv(g, R_out), AF.Sigmoid, g.res, r_sg)
                    sgs.append((sg, r_sg))
                for kk, sq, r_sq in pend:
                    rms_mm(gF, kk, R_out, sq, r_sq)
                for i in range(2):
                    oc = q * 2 + i
                    steps = [(side[:, ss, k, i * 128:(i + 1) * 128], pTs[:, k, :], 0, [r_side[ss]] + r_dgk[0:17]) for k in range(2)]
                    g = mm_group(steps, R_out)
                    sg, r_sg = sgs[i]
                    t, r_t = TF(2 + i)
                    TT(v3(t, R_out), pv(g, R_out), v3(sg, R_out), ALU.mult, g.res + r_sg, r_t)
                    TT(h[:, oc, LO:TL], h[:, oc, LO:TL], t[:, LO:TL], ALU.add, [r_h[oc]] + r_t, [r_h[oc]])

        Rf = R_L[depth_run - 1][1]
        for kk in (14, 15):
            rms_step(gF, kk, Rf)
        state["reserved"] = set()
        rs, r_rs = rms_finish(gF, Rf)
        for k in range(16):
            ob, r_ob = TF(1 + (k % 4))
            STT(ob[:, HALO:TL], h[:, k, HALO:TL], pcol(32 + k), rs[:, HALO:TL], ALU.mult, ALU.mult, [r_h[k], r_par] + r_rs, r_ob)
            DMA("sp", outT[k * 128:(k + 1) * 128, :], ob[:, HALO:TL], r_ob, [r_out[k]], f"out{k % 4}")
        S.op("sp", None, None, r_out, [])

        S.emit({"pe": s_pe, "act": s_act, "dve": s_dve, "pool": s_pool, "sp": s_sp}, dsems)
    return nc


def _pack_params(norm_g, conv_a_w, pool_scale, conv_c_w, conv_c_b, ln_c_g, ln_c_b, final_norm_g):
    par = np.zeros((128, NPAR), dtype=np.float32)
    for l in range(DEPTH):
        par[:, l * 16:(l + 1) * 16] = norm_g[l].reshape(16, 128).T
        B = _par_base(l)
        par[:, B:B + 24] = conv_a_w[l].reshape(3, 8, 128).transpose(2, 1, 0).reshape(128, 24)
        par[:, B + 24:B + 32] = pool_scale[l].reshape(8, 128).T
        par[:, B + 32:B + 40] = conv_c_b[l].reshape(8, 128).T
        par[:, B + 40:B + 48] = ln_c_g[l].reshape(8, 128).T
        par[:, B + 48:B + 56] = ln_c_b[l].reshape(8, 128).T
        par[:, B + 56:B + 304] = conv_c_w[l].reshape(31, 8, 128).transpose(2, 1, 0).reshape(128, 248)
    par[:, 32:48] = final_norm_g.reshape(16, 128).T
    par[:, C_RMS_EPS] = RMS_EPS
    par[:, C_RMS_EPS + 1] = LN_EPS
    return par


def _icnt_table(half):
    t = np.zeros((128, 64), dtype=np.float32)
    for g, w in enumerate((2, 4, 8, 16)):
        for i in range(16):
            pos = half * TOK + i
            t[:, g * 16 + i] = np.float32(1.0) / np.float32(min(pos + 1, w))
    return t


_NC_CACHE = {}


def kernel(x, p, norm_g, w_in, conv_a_w, pool_w, pool_scale, conv_c_w, conv_c_b,
           ln_c_g, ln_c_b, w_branch_out, w_o, w_ple_gate, w_ple_proj, final_norm_g):
    f = lambda a: np.ascontiguousarray(np.asarray(a, dtype=np.float32))
    x, p = f(x), f(p)
    w_in, pool_w, w_branch_out, w_o, w_ple_gate, w_ple_proj = map(f, (w_in, pool_w, w_branch_out, w_o, w_ple_gate, w_ple_proj))
    par = _pack_params(*map(f, (norm_g, conv_a_w, pool_scale, conv_c_w, conv_c_b, ln_c_g, ln_c_b, final_norm_g)))
    ident = np.eye(128, dtype=np.float32)
    if "nc" not in _NC_CACHE:
        _NC_CACHE["nc"] = build_program()
    nc = _NC_CACHE["nc"]
    in_maps = []
    for c in range(NCORES):
        b, half = c // 2, c % 2
        xT = np.zeros((D, TL), dtype=np.float32)
        pT = np.zeros((DEPTH, PLE, TL), dtype=np.float32)
        g0 = half * TOK - HALO
        s0 = max(g0, 0)
        xT[:, s0 - g0:] = x[b, s0:half * TOK + TOK, :].T
        for l in range(DEPTH):
            pT[l][:, s0 - g0:] = p[l, b, s0:half * TOK + TOK, :].T
        in_maps.append({
            "xT": xT, "pT": pT, "par": par, "icnt": _icnt_table(half), "ident": ident,
            "w_in": w_in, "pool_w": pool_w, "w_branch_out": w_branch_out, "w_o": w_o,
            "w_ple_gate": w_ple_gate, "w_ple_proj": w_ple_proj,
        })
    res = run_bass_kernel_spmd(nc, in_maps, core_ids=list(range(NCORES)))
    out = np.empty((BATCH, SEQ, D), dtype=np.float32)
    for c in range(NCORES):
        b, half = c // 2, c % 2
        out[b, half * TOK:(half + 1) * TOK, :] = res.results[c]["outT"].T
    return out
```

```python
from contextlib import ExitStack

import numpy as np
import concourse.bass as bass
import concourse.mybir as mybir
from concourse.bass_utils import run_bass_kernel_spmd

F32 = mybir.dt.float32
BF16 = mybir.dt.bfloat16
AF = mybir.ActivationFunctionType
ALU = mybir.AluOpType

D = 2048
E = 1024
SEQ = 2048
BATCH = 4
DEPTH = 2
PLE = 256
IN_COLS = 15360
HALO = 62
TOK = 1024
TL = TOK + HALO
NCORES = 8
RMS_EPS = 1e-6
LN_EPS = 1e-5
NPAR = 48 + 2 * 304 + 2
C_RMS_EPS = 48 + 2 * 304
ENGS = ("pe", "act", "dve", "pool", "sp")


class Res:
    __slots__ = ("name", "writer", "readers")

    def __init__(self, name):
        self.name = name
        self.writer = None
        self.readers = {}


class Op:
    __slots__ = ("eng", "meth", "kw", "deps", "is_dma", "dsem", "dcount", "signal", "sigval")

    def __init__(self, eng, meth, kw):
        self.eng = eng
        self.meth = meth
        self.kw = kw
        self.deps = []
        self.is_dma = False
        self.dsem = None
        self.dcount = 0
        self.signal = False
        self.sigval = 0


class Sched:
    def __init__(self, nc):
        self.nc = nc
        self.ops = {e: [] for e in ENGS}
        self.dma_counts = {}

    def _add_dep(self, op, dep):
        if dep is None or dep is op:
            return
        if (not dep.is_dma) and (not op.is_dma) and dep.eng == "pe" and op.eng == "pe":
            return
        op.deps.append(dep)
        if not dep.is_dma:
            dep.signal = True

    def op(self, eng, meth, kw=None, reads=(), writes=(), dma_sem=None):
        o = Op(eng, meth, kw)
        if dma_sem is not None:
            o.is_dma = True
            o.dsem = dma_sem
            self.dma_counts[dma_sem] = self.dma_counts.get(dma_sem, 0) + 16
            o.dcount = self.dma_counts[dma_sem]
        for r in reads:
            self._add_dep(o, r.writer)
        for w in writes:
            self._add_dep(o, w.writer)
            for rd in w.readers.values():
                self._add_dep(o, rd)
        for w in writes:
            w.writer = o
            w.readers = {}
        for r in reads:
            r.readers[id(o) if o.is_dma else o.eng] = o
        self.ops[eng].append(o)
        return o

    def emit(self, sems, dma_sems):
        nc = self.nc
        for e in ENGS:
            c = 0
            for o in self.ops[e]:
                if (not o.is_dma) and o.signal:
                    c += 1
                    o.sigval = c
        engobj = {"pe": "tensor", "act": "scalar", "dve": "vector", "pool": "gpsimd", "sp": "sync"}
        with nc.Block() as block:
            for e in ENGS:
                ops = self.ops[e]
                if not ops:
                    continue

                def body(eng, ops=ops, e=e):
                    waited = {}
                    for o in ops:
                        need = {}
                        for d in o.deps:
                            if d.is_dma:
                                key, val = ("d", d.dsem), d.dcount
                            else:
                                key, val = ("e", d.eng), d.sigval
                            if val > need.get(key, 0):
                                need[key] = val
                        todo = []
                        for key, val in need.items():
                            if waited.get(key, 0) >= val:
                                continue
                            waited[key] = val
                            todo.append((key, val))
                        emb = None
                        if o.meth is not None and not o.is_dma:
                            for t in todo:
                                if t[0][0] == "e":
                                    emb = t
                            if emb is not None:
                                todo.remove(emb)
                        for key, val in todo:
                            sem = dma_sems[key[1]] if key[0] == "d" else sems[key[1]]
                            eng.wait_ge(sem, val)
                        if o.meth is None:
                            continue
                        ins = getattr(eng, o.meth)(**o.kw)
                        if emb is not None:
                            ins._wait_ge(sems[emb[0][1]], emb[1])
                        if o.is_dma:
                            ins.then_inc(dma_sems[o.dsem], 16)
                        elif o.signal:
                            ins.then_inc(sems[e], 1)

                getattr(block, engobj[e])(body)


class Rng:
    def __init__(self, lo, n, nt):
        assert lo + n * nt == TL and n <= 512
        self.lo, self.n, self.nt = lo, n, nt


R_L = [
    (Rng(0, 362, 3), Rng(30, 352, 3)),
    (Rng(30, 352, 3), Rng(62, 512, 2)),
]
R_FINAL = Rng(62, 512, 2)


def _par_base(l):
    return 48 + l * 304


def build_program(depth_run=DEPTH):
    nc = bass.Bass("TRN2", target_bir_lowering=False)
    xT = nc.dram_tensor("xT", [D, TL], F32, kind="ExternalInput").ap()
    pT = nc.dram_tensor("pT", [DEPTH, PLE, TL], F32, kind="ExternalInput").ap()
    par_d = nc.dram_tensor("par", [128, NPAR], F32, kind="ExternalInput").ap()
    icnt_d = nc.dram_tensor("icnt", [128, 64], F32, kind="ExternalInput").ap()
    ident_d = nc.dram_tensor("ident", [128, 128], F32, kind="ExternalInput").ap()
    w_in = nc.dram_tensor("w_in", [DEPTH, D, IN_COLS], F32, kind="ExternalInput").ap()
    pool_w = nc.dram_tensor("pool_w", [DEPTH, 4, 256, 256], F32, kind="ExternalInput").ap()
    w_bo = nc.dram_tensor("w_branch_out", [DEPTH, 3, E, D], F32, kind="ExternalInput").ap()
    w_o = nc.dram_tensor("w_o", [DEPTH, D, D], F32, kind="ExternalInput").ap()
    w_pg = nc.dram_tensor("w_ple_gate", [DEPTH, D, D], F32, kind="ExternalInput").ap()
    w_pp = nc.dram_tensor("w_ple_proj", [DEPTH, PLE, D], F32, kind="ExternalInput").ap()
    outT = nc.dram_tensor("outT", [D, TOK], F32, kind="ExternalOutput").ap()

    S = Sched(nc)
    dma_names = ["x0", "x1", "x2", "x3", "par", "ident", "p", "ring0", "ring1", "side0", "side1", "out0", "out1", "out2", "out3"]

    def ACT(out, in_, func, reads, writes, **kw):
        S.op("act", "activation", dict(out=out, in_=in_, func=func, **kw), reads, writes)

    def TT(out, in0, in1, op, reads, writes):
        S.op("dve", "tensor_tensor", dict(out=out, in0=in0, in1=in1, op=op), reads, writes)

    def TS(out, in0, s1, s2, op0, op1, reads, writes):
        kw = dict(out=out, in0=in0, scalar1=s1, scalar2=s2, op0=op0)
        if op1 is not None:
            kw["op1"] = op1
        S.op("dve", "tensor_scalar", kw, reads, writes)

    def STT(out, in0, scalar, in1, op0, op1, reads, writes):
        S.op("dve", "scalar_tensor_tensor", dict(out=out, in0=in0, scalar=scalar, in1=in1, op0=op0, op1=op1), reads, writes)

    def RECIP(out, in_, reads, writes):
        S.op("dve", "reciprocal", dict(out=out, in_=in_), reads, writes)

    def MM(out, lhsT, rhs, start, stop, reads, writes):
        S.op("pe", "matmul", dict(out=out, lhsT=lhsT, rhs=rhs, start=start, stop=stop), reads, writes)

    def DMA(eng, out, in_, reads, writes, sem):
        S.op(eng, "dma_start", dict(out=out, in_=in_), reads, writes, dma_sem=sem)

    with ExitStack() as es:
        ec = es.enter_context
        h = ec(nc.sbuf_tensor("h", [128, 16, TL], F32))
        xn = ec(nc.sbuf_tensor("xn", [128, 16, TL], BF16))
        acc = ec(nc.sbuf_tensor("acc", [128, 16, TL], BF16))
        yb = ec(nc.sbuf_tensor("yb", [128, 8, TL], BF16))
        ring = ec(nc.sbuf_tensor("ring", [128, 2, 16, 256], BF16))
        side = ec(nc.sbuf_tensor("side", [128, 2, 2, 256], BF16))
        dgb = ec(nc.sbuf_tensor("dgb", [128, 31 * 128], BF16))
        par = ec(nc.sbuf_tensor("par_s", [128, NPAR], F32))
        icnt = ec(nc.sbuf_tensor("icnt_s", [128, 64], F32))
        ident = ec(nc.sbuf_tensor("ident_s", [128, 128], BF16))
        ones = ec(nc.sbuf_tensor("ones_s", [128, 128], BF16))
        tmp16 = ec(nc.sbuf_tensor("tmp16", [128, 16], F32))
        Tt = ec(nc.sbuf_tensor("Tt", [128, 6, TL], F32))
        psa = ec(nc.psum_tensor("psa", [128, 8, 512], F32))
        s_pe, s_act, s_dve, s_pool, s_sp = [ec(nc.semaphore(n)) for n in ("s_pe", "s_act", "s_dve", "s_pool", "s_sp")]
        dlist = [ec(nc.semaphore(f"d{i}")) for i in range(len(dma_names))]
        dsems = dict(zip(dma_names, dlist))
        r_bank = [Res(f"bank{i}") for i in range(8)]
        r_h = [Res(f"h{k}") for k in range(16)]
        r_xn = [Res(f"xn{k}") for k in range(16)]
        r_acc = [Res(f"acc{k}") for k in range(16)]
        r_yb = [Res(f"yb{k}") for k in range(8)]
        r_ring = [Res("ring0"), Res("ring1")]
        r_side = [Res("side0"), Res("side1")]
        r_dgk = [Res(f"dg{k}") for k in range(31)]
        r_par = Res("par")
        r_ident = Res("ident")
        r_ones = Res("ones")
        r_tmp16 = Res("tmp16")
        r_Ta = [Res(f"T{i}a") for i in range(6)]
        r_Tb = [Res(f"T{i}b") for i in range(6)]
        r_out = [Res(f"out{k}") for k in range(16)]

        dg = dgb[:, :].rearrange("p (k c) -> p k c", c=128)
        pTs = dgb[:, 0:2 * TL].rearrange("p (k t) -> p k t", t=TL)

        def TF(i):
            return Tt[:, i, :], [r_Ta[i], r_Tb[i]]

        def TB(i, half):
            v = Tt[:, i, :].bitcast(BF16)
            return v[:, half * TL:(half + 1) * TL], [r_Ta[i] if half == 0 else r_Tb[i]]

        def v3(ap2d, R):
            return ap2d[:, R.lo:TL].rearrange("p (j n) -> p j n", n=R.n)

        class PGrp:
            def __init__(self, base, nt):
                self.base, self.nt = base, nt
                self.res = r_bank[base:base + nt]

        def pv(g, R):
            return psa[:, g.base:g.base + R.nt, 0:R.n]

        def pcol(c):
            return par[:, c:c + 1]

        state = {"pg": 0, "ring": 0, "side": 0, "reserved": set()}

        def next_pg(nt):
            b = state["pg"]
            for _ in range(8):
                if b + nt > 8:
                    b = 0
                if not (set(range(b, b + nt)) & state["reserved"]):
                    break
                b += nt
            assert b + nt <= 8 and not (set(range(b, b + nt)) & state["reserved"])
            state["pg"] = b + nt
            return PGrp(b, nt)

        def load_main(src2d, kch):
            s = state["ring"]
            state["ring"] ^= 1
            hold = [r_h[11]] if state.setdefault("n_main", 0) < 2 else []
            state["n_main"] += 1
            DMA("pool", ring[:, s, 0:kch, :], src2d.rearrange("(k p) c -> p k c", p=128), hold, [r_ring[s]], f"ring{s}")
            return s

        def load_main512(src2d):
            s = state["ring"]
            state["ring"] ^= 1
            dst = ring[:, s, :, :].rearrange("p (k a) c -> p k (a c)", a=2)
            DMA("pool", dst, src2d.rearrange("(k p) c -> p k c", p=128), [], [r_ring[s]], f"ring{s}")
            return s

        def load_side(src2d):
            s = state["side"]
            state["side"] ^= 1
            DMA("pool", side[:, s, :, :], src2d.rearrange("(k p) c -> p k c", p=128), [], [r_side[s]], f"side{s}")
            return s

        def mm_group(steps, R):
            g = next_pg(R.nt)
            ns = len(steps)
            for i, (lhsT, rhs2d, shift, reads) in enumerate(steps):
                for tt in range(R.nt):
                    lo = R.lo + tt * R.n + shift
                    MM(psa[:, g.base + tt, 0:R.n], lhsT, rhs2d[:, lo:lo + R.n], (i == 0), (i == ns - 1), reads, g.res)
            return g

        def proj_steps(slot, coff, src, r_src):
            return [(ring[:, slot, k, coff:coff + 128], src[:, k, :], 0, [r_ring[slot], r_src[k]]) for k in range(16)]

        for i4 in range(4):
            DMA("sp", h[:, 4 * i4:4 * i4 + 4, :], xT[512 * i4:512 * i4 + 512, :].rearrange("(k p) t -> p k t", p=128), [],
                r_h[4 * i4:4 * i4 + 4], f"x{i4}")
        DMA("sp", par[:, :], par_d, [], [r_par], "par")
        DMA("sp", icnt[:, :], icnt_d, [], [r_par], "par")
        DMA("pool", ident[:, :], ident_d, [r_h[11]], [r_ident], "ident")
        S.op("dve", "memset", dict(ap=ones[:, :], constant=1.0), [], [r_ones])

        def rms_sq(k, R):
            if k % 2 == 0:
                sq, r_sq = TB(5, (k // 2) % 2)
                ACT(sq[:, R.lo:TL], h[:, k, R.lo:TL], AF.Square, [r_h[k]], r_sq)
            else:
                sq, r_sq = TB(4, (k // 2) % 2)
                TT(sq[:, R.lo:TL], h[:, k, R.lo:TL], h[:, k, R.lo:TL], ALU.mult, [r_h[k]], r_sq)
            return sq, r_sq

        def rms_mm(g, k, R, sq, r_sq):
            for tt in range(R.nt):
                lo = R.lo + tt * R.n
                MM(psa[:, g.base + tt, 0:R.n], ones[:, :], sq[:, lo:lo + R.n], (k == 0), (k == 15), [r_ones] + r_sq, g.res)

        def rms_step(g, k, R):
            sq, r_sq = rms_sq(k, R)
            rms_mm(g, k, R, sq, r_sq)

        def rms_finish(g, R):
            rs, r_rs = TF(0)
            ACT(v3(rs, R), pv(g, R), AF.Ln, g.res + [r_par], r_rs, scale=1.0 / D, bias=pcol(C_RMS_EPS))
            ACT(rs[:, R.lo:TL], rs[:, R.lo:TL], AF.Exp, r_rs, r_rs, scale=-0.5)
            return rs, r_rs

        def rms_stats(R):
            g = next_pg(R.nt)
            for k in range(16):
                rms_step(g, k, R)
            return rms_finish(g, R)

        for l in range(depth_run):
            R_in, R_out = R_L[l]
            PB = _par_base(l)
            wl = w_in[l]
            LO = R_out.lo

            rs, r_rs = rms_stats(R_in)
            for k in range(16):
                STT(xn[:, k, R_in.lo:TL], h[:, k, R_in.lo:TL], pcol(l * 16 + k), rs[:, R_in.lo:TL], ALU.mult, ALU.mult,
                    [r_h[k], r_par] + r_rs, [r_xn[k]])

            def merge_branch(b):
                for q in range(4):
                    sgs = []
                    for hpair in range(2):
                        d0c = q * 4 + hpair * 2
                        c0 = 9216 + b * 2048 + d0c * 128
                        slot = load_main(wl[:, c0:c0 + 256], 16)
                        for jj in range(2):
                            g = mm_group(proj_steps(slot, jj * 128, xn, r_xn), R_out)
                            sg, r_sg = TF(hpair * 2 + jj)
                            ACT(v3(sg, R_out), pv(g, R_out), AF.Sigmoid, g.res, r_sg)
                            sgs.append((sg, r_sg))
                    slot = load_main512(w_bo[l, b][:, q * 512:(q + 1) * 512])
                    wv = ring[:, slot, :, :].rearrange("p (k a) c -> p k (a c)", a=2)
                    for jj in range(4):
                        d = q * 4 + jj
                        steps = [(wv[:, k, jj * 128:(jj + 1) * 128], yb[:, k, :], 0, [r_ring[slot], r_yb[k]]) for k in range(8)]
                        g = mm_group(steps, R_out)
                        sg, r_sg = sgs[jj]
                        if b == 0:
                            TT(v3(acc[:, d, :], R_out), pv(g, R_out), v3(sg, R_out), ALU.mult, g.res + r_sg, [r_acc[d]])
                        else:
                            t, r_t = TF(4 + (jj % 2))
                            TT(v3(t, R_out), pv(g, R_out), v3(sg, R_out), ALU.mult, g.res + r_sg, r_t)
                            TT(acc[:, d, LO:TL], acc[:, d, LO:TL], t[:, LO:TL], ALU.add, [r_acc[d]] + r_t, [r_acc[d]])

            for j in range(4):
                u = [2 * j, 2 * j + 1]
                tA = [TF(0), TF(1)]
                cv = [TF(2), TF(3)]
                slot = load_main(wl[:, 0 * E + u[0] * 128: 0 * E + u[0] * 128 + 256], 16)
                for i in range(2):
                    g = mm_group(proj_steps(slot, i * 128, xn, r_xn), R_in)
                    ACT(v3(tA[i][0], R_in), pv(g, R_in), AF.Copy, g.res, tA[i][1])
                slot = load_main(wl[:, 2 * E + u[0] * 128: 2 * E + u[0] * 128 + 256], 16)
                for i in range(2):
                    g = mm_group(proj_steps(slot, i * 128, xn, r_xn), R_in)
                    cx, r_cx = tA[i]
                    co, r_co = cv[i]
                    TT(v3(cx, R_in), pv(g, R_in), v3(cx, R_in), ALU.mult, g.res + r_cx, r_cx)
                    wc = PB + u[i] * 3
                    TS(co[:, LO:TL], cx[:, LO:TL], pcol(wc + 2), None, ALU.mult, None, r_cx + [r_par], r_co)
                    for sh in (1, 2):
                        STT(co[:, LO:TL], cx[:, LO - sh:TL - sh], pcol(wc + 2 - sh), co[:, LO:TL], ALU.mult, ALU.add,
                            r_cx + r_co + [r_par], r_co)
                slot = load_main(wl[:, 1 * E + u[0] * 128: 1 * E + u[0] * 128 + 256], 16)
                for i in range(2):
                    g = mm_group(proj_steps(slot, i * 128, xn, r_xn), R_out)
                    co, r_co = cv[i]
                    TT(v3(co, R_out), pv(g, R_out), v3(co, R_out), ALU.mult, g.res + r_co, r_co)
                slot = load_main(wl[:, 3 * E + u[0] * 128: 3 * E + u[0] * 128 + 256], 16)
                for i in range(2):
                    g = mm_group(proj_steps(slot, i * 128, xn, r_xn), R_out)
                    sz, r_sz = tA[i]
                    co, r_co = cv[i]
                    ACT(v3(sz, R_out), pv(g, R_out), AF.Silu, g.res, r_sz)
                    TT(yb[:, u[i], LO:TL], co[:, LO:TL], sz[:, LO:TL], ALU.mult, r_co + r_sz, [r_yb[u[i]]])
            merge_branch(0)

            for j in range(4):
                u = [2 * j, 2 * j + 1]
                wlen = 2 ** (j + 1)
                U = [TF(0), TF(1)]
                XY = [TF(2), TF(3)]
                P = [TB(4, 0), TB(4, 1)]
                slot = load_main(wl[:, 4 * E + u[0] * 128: 4 * E + u[0] * 128 + 256], 16)
                for i in range(2):
                    g = mm_group(proj_steps(slot, i * 128, xn, r_xn), R_in)
                    Ui, r_Ui = U[i]
                    ACT(v3(Ui, R_in), pv(g, R_in), AF.Copy, g.res, r_Ui)
                    src, r_src = Ui, r_Ui
                    lo_valid = R_in.lo
                    sh = 1
                    st = 0
                    while sh < wlen:
                        dst, r_dst = XY[st % 2]
                        lo_new = lo_valid + sh
                        TT(dst[:, lo_new:TL], src[:, lo_new:TL], src[:, lo_new - sh:TL - sh], ALU.add, r_src, r_dst)
                        src, r_src = dst, r_dst
                        lo_valid = lo_new
                        sh *= 2
                        st += 1
                    assert lo_valid <= LO
                    Pi, r_Pi = P[i]
                    STT(Pi[:, LO:TL], src[:, LO:TL], 1.0 / wlen, Ui[:, LO:TL], ALU.mult, ALU.subtract, r_src + r_Ui, r_Pi)
                    TT(tmp16[:, :], src[:, HALO:HALO + 16], icnt[:, j * 16:(j + 1) * 16], ALU.mult, r_src + [r_par], [r_tmp16])
                    TT(Pi[:, HALO:HALO + 16], tmp16[:, :], Ui[:, HALO:HALO + 16], ALU.subtract, [r_tmp16] + r_Ui, r_Pi)
                slot = load_main(wl[:, 5 * E + u[0] * 128: 5 * E + u[0] * 128 + 256], 16)
                for i in range(2):
                    g = mm_group(proj_steps(slot, i * 128, xn, r_xn), R_out)
                    ACT(v3(U[i][0], R_out), pv(g, R_out), AF.Silu, g.res, U[i][1])
                ss = load_side(pool_w[l, j])
                for o in range(2):
                    steps = [(side[:, ss, i, o * 128:(o + 1) * 128], P[i][0], 0, [r_side[ss]] + P[i][1]) for i in range(2)]
                    g = mm_group(steps, R_out)
                    STT(v3(yb[:, u[o], :], R_out), pv(g, R_out), pcol(PB + 24 + u[o]), v3(U[o][0], R_out), ALU.mult, ALU.mult,
                        g.res + [r_par] + U[o][1], [r_yb[u[o]]])
            merge_branch(1)

            ND = 10
            for j in range(4):
                u = [2 * j, 2 * j + 1]
                SG = [TF(0), TF(1)]
                V = [TB(2, 0), TB(2, 1)]
                DACC = [TF(3), TF(4)]
                PEP = [TF(0), TF(5)]

                def build_diag(ui):
                    wc = PB + 56 + ui * 31
                    for k in range(ND, 31):
                        TS(dg[:, k, :], ident[:, :], pcol(wc + k), None, ALU.mult, None, [r_ident, r_par], [r_dgk[k]])

                def chain(i, ks):
                    Vi, r_Vi = V[i]
                    da, r_da = DACC[i]
                    wc = PB + 56 + u[i] * 31
                    for k in ks:
                        srcv = Vi[:, LO - 30 + k:TL - 30 + k]
                        if k == 0:
                            TS(da[:, LO:TL], srcv, pcol(wc + k), None, ALU.mult, None, r_Vi + [r_par], r_da)
                        else:
                            STT(da[:, LO:TL], srcv, pcol(wc + k), da[:, LO:TL], ALU.mult, ALU.add, r_Vi + r_da + [r_par], r_da)

                def conv_pe(i):
                    Vi, r_Vi = V[i]
                    steps = [(dg[:, k, :], Vi, k - 30, [r_dgk[k]] + r_Vi) for k in range(ND, 31)]
                    g = mm_group(steps, R_out)
                    pp, r_pp = PEP[i]
                    ACT(v3(pp, R_out), pv(g, R_out), AF.Identity, g.res + [r_par], r_pp, bias=pcol(PB + 32 + u[i]))

                build_diag(u[0])
                slot = load_main(wl[:, 7 * E + u[0] * 128: 7 * E + u[0] * 128 + 256], 16)
                for i in range(2):
                    g = mm_group(proj_steps(slot, i * 128, xn, r_xn), R_in)
                    ACT(v3(SG[i][0], R_in), pv(g, R_in), AF.Sigmoid, g.res, SG[i][1])
                slot = load_main(wl[:, 6 * E + u[0] * 128: 6 * E + u[0] * 128 + 256], 16)
                for i in range(2):
                    g = mm_group(proj_steps(slot, i * 128, xn, r_xn), R_in)
                    TT(v3(V[i][0], R_in), pv(g, R_in), v3(SG[i][0], R_in), ALU.mult, g.res + SG[i][1], V[i][1])
                    if i == 0:
                        chain(0, range(0, ND // 2))
                chain(0, range(ND // 2, ND))
                conv_pe(0)
                build_diag(u[1])
                conv_pe(1)
                chain(1, range(0, ND))
                for i in range(2):
                    TT(yb[:, u[i], LO:TL], PEP[i][0][:, LO:TL], DACC[i][0][:, LO:TL], ALU.add, PEP[i][1] + DACC[i][1], [r_yb[u[i]]])
            steps = [(ones[:, :], yb[:, k, :], 0, [r_ones, r_yb[k]]) for k in range(8)]
            g1 = mm_group(steps, R_out)
            mu, r_mu = TF(3)
            TS(v3(mu, R_out), pv(g1, R_out), 1.0 / E, None, ALU.mult, None, g1.res, r_mu)
            g2 = next_pg(R_out.nt)
            for k in range(8):
                if k % 2 == 0:
                    sq, r_sq = TB(5, (k // 2) % 2)
                    ACT(sq[:, LO:TL], yb[:, k, LO:TL], AF.Square, [r_yb[k]], r_sq)
                else:
                    sq, r_sq = TB(4, (k // 2) % 2)
                    TT(sq[:, LO:TL], yb[:, k, LO:TL], yb[:, k, LO:TL], ALU.mult, [r_yb[k]], r_sq)
                for tt in range(R_out.nt):
                    lo = LO + tt * R_out.n
                    MM(psa[:, g2.base + tt, 0:R_out.n], ones[:, :], sq[:, lo:lo + R_out.n], (k == 0), (k == 7), [r_ones] + r_sq, g2.res)
            rstd, r_rstd = TF(4)
            TS(v3(rstd, R_out), pv(g2, R_out), 1.0 / E, None, ALU.mult, None, g2.res, r_rstd)
            msq, r_msq = TF(5)
            TT(msq[:, LO:TL], mu[:, LO:TL], mu[:, LO:TL], ALU.mult, r_mu, r_msq)
            TT(rstd[:, LO:TL], rstd[:, LO:TL], msq[:, LO:TL], ALU.subtract, r_rstd + r_msq, r_rstd)
            TS(rstd[:, LO:TL], rstd[:, LO:TL], 0.0, LN_EPS, ALU.max, ALU.add, r_rstd, r_rstd)
            ACT(rstd[:, LO:TL], rstd[:, LO:TL], AF.Ln, r_rstd, r_rstd)
            ACT(rstd[:, LO:TL], rstd[:, LO:TL], AF.Exp, r_rstd, r_rstd, scale=-0.5)
            for j in range(4):
                u = [2 * j, 2 * j + 1]
                slot = load_main(wl[:, 8 * E + u[0] * 128: 8 * E + u[0] * 128 + 256], 16)
                for i in range(2):
                    g = mm_group(proj_steps(slot, i * 128, xn, r_xn), R_out)
                    Z, r_Z = TF(i)
                    ACT(v3(Z, R_out), pv(g, R_out), AF.Silu, g.res, r_Z)
                    a, r_a = TF(2)
                    s, r_s = TF(5)
                    ui = u[i]
                    TT(a[:, LO:TL], yb[:, ui, LO:TL], mu[:, LO:TL], ALU.subtract, [r_yb[ui]] + r_mu, r_a)
                    TT(a[:, LO:TL], a[:, LO:TL], rstd[:, LO:TL], ALU.mult, r_a + r_rstd, r_a)
                    ACT(s[:, LO:TL], a[:, LO:TL], AF.Silu, r_a + [r_par], r_s, bias=pcol(PB + 48 + ui), scale=pcol(PB + 40 + ui))
                    TT(yb[:, ui, LO:TL], s[:, LO:TL], Z[:, LO:TL], ALU.mult, r_s + r_Z, [r_yb[ui]])
            merge_branch(2)

            for q in range(8):
                slot = load_main(w_o[l][:, q * 256:(q + 1) * 256], 16)
                for i in range(2):
                    oc = q * 2 + i
                    g = mm_group(proj_steps(slot, i * 128, acc, r_acc), R_out)
                    TT(v3(h[:, oc, :], R_out), pv(g, R_out), v3(h[:, oc, :], R_out), ALU.add, g.res + [r_h[oc]], [r_h[oc]])
                    ACT(xn[:, oc, LO:TL], h[:, oc, LO:TL], AF.Copy, [r_h[oc]], [r_xn[oc]])

            DMA("pool", pTs, pT[l].rearrange("(k p) t -> p k t", p=128), [], r_dgk, "p")
            last = (l == depth_run - 1)
            if last:
                if R_out.nt == 2:
                    state["pg"] = 6
                gF = next_pg(R_out.nt)
                state["reserved"] = set(range(gF.base, gF.base + R_out.nt))
            for q in range(8):
                pend = []
                if last and q >= 1:
                    for kk in (2 * q - 2, 2 * q - 1):
                        pend.append((kk,) + rms_sq(kk, R_out))
                slot = load_main(w_pg[l][:, q * 256:(q + 1) * 256], 16)
                ss = load_side(w_pp[l][:, q * 256:(q + 1) * 256])
                sgs = []
                for i in range(2):
                    g = mm_group(proj_steps(slot, i * 128, xn, r_xn), R_out)
                    sg, r_sg = TF(i)
                    ACT(v3(sg, R_out), pv(g, R_out), AF.Sigmoid, g.res, r_sg)
                    sgs.append((sg, r_sg))
                for kk, sq, r_sq in pend:
                    rms_mm(gF, kk, R_out, sq, r_sq)
                for i in range(2):
                    oc = q * 2 + i
                    steps = [(side[:, ss, k, i * 128:(i + 1) * 128], pTs[:, k, :], 0, [r_side[ss]] + r_dgk[0:17]) for k in range(2)]
                    g = mm_group(steps, R_out)
                    sg, r_sg = sgs[i]
                    t, r_t = TF(2 + i)
                    TT(v3(t, R_out), pv(g, R_out), v3(sg, R_out), ALU.mult, g.res + r_sg, r_t)
                    TT(h[:, oc, LO:TL], h[:, oc, LO:TL], t[:, LO:TL], ALU.add, [r_h[oc]] + r_t, [r_h[oc]])

        Rf = R_L[depth_run - 1][1]
        for kk in (14, 15):
            rms_step(gF, kk, Rf)
        state["reserved"] = set()
        rs, r_rs = rms_finish(gF, Rf)
        for k in range(16):
            ob, r_ob = TF(1 + (k % 4))
            STT(ob[:, HALO:TL], h[:, k, HALO:TL], pcol(32 + k), rs[:, HALO:TL], ALU.mult, ALU.mult, [r_h[k], r_par] + r_rs, r_ob)
            DMA("sp", outT[k * 128:(k + 1) * 128, :], ob[:, HALO:TL], r_ob, [r_out[k]], f"out{k % 4}")
        S.op("sp", None, None, r_out, [])

        S.emit({"pe": s_pe, "act": s_act, "dve": s_dve, "pool": s_pool, "sp": s_sp}, dsems)
    return nc


def _pack_params(norm_g, conv_a_w, pool_scale, conv_c_w, conv_c_b, ln_c_g, ln_c_b, final_norm_g):
    par = np.zeros((128, NPAR), dtype=np.float32)
    for l in range(DEPTH):
        par[:, l * 16:(l + 1) * 16] = norm_g[l].reshape(16, 128).T
        B = _par_base(l)
        par[:, B:B + 24] = conv_a_w[l].reshape(3, 8, 128).transpose(2, 1, 0).reshape(128, 24)
        par[:, B + 24:B + 32] = pool_scale[l].reshape(8, 128).T
        par[:, B + 32:B + 40] = conv_c_b[l].reshape(8, 128).T
        par[:, B + 40:B + 48] = ln_c_g[l].reshape(8, 128).T
        par[:, B + 48:B + 56] = ln_c_b[l].reshape(8, 128).T
        par[:, B + 56:B + 304] = conv_c_w[l].reshape(31, 8, 128).transpose(2, 1, 0).reshape(128, 248)
    par[:, 32:48] = final_norm_g.reshape(16, 128).T
    par[:, C_RMS_EPS] = RMS_EPS
    par[:, C_RMS_EPS + 1] = LN_EPS
    return par


def _icnt_table(half):
    t = np.zeros((128, 64), dtype=np.float32)
    for g, w in enumerate((2, 4, 8, 16)):
        for i in range(16):
            pos = half * TOK + i
            t[:, g * 16 + i] = np.float32(1.0) / np.float32(min(pos + 1, w))
    return t


_NC_CACHE = {}


def kernel(x, p, norm_g, w_in, conv_a_w, pool_w, pool_scale, conv_c_w, conv_c_b,
           ln_c_g, ln_c_b, w_branch_out, w_o, w_ple_gate, w_ple_proj, final_norm_g):
    f = lambda a: np.ascontiguousarray(np.asarray(a, dtype=np.float32))
    x, p = f(x), f(p)
    w_in, pool_w, w_branch_out, w_o, w_ple_gate, w_ple_proj = map(f, (w_in, pool_w, w_branch_out, w_o, w_ple_gate, w_ple_proj))
    par = _pack_params(*map(f, (norm_g, conv_a_w, pool_scale, conv_c_w, conv_c_b, ln_c_g, ln_c_b, final_norm_g)))
    ident = np.eye(128, dtype=np.float32)
    if "nc" not in _NC_CACHE:
        _NC_CACHE["nc"] = build_program()
    nc = _NC_CACHE["nc"]
    in_maps = []
    for c in range(NCORES):
        b, half = c // 2, c % 2
        xT = np.zeros((D, TL), dtype=np.float32)
        pT = np.zeros((DEPTH, PLE, TL), dtype=np.float32)
        g0 = half * TOK - HALO
        s0 = max(g0, 0)
        xT[:, s0 - g0:] = x[b, s0:half * TOK + TOK, :].T
        for l in range(DEPTH):
            pT[l][:, s0 - g0:] = p[l, b, s0:half * TOK + TOK, :].T
        in_maps.append({
            "xT": xT, "pT": pT, "par": par, "icnt": _icnt_table(half), "ident": ident,
            "w_in": w_in, "pool_w": pool_w, "w_branch_out": w_branch_out, "w_o": w_o,
            "w_ple_gate": w_ple_gate, "w_ple_proj": w_ple_proj,
        })
    res = run_bass_kernel_spmd(nc, in_maps, core_ids=list(range(NCORES)))
    out = np.empty((BATCH, SEQ, D), dtype=np.float32)
    for c in range(NCORES):
        b, half = c // 2, c % 2
        out[b, half * TOK:(half + 1) * TOK, :] = res.results[c]["outT"].T
    return out
```

```python
from contextlib import ExitStack

import numpy as np
import concourse.bass as bass
import concourse.mybir as mybir
from concourse.bass_utils import run_bass_kernel_spmd

F32 = mybir.dt.float32
BF16 = mybir.dt.bfloat16
AF = mybir.ActivationFunctionType
ALU = mybir.AluOpType

D = 2048
E = 1024
SEQ = 2048
BATCH = 4
DEPTH = 2
PLE = 256
IN_COLS = 15360
HALO = 62
TOK = 1024
TL = TOK + HALO
NCORES = 8
RMS_EPS = 1e-6
LN_EPS = 1e-5
NPAR = 48 + 2 * 304 + 2
C_RMS_EPS = 48 + 2 * 304
ENGS = ("pe", "act", "dve", "pool", "sp")


class Res:
    __slots__ = ("name", "writer", "readers")

    def __init__(self, name):
        self.name = name
        self.writer = None
        self.readers = {}


class Op:
    __slots__ = ("eng", "meth", "kw", "deps", "is_dma", "dsem", "dcount", "signal", "sigval")

    def __init__(self, eng, meth, kw):
        self.eng = eng
        self.meth = meth
        self.kw = kw
        self.deps = []
        self.is_dma = False
        self.dsem = None
        self.dcount = 0
        self.signal = False
        self.sigval = 0


class Sched:
    def __init__(self, nc):
        self.nc = nc
        self.ops = {e: [] for e in ENGS}
        self.dma_counts = {}

    def _add_dep(self, op, dep):
        if dep is None or dep is op:
            return
        if (not dep.is_dma) and (not op.is_dma) and dep.eng == "pe" and op.eng == "pe":
            return
        op.deps.append(dep)
        if not dep.is_dma:
            dep.signal = True

    def op(self, eng, meth, kw=None, reads=(), writes=(), dma_sem=None):
        o = Op(eng, meth, kw)
        if dma_sem is not None:
            o.is_dma = True
            o.dsem = dma_sem
            self.dma_counts[dma_sem] = self.dma_counts.get(dma_sem, 0) + 16
            o.dcount = self.dma_counts[dma_sem]
        for r in reads:
            self._add_dep(o, r.writer)
        for w in writes:
            self._add_dep(o, w.writer)
            for rd in w.readers.values():
                self._add_dep(o, rd)
        for w in writes:
            w.writer = o
            w.readers = {}
        for r in reads:
            r.readers[id(o) if o.is_dma else o.eng] = o
        self.ops[eng].append(o)
        return o

    def emit(self, sems, dma_sems):
        nc = self.nc
        for e in ENGS:
            c = 0
            for o in self.ops[e]:
                if (not o.is_dma) and o.signal:
                    c += 1
                    o.sigval = c
        engobj = {"pe": "tensor", "act": "scalar", "dve": "vector", "pool": "gpsimd", "sp": "sync"}
        with nc.Block() as block:
            for e in ENGS:
                ops = self.ops[e]
                if not ops:
                    continue

                def body(eng, ops=ops, e=e):
                    waited = {}
                    for o in ops:
                        need = {}
                        for d in o.deps:
                            if d.is_dma:
                                key, val = ("d", d.dsem), d.dcount
                            else:
                                key, val = ("e", d.eng), d.sigval
                            if val > need.get(key, 0):
                                need[key] = val
                        todo = []
                        for key, val in need.items():
                            if waited.get(key, 0) >= val:
                                continue
                            waited[key] = val
                            todo.append((key, val))
                        emb = None
                        if o.meth is not None and not o.is_dma:
                            for t in todo:
                                if t[0][0] == "e":
                                    emb = t
                            if emb is not None:
                                todo.remove(emb)
                        for key, val in todo:
                            sem = dma_sems[key[1]] if key[0] == "d" else sems[key[1]]
                            eng.wait_ge(sem, val)
                        if o.meth is None:
                            continue
                        ins = getattr(eng, o.meth)(**o.kw)
                        if emb is not None:
                            ins._wait_ge(sems[emb[0][1]], emb[1])
                        if o.is_dma:
                            ins.then_inc(dma_sems[o.dsem], 16)
                        elif o.signal:
                            ins.then_inc(sems[e], 1)

                getattr(block, engobj[e])(body)


class Rng:
    def __init__(self, lo, n, nt):
        assert lo + n * nt == TL and n <= 512
        self.lo, self.n, self.nt = lo, n, nt


R_L = [
    (Rng(0, 362, 3), Rng(30, 352, 3)),
    (Rng(30, 352, 3), Rng(62, 512, 2)),
]
R_FINAL = Rng(62, 512, 2)


def _par_base(l):
    return 48 + l * 304


def build_program(depth_run=DEPTH):
    nc = bass.Bass("TRN2", target_bir_lowering=False)
    xT = nc.dram_tensor("xT", [D, TL], F32, kind="ExternalInput").ap()
    pT = nc.dram_tensor("pT", [DEPTH, PLE, TL], F32, kind="ExternalInput").ap()
    par_d = nc.dram_tensor("par", [128, NPAR], F32, kind="ExternalInput").ap()
    icnt_d = nc.dram_tensor("icnt", [128, 64], F32, kind="ExternalInput").ap()
    ident_d = nc.dram_tensor("ident", [128, 128], F32, kind="ExternalInput").ap()
    w_in = nc.dram_tensor("w_in", [DEPTH, D, IN_COLS], F32, kind="ExternalInput").ap()
    pool_w = nc.dram_tensor("pool_w", [DEPTH, 4, 256, 256], F32, kind="ExternalInput").ap()
    w_bo = nc.dram_tensor("w_branch_out", [DEPTH, 3, E, D], F32, kind="ExternalInput").ap()
    w_o = nc.dram_tensor("w_o", [DEPTH, D, D], F32, kind="ExternalInput").ap()
    w_pg = nc.dram_tensor("w_ple_gate", [DEPTH, D, D], F32, kind="ExternalInput").ap()
    w_pp = nc.dram_tensor("w_ple_proj", [DEPTH, PLE, D], F32, kind="ExternalInput").ap()
    outT = nc.dram_tensor("outT", [D, TOK], F32, kind="ExternalOutput").ap()

    S = Sched(nc)
    dma_names = ["x0", "x1", "x2", "x3", "par", "ident", "p", "ring0", "ring1", "side0", "side1", "out0", "out1", "out2", "out3"]

    def ACT(out, in_, func, reads, writes, **kw):
        S.op("act", "activation", dict(out=out, in_=in_, func=func, **kw), reads, writes)

    def TT(out, in0, in1, op, reads, writes):
        S.op("dve", "tensor_tensor", dict(out=out, in0=in0, in1=in1, op=op), reads, writes)

    def TS(out, in0, s1, s2, op0, op1, reads, writes):
        kw = dict(out=out, in0=in0, scalar1=s1, scalar2=s2, op0=op0)
        if op1 is not None:
            kw["op1"] = op1
        S.op("dve", "tensor_scalar", kw, reads, writes)

    def STT(out, in0, scalar, in1, op0, op1, reads, writes):
        S.op("dve", "scalar_tensor_tensor", dict(out=out, in0=in0, scalar=scalar, in1=in1, op0=op0, op1=op1), reads, writes)

    def RECIP(out, in_, reads, writes):
        S.op("dve", "reciprocal", dict(out=out, in_=in_), reads, writes)

    def MM(out, lhsT, rhs, start, stop, reads, writes):
        S.op("pe", "matmul", dict(out=out, lhsT=lhsT, rhs=rhs, start=start, stop=stop), reads, writes)

    def DMA(eng, out, in_, reads, writes, sem):
        S.op(eng, "dma_start", dict(out=out, in_=in_), reads, writes, dma_sem=sem)

    with ExitStack() as es:
        ec = es.enter_context
        h = ec(nc.sbuf_tensor("h", [128, 16, TL], F32))
        xn = ec(nc.sbuf_tensor("xn", [128, 16, TL], BF16))
        acc = ec(nc.sbuf_tensor("acc", [128, 16, TL], BF16))
        yb = ec(nc.sbuf_tensor("yb", [128, 8, TL], BF16))
        ring = ec(nc.sbuf_tensor("ring", [128, 2, 16, 256], BF16))
        side = ec(nc.sbuf_tensor("side", [128, 2, 2, 256], BF16))
        dgb = ec(nc.sbuf_tensor("dgb", [128, 31 * 128], BF16))
        par = ec(nc.sbuf_tensor("par_s", [128, NPAR], F32))
        icnt = ec(nc.sbuf_tensor("icnt_s", [128, 64], F32))
        ident = ec(nc.sbuf_tensor("ident_s", [128, 128], BF16))
        ones = ec(nc.sbuf_tensor("ones_s", [128, 128], BF16))
        tmp16 = ec(nc.sbuf_tensor("tmp16", [128, 16], F32))
        Tt = ec(nc.sbuf_tensor("Tt", [128, 6, TL], F32))
        psa = ec(nc.psum_tensor("psa", [128, 8, 512], F32))
        s_pe, s_act, s_dve, s_pool, s_sp = [ec(nc.semaphore(n)) for n in ("s_pe", "s_act", "s_dve", "s_pool", "s_sp")]
        dlist = [ec(nc.semaphore(f"d{i}")) for i in range(len(dma_names))]
        dsems = dict(zip(dma_names, dlist))
        r_bank = [Res(f"bank{i}") for i in range(8)]
        r_h = [Res(f"h{k}") for k in range(16)]
        r_xn = [Res(f"xn{k}") for k in range(16)]
        r_acc = [Res(f"acc{k}") for k in range(16)]
        r_yb = [Res(f"yb{k}") for k in range(8)]
        r_ring = [Res("ring0"), Res("ring1")]
        r_side = [Res("side0"), Res("side1")]
        r_dgk = [Res(f"dg{k}") for k in range(31)]
        r_par = Res("par")
        r_ident = Res("ident")
        r_ones = Res("ones")
        r_tmp16 = Res("tmp16")
        r_Ta = [Res(f"T{i}a") for i in range(6)]
        r_Tb = [Res(f"T{i}b") for i in range(6)]
        r_out = [Res(f"out{k}") for k in range(16)]

        dg = dgb[:, :].rearrange("p (k c) -> p k c", c=128)
        pTs = dgb[:, 0:2 * TL].rearrange("p (k t) -> p k t", t=TL)

        def TF(i):
            return Tt[:, i, :], [r_Ta[i], r_Tb[i]]

        def TB(i, half):
            v = Tt[:, i, :].bitcast(BF16)
            return v[:, half * TL:(half + 1) * TL], [r_Ta[i] if half == 0 else r_Tb[i]]

        def v3(ap2d, R):
            return ap2d[:, R.lo:TL].rearrange("p (j n) -> p j n", n=R.n)

        class PGrp:
            def __init__(self, base, nt):
                self.base, self.nt = base, nt
                self.res = r_bank[base:base + nt]

        def pv(g, R):
            return psa[:, g.base:g.base + R.nt, 0:R.n]

        def pcol(c):
            return par[:, c:c + 1]

        state = {"pg": 0, "ring": 0, "side": 0, "reserved": set()}

        def next_pg(nt):
            b = state["pg"]
            for _ in range(8):
                if b + nt > 8:
                    b = 0
                if not (set(range(b, b + nt)) & state["reserved"]):
                    break
                b += nt
            assert b + nt <= 8 and not (set(range(b, b + nt)) & state["reserved"])
            state["pg"] = b + nt
            return PGrp(b, nt)

        def load_main(src2d, kch):
            s = state["ring"]
            state["ring"] ^= 1
            hold = [r_h[11]] if state.setdefault("n_main", 0) < 2 else []
            state["n_main"] += 1
            DMA("pool", ring[:, s, 0:kch, :], src2d.rearrange("(k p) c -> p k c", p=128), hold, [r_ring[s]], f"ring{s}")
            return s

        def load_main512(src2d):
            s = state["ring"]
            state["ring"] ^= 1
            dst = ring[:, s, :, :].rearrange("p (k a) c -> p k (a c)", a=2)
            DMA("pool", dst, src2d.rearrange("(k p) c -> p k c", p=128), [], [r_ring[s]], f"ring{s}")
            return s

        def load_side(src2d):
            s = state["side"]
            state["side"] ^= 1
            DMA("pool", side[:, s, :, :], src2d.rearrange("(k p) c -> p k c", p=128), [], [r_side[s]], f"side{s}")
            return s

        def mm_group(steps, R):
            g = next_pg(R.nt)
            ns = len(steps)
            for i, (lhsT, rhs2d, shift, reads) in enumerate(steps):
                for tt in range(R.nt):
                    lo = R.lo + tt * R.n + shift
                    MM(psa[:, g.base + tt, 0:R.n], lhsT, rhs2d[:, lo:lo + R.n], (i == 0), (i == ns - 1), reads, g.res)
            return g

        def proj_steps(slot, coff, src, r_src):
            return [(ring[:, slot, k, coff:coff + 128], src[:, k, :], 0, [r_ring[slot], r_src[k]]) for k in range(16)]

        for i4 in range(4):
            DMA("sp", h[:, 4 * i4:4 * i4 + 4, :], xT[512 * i4:512 * i4 + 512, :].rearrange("(k p) t -> p k t", p=128), [],
                r_h[4 * i4:4 * i4 + 4], f"x{i4}")
        DMA("sp", par[:, :], par_d, [], [r_par], "par")
        DMA("sp", icnt[:, :], icnt_d, [], [r_par], "par")
        DMA("pool", ident[:, :], ident_d, [r_h[11]], [r_ident], "ident")
        S.op("dve", "memset", dict(ap=ones[:, :], constant=1.0), [], [r_ones])

        def rms_sq(k, R):
            if k % 2 == 0:
                sq, r_sq = TB(5, (k // 2) % 2)
                ACT(sq[:, R.lo:TL], h[:, k, R.lo:TL], AF.Square, [r_h[k]], r_sq)
            else:
                sq, r_sq = TB(4, (k // 2) % 2)
                TT(sq[:, R.lo:TL], h[:, k, R.lo:TL], h[:, k, R.lo:TL], ALU.mult, [r_h[k]], r_sq)
            return sq, r_sq

        def rms_mm(g, k, R, sq, r_sq):
            for tt in range(R.nt):
                lo = R.lo + tt * R.n
                MM(psa[:, g.base + tt, 0:R.n], ones[:, :], sq[:, lo:lo + R.n], (k == 0), (k == 15), [r_ones] + r_sq, g.res)

        def rms_step(g, k, R):
            sq, r_sq = rms_sq(k, R)
            rms_mm(g, k, R, sq, r_sq)

        def rms_finish(g, R):
            rs, r_rs = TF(0)
            ACT(v3(rs, R), pv(g, R), AF.Ln, g.res + [r_par], r_rs, scale=1.0 / D, bias=pcol(C_RMS_EPS))
            ACT(rs[:, R.lo:TL], rs[:, R.lo:TL], AF.Exp, r_rs, r_rs, scale=-0.5)
            return rs, r_rs

        def rms_stats(R):
            g = next_pg(R.nt)
            for k in range(16):
                rms_step(g, k, R)
            return rms_finish(g, R)

        for l in range(depth_run):
            R_in, R_out = R_L[l]
            PB = _par_base(l)
            wl = w_in[l]
            LO = R_out.lo

            rs, r_rs = rms_stats(R_in)
            for k in range(16):
                STT(xn[:, k, R_in.lo:TL], h[:, k, R_in.lo:TL], pcol(l * 16 + k), rs[:, R_in.lo:TL], ALU.mult, ALU.mult,
                    [r_h[k], r_par] + r_rs, [r_xn[k]])

            def merge_branch(b):
                for q in range(4):
                    sgs = []
                    for hpair in range(2):
                        d0c = q * 4 + hpair * 2
                        c0 = 9216 + b * 2048 + d0c * 128
                        slot = load_main(wl[:, c0:c0 + 256], 16)
                        for jj in range(2):
                            g = mm_group(proj_steps(slot, jj * 128, xn, r_xn), R_out)
                            sg, r_sg = TF(hpair * 2 + jj)
                            ACT(v3(sg, R_out), pv(g, R_out), AF.Sigmoid, g.res, r_sg)
                            sgs.append((sg, r_sg))
                    slot = load_main512(w_bo[l, b][:, q * 512:(q + 1) * 512])
                    wv = ring[:, slot, :, :].rearrange("p (k a) c -> p k (a c)", a=2)
                    for jj in range(4):
                        d = q * 4 + jj
                        steps = [(wv[:, k, jj * 128:(jj + 1) * 128], yb[:, k, :], 0, [r_ring[slot], r_yb[k]]) for k in range(8)]
                        g = mm_group(steps, R_out)
                        sg, r_sg = sgs[jj]
                        if b == 0:
                            TT(v3(acc[:, d, :], R_out), pv(g, R_out), v3(sg, R_out), ALU.mult, g.res + r_sg, [r_acc[d]])
                        else:
                            t, r_t = TF(4 + (jj % 2))
                            TT(v3(t, R_out), pv(g, R_out), v3(sg, R_out), ALU.mult, g.res + r_sg, r_t)
                            TT(acc[:, d, LO:TL], acc[:, d, LO:TL], t[:, LO:TL], ALU.add, [r_acc[d]] + r_t, [r_acc[d]])

            for j in range(4):
                u = [2 * j, 2 * j + 1]
                tA = [TF(0), TF(1)]
                cv = [TF(2), TF(3)]
                slot = load_main(wl[:, 0 * E + u[0] * 128: 0 * E + u[0] * 128 + 256], 16)
                for i in range(2):
                    g = mm_group(proj_steps(slot, i * 128, xn, r_xn), R_in)
                    ACT(v3(tA[i][0], R_in), pv(g, R_in), AF.Copy, g.res, tA[i][1])
                slot = load_main(wl[:, 2 * E + u[0] * 128: 2 * E + u[0] * 128 + 256], 16)
                for i in range(2):
                    g = mm_group(proj_steps(slot, i * 128, xn, r_xn), R_in)
                    cx, r_cx = tA[i]
                    co, r_co = cv[i]
                    TT(v3(cx, R_in), pv(g, R_in), v3(cx, R_in), ALU.mult, g.res + r_cx, r_cx)
                    wc = PB + u[i] * 3
                    TS(co[:, LO:TL], cx[:, LO:TL], pcol(wc + 2), None, ALU.mult, None, r_cx + [r_par], r_co)
                    for sh in (1, 2):
                        STT(co[:, LO:TL], cx[:, LO - sh:TL - sh], pcol(wc + 2 - sh), co[:, LO:TL], ALU.mult, ALU.add,
                            r_cx + r_co + [r_par], r_co)
                slot = load_main(wl[:, 1 * E + u[0] * 128: 1 * E + u[0] * 128 + 256], 16)
                for i in range(2):
                    g = mm_group(proj_steps(slot, i * 128, xn, r_xn), R_out)
                    co, r_co = cv[i]
                    TT(v3(co, R_out), pv(g, R_out), v3(co, R_out), ALU.mult, g.res + r_co, r_co)
                slot = load_main(wl[:, 3 * E + u[0] * 128: 3 * E + u[0] * 128 + 256], 16)
                for i in range(2):
                    g = mm_group(proj_steps(slot, i * 128, xn, r_xn), R_out)
                    sz, r_sz = tA[i]
                    co, r_co = cv[i]
                    ACT(v3(sz, R_out), pv(g, R_out), AF.Silu, g.res, r_sz)
                    TT(yb[:, u[i], LO:TL], co[:, LO:TL], sz[:, LO:TL], ALU.mult, r_co + r_sz, [r_yb[u[i]]])
            merge_branch(0)

            for j in range(4):
                u = [2 * j, 2 * j + 1]
                wlen = 2 ** (j + 1)
                U = [TF(0), TF(1)]
                XY = [TF(2), TF(3)]
                P = [TB(4, 0), TB(4, 1)]
                slot = load_main(wl[:, 4 * E + u[0] * 128: 4 * E + u[0] * 128 + 256], 16)
                for i in range(2):
                    g = mm_group(proj_steps(slot, i * 128, xn, r_xn), R_in)
                    Ui, r_Ui = U[i]
                    ACT(v3(Ui, R_in), pv(g, R_in), AF.Copy, g.res, r_Ui)
                    src, r_src = Ui, r_Ui
                    lo_valid = R_in.lo
                    sh = 1
                    st = 0
                    while sh < wlen:
                        dst, r_dst = XY[st % 2]
                        lo_new = lo_valid + sh
                        TT(dst[:, lo_new:TL], src[:, lo_new:TL], src[:, lo_new - sh:TL - sh], ALU.add, r_src, r_dst)
                        src, r_src = dst, r_dst
                        lo_valid = lo_new
                        sh *= 2
                        st += 1
                    assert lo_valid <= LO
                    Pi, r_Pi = P[i]
                    STT(Pi[:, LO:TL], src[:, LO:TL], 1.0 / wlen, Ui[:, LO:TL], ALU.mult, ALU.subtract, r_src + r_Ui, r_Pi)
                    TT(tmp16[:, :], src[:, HALO:HALO + 16], icnt[:, j * 16:(j + 1) * 16], ALU.mult, r_src + [r_par], [r_tmp16])
                    TT(Pi[:, HALO:HALO + 16], tmp16[:, :], Ui[:, HALO:HALO + 16], ALU.subtract, [r_tmp16] + r_Ui, r_Pi)
                slot = load_main(wl[:, 5 * E + u[0] * 128: 5 * E + u[0] * 128 + 256], 16)
                for i in range(2):
                    g = mm_group(proj_steps(slot, i * 128, xn, r_xn), R_out)
                    ACT(v3(U[i][0], R_out), pv(g, R_out), AF.Silu, g.res, U[i][1])
                ss = load_side(pool_w[l, j])
                for o in range(2):
                    steps = [(side[:, ss, i, o * 128:(o + 1) * 128], P[i][0], 0, [r_side[ss]] + P[i][1]) for i in range(2)]
                    g = mm_group(steps, R_out)
                    STT(v3(yb[:, u[o], :], R_out), pv(g, R_out), pcol(PB + 24 + u[o]), v3(U[o][0], R_out), ALU.mult, ALU.mult,
                        g.res + [r_par] + U[o][1], [r_yb[u[o]]])
            merge_branch(1)

            ND = 11
            for j in range(4):
                u = [2 * j, 2 * j + 1]
                SG = [TF(0), TF(1)]
                V = [TB(2, 0), TB(2, 1)]
                DACC = [TF(3), TF(4)]
                PEP = [TF(0), TF(5)]

                def build_diag(ui):
                    wc = PB + 56 + ui * 31
                    for k in range(ND, 31):
                        TS(dg[:, k, :], ident[:, :], pcol(wc + k), None, ALU.mult, None, [r_ident, r_par], [r_dgk[k]])

                def chain(i, ks):
                    Vi, r_Vi = V[i]
                    da, r_da = DACC[i]
                    wc = PB + 56 + u[i] * 31
                    for k in ks:
                        srcv = Vi[:, LO - 30 + k:TL - 30 + k]
                        if k == 0:
                            TS(da[:, LO:TL], srcv, pcol(wc + k), None, ALU.mult, None, r_Vi + [r_par], r_da)
                        else:
                            STT(da[:, LO:TL], srcv, pcol(wc + k), da[:, LO:TL], ALU.mult, ALU.add, r_Vi + r_da + [r_par], r_da)

                def conv_pe(i):
                    Vi, r_Vi = V[i]
                    steps = [(dg[:, k, :], Vi, k - 30, [r_dgk[k]] + r_Vi) for k in range(ND, 31)]
                    g = mm_group(steps, R_out)
                    pp, r_pp = PEP[i]
                    ACT(v3(pp, R_out), pv(g, R_out), AF.Identity, g.res + [r_par], r_pp, bias=pcol(PB + 32 + u[i]))

                build_diag(u[0])
                slot = load_main(wl[:, 7 * E + u[0] * 128: 7 * E + u[0] * 128 + 256], 16)
                for i in range(2):
                    g = mm_group(proj_steps(slot, i * 128, xn, r_xn), R_in)
                    ACT(v3(SG[i][0], R_in), pv(g, R_in), AF.Sigmoid, g.res, SG[i][1])
                slot = load_main(wl[:, 6 * E + u[0] * 128: 6 * E + u[0] * 128 + 256], 16)
                for i in range(2):
                    g = mm_group(proj_steps(slot, i * 128, xn, r_xn), R_in)
                    TT(v3(V[i][0], R_in), pv(g, R_in), v3(SG[i][0], R_in), ALU.mult, g.res + SG[i][1], V[i][1])
                    if i == 0:
                        chain(0, range(0, ND // 2))
                chain(0, range(ND // 2, ND))
                conv_pe(0)
                build_diag(u[1])
                conv_pe(1)
                chain(1, range(0, ND))
                for i in range(2):
                    TT(yb[:, u[i], LO:TL], PEP[i][0][:, LO:TL], DACC[i][0][:, LO:TL], ALU.add, PEP[i][1] + DACC[i][1], [r_yb[u[i]]])
            def cz_proj(jj, i):
                g = mm_group(proj_steps(cz_slot[jj], i * 128, xn, r_xn), R_out)
                Z, r_Z = TF(i)
                ACT(v3(Z, R_out), pv(g, R_out), AF.Silu, g.res, r_Z)
                return 2 * jj + i, Z, r_Z

            def cz_norm(ui, Z, r_Z):
                a, r_a = TF(2)
                s, r_s = TF(5)
                TT(a[:, LO:TL], yb[:, ui, LO:TL], mu[:, LO:TL], ALU.subtract, [r_yb[ui]] + r_mu, r_a)
                TT(a[:, LO:TL], a[:, LO:TL], rstd[:, LO:TL], ALU.mult, r_a + r_rstd, r_a)
                ACT(s[:, LO:TL], a[:, LO:TL], AF.Silu, r_a + [r_par], r_s, bias=pcol(PB + 48 + ui), scale=pcol(PB + 40 + ui))
                TT(yb[:, ui, LO:TL], s[:, LO:TL], Z[:, LO:TL], ALU.mult, r_s + r_Z, [r_yb[ui]])

            cz_slot = {0: load_main(wl[:, 8 * E: 8 * E + 256], 16)}
            pre = [cz_proj(0, 0), cz_proj(0, 1)]
            steps = [(ones[:, :], yb[:, k, :], 0, [r_ones, r_yb[k]]) for k in range(8)]
            g1 = mm_group(steps, R_out)
            mu, r_mu = TF(3)
            TS(v3(mu, R_out), pv(g1, R_out), 1.0 / E, None, ALU.mult, None, g1.res, r_mu)
            g2 = next_pg(R_out.nt)
            for k in range(8):
                if k % 2 == 0:
                    sq, r_sq = TB(5, (k // 2) % 2)
                    ACT(sq[:, LO:TL], yb[:, k, LO:TL], AF.Square, [r_yb[k]], r_sq)
                else:
                    sq, r_sq = TB(4, (k // 2) % 2)
                    TT(sq[:, LO:TL], yb[:, k, LO:TL], yb[:, k, LO:TL], ALU.mult, [r_yb[k]], r_sq)
                for tt in range(R_out.nt):
                    lo = LO + tt * R_out.n
                    MM(psa[:, g2.base + tt, 0:R_out.n], ones[:, :], sq[:, lo:lo + R_out.n], (k == 0), (k == 7), [r_ones] + r_sq, g2.res)
            rstd, r_rstd = TF(4)
            TS(v3(rstd, R_out), pv(g2, R_out), 1.0 / E, None, ALU.mult, None, g2.res, r_rstd)
            msq, r_msq = TF(5)
            TT(msq[:, LO:TL], mu[:, LO:TL], mu[:, LO:TL], ALU.mult, r_mu, r_msq)
            TT(rstd[:, LO:TL], rstd[:, LO:TL], msq[:, LO:TL], ALU.subtract, r_rstd + r_msq, r_rstd)
            TS(rstd[:, LO:TL], rstd[:, LO:TL], 0.0, LN_EPS, ALU.max, ALU.add, r_rstd, r_rstd)
            ACT(rstd[:, LO:TL], rstd[:, LO:TL], AF.Ln, r_rstd, r_rstd)
            ACT(rstd[:, LO:TL], rstd[:, LO:TL], AF.Exp, r_rstd, r_rstd, scale=-0.5)
            for o in pre:
                cz_norm(*o)
            for j in range(1, 4):
                cz_slot[j] = load_main(wl[:, 8 * E + 2 * j * 128: 8 * E + 2 * j * 128 + 256], 16)
                for i in range(2):
                    cz_norm(*cz_proj(j, i))
            merge_branch(2)

            for q in range(8):
                slot = load_main(w_o[l][:, q * 256:(q + 1) * 256], 16)
                for i in range(2):
                    oc = q * 2 + i
                    g = mm_group(proj_steps(slot, i * 128, acc, r_acc), R_out)
                    TT(v3(h[:, oc, :], R_out), pv(g, R_out), v3(h[:, oc, :], R_out), ALU.add, g.res + [r_h[oc]], [r_h[oc]])
                    ACT(xn[:, oc, LO:TL], h[:, oc, LO:TL], AF.Copy, [r_h[oc]], [r_xn[oc]])

            DMA("pool", pTs, pT[l].rearrange("(k p) t -> p k t", p=128), [], r_dgk, "p")
            last = (l == depth_run - 1)
            if last:
                if R_out.nt == 2:
                    state["pg"] = 6
                gF = next_pg(R_out.nt)
                state["reserved"] = set(range(gF.base, gF.base + R_out.nt))
            for q in range(8):
                pend = []
                if last and q >= 1:
                    for kk in (2 * q - 2, 2 * q - 1):
                        pend.append((kk,) + rms_sq(kk, R_out))
                slot = load_main(w_pg[l][:, q * 256:(q + 1) * 256], 16)
                ss = load_side(w_pp[l][:, q * 256:(q + 1) * 256])
                sgs = []
                for i in range(2):
                    g = mm_group(proj_steps(slot, i * 128, xn, r_xn), R_out)
                    sg, r_sg = TF(i)
                    ACT(v3(sg, R_out), pv(g, R_out), AF.Sigmoid, g.res, r_sg)
                    sgs.append((sg, r_sg))
                for kk, sq, r_sq in pend:
                    rms_mm(gF, kk, R_out, sq, r_sq)
                for i in range(2):
                    oc = q * 2 + i
                    steps = [(side[:, ss, k, i * 128:(i + 1) * 128], pTs[:, k, :], 0, [r_side[ss]] + r_dgk[0:17]) for k in range(2)]
                    g = mm_group(steps, R_out)
                    sg, r_sg = sgs[i]
                    t, r_t = TF(2 + i)
                    TT(v3(t, R_out), pv(g, R_out), v3(sg, R_out), ALU.mult, g.res + r_sg, r_t)
                    TT(h[:, oc, LO:TL], h[:, oc, LO:TL], t[:, LO:TL], ALU.add, [r_h[oc]] + r_t, [r_h[oc]])

        Rf = R_L[depth_run - 1][1]
        for kk in (14, 15):
            rms_step(gF, kk, Rf)
        state["reserved"] = set()
        rs, r_rs = rms_finish(gF, Rf)
        for k in range(16):
            ob, r_ob = TF(1 + (k % 4))
            STT(ob[:, HALO:TL], h[:, k, HALO:TL], pcol(32 + k), rs[:, HALO:TL], ALU.mult, ALU.mult, [r_h[k], r_par] + r_rs, r_ob)
            DMA("sp", outT[k * 128:(k + 1) * 128, :], ob[:, HALO:TL], r_ob, [r_out[k]], f"out{k % 4}")
        S.op("sp", None, None, r_out, [])

        S.emit({"pe": s_pe, "act": s_act, "dve": s_dve, "pool": s_pool, "sp": s_sp}, dsems)
    return nc


def _pack_params(norm_g, conv_a_w, pool_scale, conv_c_w, conv_c_b, ln_c_g, ln_c_b, final_norm_g):
    par = np.zeros((128, NPAR), dtype=np.float32)
    for l in range(DEPTH):
        par[:, l * 16:(l + 1) * 16] = norm_g[l].reshape(16, 128).T
        B = _par_base(l)
        par[:, B:B + 24] = conv_a_w[l].reshape(3, 8, 128).transpose(2, 1, 0).reshape(128, 24)
        par[:, B + 24:B + 32] = pool_scale[l].reshape(8, 128).T
        par[:, B + 32:B + 40] = conv_c_b[l].reshape(8, 128).T
        par[:, B + 40:B + 48] = ln_c_g[l].reshape(8, 128).T
        par[:, B + 48:B + 56] = ln_c_b[l].reshape(8, 128).T
        par[:, B + 56:B + 304] = conv_c_w[l].reshape(31, 8, 128).transpose(2, 1, 0).reshape(128, 248)
    par[:, 32:48] = final_norm_g.reshape(16, 128).T
    par[:, C_RMS_EPS] = RMS_EPS
    par[:, C_RMS_EPS + 1] = LN_EPS
    return par


def _icnt_table(half):
    t = np.zeros((128, 64), dtype=np.float32)
    for g, w in enumerate((2, 4, 8, 16)):
        for i in range(16):
            pos = half * TOK + i
            t[:, g * 16 + i] = np.float32(1.0) / np.float32(min(pos + 1, w))
    return t


_NC_CACHE = {}


def kernel(x, p, norm_g, w_in, conv_a_w, pool_w, pool_scale, conv_c_w, conv_c_b,
           ln_c_g, ln_c_b, w_branch_out, w_o, w_ple_gate, w_ple_proj, final_norm_g):
    f = lambda a: np.ascontiguousarray(np.asarray(a, dtype=np.float32))
    x, p = f(x), f(p)
    w_in, pool_w, w_branch_out, w_o, w_ple_gate, w_ple_proj = map(f, (w_in, pool_w, w_branch_out, w_o, w_ple_gate, w_ple_proj))
    par = _pack_params(*map(f, (norm_g, conv_a_w, pool_scale, conv_c_w, conv_c_b, ln_c_g, ln_c_b, final_norm_g)))
    ident = np.eye(128, dtype=np.float32)
    if "nc" not in _NC_CACHE:
        _NC_CACHE["nc"] = build_program()
    nc = _NC_CACHE["nc"]
    in_maps = []
    for c in range(NCORES):
        b, half = c // 2, c % 2
        xT = np.zeros((D, TL), dtype=np.float32)
        pT = np.zeros((DEPTH, PLE, TL), dtype=np.float32)
        g0 = half * TOK - HALO
        s0 = max(g0, 0)
        xT[:, s0 - g0:] = x[b, s0:half * TOK + TOK, :].T
        for l in range(DEPTH):
            pT[l][:, s0 - g0:] = p[l, b, s0:half * TOK + TOK, :].T
        in_maps.append({
            "xT": xT, "pT": pT, "par": par, "icnt": _icnt_table(half), "ident": ident,
            "w_in": w_in, "pool_w": pool_w, "w_branch_out": w_branch_out, "w_o": w_o,
            "w_ple_gate": w_ple_gate, "w_ple_proj": w_ple_proj,
        })
    res = run_bass_kernel_spmd(nc, in_maps, core_ids=list(range(NCORES)))
    out = np.empty((BATCH, SEQ, D), dtype=np.float32)
    for c in range(NCORES):
        b, half = c // 2, c % 2
        out[b, half * TOK:(half + 1) * TOK, :] = res.results[c]["outT"].T
    return out
```

```python
from contextlib import ExitStack

import numpy as np
import concourse.bass as bass
import concourse.mybir as mybir
from concourse.bass_utils import run_bass_kernel_spmd

F32 = mybir.dt.float32
BF16 = mybir.dt.bfloat16
AF = mybir.ActivationFunctionType
ALU = mybir.AluOpType

D = 2048
E = 1024
SEQ = 2048
BATCH = 4
DEPTH = 2
PLE = 256
IN_COLS = 15360
HALO = 62
TOK = 1024
TL = TOK + HALO
NCORES = 8
RMS_EPS = 1e-6
LN_EPS = 1e-5
NPAR = 48 + 2 * 304 + 2
C_RMS_EPS = 48 + 2 * 304
ENGS = ("pe", "act", "dve", "pool", "sp")


class Res:
    __slots__ = ("name", "writer", "readers")

    def __init__(self, name):
        self.name = name
        self.writer = None
        self.readers = {}


class Op:
    __slots__ = ("eng", "meth", "kw", "deps", "is_dma", "dsem", "dcount", "signal", "sigval")

    def __init__(self, eng, meth, kw):
        self.eng = eng
        self.meth = meth
        self.kw = kw
        self.deps = []
        self.is_dma = False
        self.dsem = None
        self.dcount = 0
        self.signal = False
        self.sigval = 0


class Sched:
    def __init__(self, nc):
        self.nc = nc
        self.ops = {e: [] for e in ENGS}
        self.dma_counts = {}

    def _add_dep(self, op, dep):
        if dep is None or dep is op:
            return
        if (not dep.is_dma) and (not op.is_dma) and dep.eng == "pe" and op.eng == "pe":
            return
        op.deps.append(dep)
        if not dep.is_dma:
            dep.signal = True

    def op(self, eng, meth, kw=None, reads=(), writes=(), dma_sem=None):
        o = Op(eng, meth, kw)
        if dma_sem is not None:
            o.is_dma = True
            o.dsem = dma_sem
            self.dma_counts[dma_sem] = self.dma_counts.get(dma_sem, 0) + 16
            o.dcount = self.dma_counts[dma_sem]
        for r in reads:
            self._add_dep(o, r.writer)
        for w in writes:
            self._add_dep(o, w.writer)
            for rd in w.readers.values():
                self._add_dep(o, rd)
        for w in writes:
            w.writer = o
            w.readers = {}
        for r in reads:
            r.readers[id(o) if o.is_dma else o.eng] = o
        self.ops[eng].append(o)
        return o

    def emit(self, sems, dma_sems):
        nc = self.nc
        for e in ENGS:
            c = 0
            for o in self.ops[e]:
                if (not o.is_dma) and o.signal:
                    c += 1
                    o.sigval = c
        engobj = {"pe": "tensor", "act": "scalar", "dve": "vector", "pool": "gpsimd", "sp": "sync"}
        with nc.Block() as block:
            for e in ENGS:
                ops = self.ops[e]
                if not ops:
                    continue

                def body(eng, ops=ops, e=e):
                    waited = {}
                    for o in ops:
                        need = {}
                        for d in o.deps:
                            if d.is_dma:
                                key, val = ("d", d.dsem), d.dcount
                            else:
                                key, val = ("e", d.eng), d.sigval
                            if val > need.get(key, 0):
                                need[key] = val
                        todo = []
                        for key, val in need.items():
                            if waited.get(key, 0) >= val:
                                continue
                            waited[key] = val
                            todo.append((key, val))
                        emb = None
                        if o.meth is not None and not o.is_dma:
                            for t in todo:
                                if t[0][0] == "e":
                                    emb = t
                            if emb is not None:
                                todo.remove(emb)
                        for key, val in todo:
                            sem = dma_sems[key[1]] if key[0] == "d" else sems[key[1]]
                            eng.wait_ge(sem, val)
                        if o.meth is None:
                            continue
                        ins = getattr(eng, o.meth)(**o.kw)
                        if emb is not None:
                            ins._wait_ge(sems[emb[0][1]], emb[1])
                        if o.is_dma:
                            ins.then_inc(dma_sems[o.dsem], 16)
                        elif o.signal:
                            ins.then_inc(sems[e], 1)

                getattr(block, engobj[e])(body)


class Rng:
    def __init__(self, lo, n, nt):
        assert lo + n * nt == TL and n <= 512
        self.lo, self.n, self.nt = lo, n, nt


R_L = [
    (Rng(0, 362, 3), Rng(30, 352, 3)),
    (Rng(30, 352, 3), Rng(62, 512, 2)),
]
R_FINAL = Rng(62, 512, 2)


def _par_base(l):
    return 48 + l * 304


def build_program(depth_run=DEPTH):
    nc = bass.Bass("TRN2", target_bir_lowering=False)
    xT = nc.dram_tensor("xT", [D, TL], F32, kind="ExternalInput").ap()
    pT = nc.dram_tensor("pT", [DEPTH, PLE, TL], F32, kind="ExternalInput").ap()
    par_d = nc.dram_tensor("par", [128, NPAR], F32, kind="ExternalInput").ap()
    icnt_d = nc.dram_tensor("icnt", [128, 64], F32, kind="ExternalInput").ap()
    ident_d = nc.dram_tensor("ident", [128, 128], F32, kind="ExternalInput").ap()
    w_in = nc.dram_tensor("w_in", [DEPTH, D, IN_COLS], F32, kind="ExternalInput").ap()
    pool_w = nc.dram_tensor("pool_w", [DEPTH, 4, 256, 256], F32, kind="ExternalInput").ap()
    w_bo = nc.dram_tensor("w_branch_out", [DEPTH, 3, E, D], F32, kind="ExternalInput").ap()
    w_o = nc.dram_tensor("w_o", [DEPTH, D, D], F32, kind="ExternalInput").ap()
    w_pg = nc.dram_tensor("w_ple_gate", [DEPTH, D, D], F32, kind="ExternalInput").ap()
    w_pp = nc.dram_tensor("w_ple_proj", [DEPTH, PLE, D], F32, kind="ExternalInput").ap()
    outT = nc.dram_tensor("outT", [D, TOK], F32, kind="ExternalOutput").ap()

    S = Sched(nc)
    dma_names = ["x0", "x1", "x2", "x3", "par", "ident", "p", "ring0", "ring1", "side0", "side1", "out0", "out1", "out2", "out3"]

    def ACT(out, in_, func, reads, writes, **kw):
        S.op("act", "activation", dict(out=out, in_=in_, func=func, **kw), reads, writes)

    def TT(out, in0, in1, op, reads, writes):
        S.op("dve", "tensor_tensor", dict(out=out, in0=in0, in1=in1, op=op), reads, writes)

    def TS(out, in0, s1, s2, op0, op1, reads, writes):
        kw = dict(out=out, in0=in0, scalar1=s1, scalar2=s2, op0=op0)
        if op1 is not None:
            kw["op1"] = op1
        S.op("dve", "tensor_scalar", kw, reads, writes)

    def STT(out, in0, scalar, in1, op0, op1, reads, writes):
        S.op("dve", "scalar_tensor_tensor", dict(out=out, in0=in0, scalar=scalar, in1=in1, op0=op0, op1=op1), reads, writes)

    def RECIP(out, in_, reads, writes):
        S.op("dve", "reciprocal", dict(out=out, in_=in_), reads, writes)

    def MM(out, lhsT, rhs, start, stop, reads, writes):
        S.op("pe", "matmul", dict(out=out, lhsT=lhsT, rhs=rhs, start=start, stop=stop), reads, writes)

    def DMA(eng, out, in_, reads, writes, sem):
        S.op(eng, "dma_start", dict(out=out, in_=in_), reads, writes, dma_sem=sem)

    with ExitStack() as es:
        ec = es.enter_context
        h = ec(nc.sbuf_tensor("h", [128, 16, TL], F32))
        xn = ec(nc.sbuf_tensor("xn", [128, 16, TL], BF16))
        acc = ec(nc.sbuf_tensor("acc", [128, 16, TL], BF16))
        yb = ec(nc.sbuf_tensor("yb", [128, 8, TL], BF16))
        ring = ec(nc.sbuf_tensor("ring", [128, 2, 16, 256], BF16))
        side = ec(nc.sbuf_tensor("side", [128, 2, 2, 256], BF16))
        dgb = ec(nc.sbuf_tensor("dgb", [128, 31 * 128], BF16))
        par = ec(nc.sbuf_tensor("par_s", [128, NPAR], F32))
        icnt = ec(nc.sbuf_tensor("icnt_s", [128, 64], F32))
        ident = ec(nc.sbuf_tensor("ident_s", [128, 128], BF16))
        ones = ec(nc.sbuf_tensor("ones_s", [128, 128], BF16))
        tmp16 = ec(nc.sbuf_tensor("tmp16", [128, 16], F32))
        Tt = ec(nc.sbuf_tensor("Tt", [128, 6, TL], F32))
        psa = ec(nc.psum_tensor("psa", [128, 8, 512], F32))
        s_pe, s_act, s_dve, s_pool, s_sp = [ec(nc.semaphore(n)) for n in ("s_pe", "s_act", "s_dve", "s_pool", "s_sp")]
        dlist = [ec(nc.semaphore(f"d{i}")) for i in range(len(dma_names))]
        dsems = dict(zip(dma_names, dlist))
        r_bank = [Res(f"bank{i}") for i in range(8)]
        r_h = [Res(f"h{k}") for k in range(16)]
        r_xn = [Res(f"xn{k}") for k in range(16)]
        r_acc = [Res(f"acc{k}") for k in range(16)]
        r_yb = [Res(f"yb{k}") for k in range(8)]
        r_ring = [Res("ring0"), Res("ring1")]
        r_side = [Res("side0"), Res("side1")]
        r_dgk = [Res(f"dg{k}") for k in range(31)]
        r_par = Res("par")
        r_ident = Res("ident")
        r_ones = Res("ones")
        r_tmp16 = Res("tmp16")
        r_Ta = [Res(f"T{i}a") for i in range(6)]
        r_Tb = [Res(f"T{i}b") for i in range(6)]
        r_out = [Res(f"out{k}") for k in range(16)]

        dg = dgb[:, :].rearrange("p (k c) -> p k c", c=128)
        pTs = dgb[:, 0:2 * TL].rearrange("p (k t) -> p k t", t=TL)

        def TF(i):
            return Tt[:, i, :], [r_Ta[i], r_Tb[i]]

        def TB(i, half):
            v = Tt[:, i, :].bitcast(BF16)
            return v[:, half * TL:(half + 1) * TL], [r_Ta[i] if half == 0 else r_Tb[i]]

        def v3(ap2d, R):
            return ap2d[:, R.lo:TL].rearrange("p (j n) -> p j n", n=R.n)

        class PGrp:
            def __init__(self, base, nt):
                self.base, self.nt = base, nt
                self.res = r_bank[base:base + nt]

        def pv(g, R):
            return psa[:, g.base:g.base + R.nt, 0:R.n]

        def pcol(c):
            return par[:, c:c + 1]

        state = {"pg": 0, "ring": 0, "side": 0, "reserved": set()}

        def next_pg(nt):
            b = state["pg"]
            for _ in range(8):
                if b + nt > 8:
                    b = 0
                if not (set(range(b, b + nt)) & state["reserved"]):
                    break
                b += nt
            assert b + nt <= 8 and not (set(range(b, b + nt)) & state["reserved"])
            state["pg"] = b + nt
            return PGrp(b, nt)

        def load_main(src2d, kch):
            s = state["ring"]
            state["ring"] ^= 1
            hold = [r_h[11]] if state.setdefault("n_main", 0) < 2 else []
            state["n_main"] += 1
            DMA("pool", ring[:, s, 0:kch, :], src2d.rearrange("(k p) c -> p k c", p=128), hold, [r_ring[s]], f"ring{s}")
            return s

        def load_main512(src2d):
            s = state["ring"]
            state["ring"] ^= 1
            dst = ring[:, s, :, :].rearrange("p (k a) c -> p k (a c)", a=2)
            DMA("pool", dst, src2d.rearrange("(k p) c -> p k c", p=128), [], [r_ring[s]], f"ring{s}")
            return s

        def load_side(src2d):
            s = state["side"]
            state["side"] ^= 1
            DMA("pool", side[:, s, :, :], src2d.rearrange("(k p) c -> p k c", p=128), [], [r_side[s]], f"side{s}")
            return s

        def mm_group(steps, R):
            g = next_pg(R.nt)
            ns = len(steps)
            for i, (lhsT, rhs2d, shift, reads) in enumerate(steps):
                for tt in range(R.nt):
                    lo = R.lo + tt * R.n + shift
                    MM(psa[:, g.base + tt, 0:R.n], lhsT, rhs2d[:, lo:lo + R.n], (i == 0), (i == ns - 1), reads, g.res)
            return g

        def mm_group2(stepsA, stepsB, R):
            gA = next_pg(R.nt)
            gB = next_pg(R.nt)
            ns = len(stepsA)
            for i in range(ns):
                for g, steps in ((gA, stepsA), (gB, stepsB)):
                    lhsT, rhs2d, shift, reads = steps[i]
                    for tt in range(R.nt):
                        lo = R.lo + tt * R.n + shift
                        MM(psa[:, g.base + tt, 0:R.n], lhsT, rhs2d[:, lo:lo + R.n], (i == 0), (i == ns - 1), reads, g.res)
            return [gA, gB]

        def proj_steps(slot, coff, src, r_src):
            return [(ring[:, slot, k, coff:coff + 128], src[:, k, :], 0, [r_ring[slot], r_src[k]]) for k in range(16)]

        for i4 in range(4):
            DMA("sp", h[:, 4 * i4:4 * i4 + 4, :], xT[512 * i4:512 * i4 + 512, :].rearrange("(k p) t -> p k t", p=128), [],
                r_h[4 * i4:4 * i4 + 4], f"x{i4}")
        DMA("sp", par[:, :], par_d, [], [r_par], "par")
        DMA("sp", icnt[:, :], icnt_d, [], [r_par], "par")
        DMA("pool", ident[:, :], ident_d, [r_h[11]], [r_ident], "ident")
        S.op("dve", "memset", dict(ap=ones[:, :], constant=1.0), [], [r_ones])

        def rms_sq(k, R):
            if k % 2 == 0:
                sq, r_sq = TB(5, (k // 2) % 2)
                ACT(sq[:, R.lo:TL], h[:, k, R.lo:TL], AF.Square, [r_h[k]], r_sq)
            else:
                sq, r_sq = TB(4, (k // 2) % 2)
                TT(sq[:, R.lo:TL], h[:, k, R.lo:TL], h[:, k, R.lo:TL], ALU.mult, [r_h[k]], r_sq)
            return sq, r_sq

        def rms_mm(g, k, R, sq, r_sq):
            for tt in range(R.nt):
                lo = R.lo + tt * R.n
                MM(psa[:, g.base + tt, 0:R.n], ones[:, :], sq[:, lo:lo + R.n], (k == 0), (k == 15), [r_ones] + r_sq, g.res)

        def rms_step(g, k, R):
            sq, r_sq = rms_sq(k, R)
            rms_mm(g, k, R, sq, r_sq)

        def rms_finish(g, R):
            rs, r_rs = TF(0)
            ACT(v3(rs, R), pv(g, R), AF.Ln, g.res + [r_par], r_rs, scale=1.0 / D, bias=pcol(C_RMS_EPS))
            ACT(rs[:, R.lo:TL], rs[:, R.lo:TL], AF.Exp, r_rs, r_rs, scale=-0.5)
            return rs, r_rs

        def rms_stats(R):
            g = next_pg(R.nt)
            for k in range(16):
                rms_step(g, k, R)
            return rms_finish(g, R)

        for l in range(depth_run):
            R_in, R_out = R_L[l]
            PB = _par_base(l)
            wl = w_in[l]
            LO = R_out.lo

            rs, r_rs = rms_stats(R_in)
            for k in range(16):
                STT(xn[:, k, R_in.lo:TL], h[:, k, R_in.lo:TL], pcol(l * 16 + k), rs[:, R_in.lo:TL], ALU.mult, ALU.mult,
                    [r_h[k], r_par] + r_rs, [r_xn[k]])

            def merge_branch(b):
                for q in range(4):
                    sgs = []
                    for hpair in range(2):
                        d0c = q * 4 + hpair * 2
                        c0 = 9216 + b * 2048 + d0c * 128
                        slot = load_main(wl[:, c0:c0 + 256], 16)
                        for jj in range(2):
                            g = mm_group(proj_steps(slot, jj * 128, xn, r_xn), R_out)
                            sg, r_sg = TF(hpair * 2 + jj)
                            ACT(v3(sg, R_out), pv(g, R_out), AF.Sigmoid, g.res, r_sg)
                            sgs.append((sg, r_sg))
                    slot = load_main512(w_bo[l, b][:, q * 512:(q + 1) * 512])
                    wv = ring[:, slot, :, :].rearrange("p (k a) c -> p k (a c)", a=2)
                    for jj in range(4):
                        d = q * 4 + jj
                        steps = [(wv[:, k, jj * 128:(jj + 1) * 128], yb[:, k, :], 0, [r_ring[slot], r_yb[k]]) for k in range(8)]
                        g = mm_group(steps, R_out)
                        sg, r_sg = sgs[jj]
                        if b == 0:
                            TT(v3(acc[:, d, :], R_out), pv(g, R_out), v3(sg, R_out), ALU.mult, g.res + r_sg, [r_acc[d]])
                        else:
                            t, r_t = TF(4 + (jj % 2))
                            TT(v3(t, R_out), pv(g, R_out), v3(sg, R_out), ALU.mult, g.res + r_sg, r_t)
                            TT(acc[:, d, LO:TL], acc[:, d, LO:TL], t[:, LO:TL], ALU.add, [r_acc[d]] + r_t, [r_acc[d]])

            for j in range(4):
                u = [2 * j, 2 * j + 1]
                tA = [TF(0), TF(1)]
                cv = [TF(2), TF(3)]
                slot = load_main(wl[:, 0 * E + u[0] * 128: 0 * E + u[0] * 128 + 256], 16)
                if j == 0:
                    gs = mm_group2(proj_steps(slot, 0, xn, r_xn), proj_steps(slot, 128, xn, r_xn), R_in)
                else:
                    gs = [mm_group(proj_steps(slot, i * 128, xn, r_xn), R_in) for i in range(2)]
                for i in range(2):
                    ACT(v3(tA[i][0], R_in), pv(gs[i], R_in), AF.Copy, gs[i].res, tA[i][1])
                slot = load_main(wl[:, 2 * E + u[0] * 128: 2 * E + u[0] * 128 + 256], 16)
                for i in range(2):
                    g = mm_group(proj_steps(slot, i * 128, xn, r_xn), R_in)
                    cx, r_cx = tA[i]
                    co, r_co = cv[i]
                    TT(v3(cx, R_in), pv(g, R_in), v3(cx, R_in), ALU.mult, g.res + r_cx, r_cx)
                    wc = PB + u[i] * 3
                    TS(co[:, LO:TL], cx[:, LO:TL], pcol(wc + 2), None, ALU.mult, None, r_cx + [r_par], r_co)
                    for sh in (1, 2):
                        STT(co[:, LO:TL], cx[:, LO - sh:TL - sh], pcol(wc + 2 - sh), co[:, LO:TL], ALU.mult, ALU.add,
                            r_cx + r_co + [r_par], r_co)
                slot = load_main(wl[:, 1 * E + u[0] * 128: 1 * E + u[0] * 128 + 256], 16)
                for i in range(2):
                    g = mm_group(proj_steps(slot, i * 128, xn, r_xn), R_out)
                    co, r_co = cv[i]
                    TT(v3(co, R_out), pv(g, R_out), v3(co, R_out), ALU.mult, g.res + r_co, r_co)
                slot = load_main(wl[:, 3 * E + u[0] * 128: 3 * E + u[0] * 128 + 256], 16)
                for i in range(2):
                    g = mm_group(proj_steps(slot, i * 128, xn, r_xn), R_out)
                    sz, r_sz = tA[i]
                    co, r_co = cv[i]
                    ACT(v3(sz, R_out), pv(g, R_out), AF.Silu, g.res, r_sz)
                    TT(yb[:, u[i], LO:TL], co[:, LO:TL], sz[:, LO:TL], ALU.mult, r_co + r_sz, [r_yb[u[i]]])
            merge_branch(0)

            for j in range(4):
                u = [2 * j, 2 * j + 1]
                wlen = 2 ** (j + 1)
                U = [TF(0), TF(1)]
                XY = [TF(2), TF(3)]
                P = [TB(4, 0), TB(4, 1)]
                slot = load_main(wl[:, 4 * E + u[0] * 128: 4 * E + u[0] * 128 + 256], 16)
                for i in range(2):
                    g = mm_group(proj_steps(slot, i * 128, xn, r_xn), R_in)
                    Ui, r_Ui = U[i]
                    ACT(v3(Ui, R_in), pv(g, R_in), AF.Copy, g.res, r_Ui)
                    src, r_src = Ui, r_Ui
                    lo_valid = R_in.lo
                    sh = 1
                    st = 0
                    while sh < wlen:
                        dst, r_dst = XY[st % 2]
                        lo_new = lo_valid + sh
                        TT(dst[:, lo_new:TL], src[:, lo_new:TL], src[:, lo_new - sh:TL - sh], ALU.add, r_src, r_dst)
                        src, r_src = dst, r_dst
                        lo_valid = lo_new
                        sh *= 2
                        st += 1
                    assert lo_valid <= LO
                    Pi, r_Pi = P[i]
                    STT(Pi[:, LO:TL], src[:, LO:TL], 1.0 / wlen, Ui[:, LO:TL], ALU.mult, ALU.subtract, r_src + r_Ui, r_Pi)
                    TT(tmp16[:, :], src[:, HALO:HALO + 16], icnt[:, j * 16:(j + 1) * 16], ALU.mult, r_src + [r_par], [r_tmp16])
                    TT(Pi[:, HALO:HALO + 16], tmp16[:, :], Ui[:, HALO:HALO + 16], ALU.subtract, [r_tmp16] + r_Ui, r_Pi)
                slot = load_main(wl[:, 5 * E + u[0] * 128: 5 * E + u[0] * 128 + 256], 16)
                for i in range(2):
                    g = mm_group(proj_steps(slot, i * 128, xn, r_xn), R_out)
                    ACT(v3(U[i][0], R_out), pv(g, R_out), AF.Silu, g.res, U[i][1])
                ss = load_side(pool_w[l, j])
                for o in range(2):
                    steps = [(side[:, ss, i, o * 128:(o + 1) * 128], P[i][0], 0, [r_side[ss]] + P[i][1]) for i in range(2)]
                    g = mm_group(steps, R_out)
                    STT(v3(yb[:, u[o], :], R_out), pv(g, R_out), pcol(PB + 24 + u[o]), v3(U[o][0], R_out), ALU.mult, ALU.mult,
                        g.res + [r_par] + U[o][1], [r_yb[u[o]]])
            merge_branch(1)

            ND = 11
            for j in range(4):
                u = [2 * j, 2 * j + 1]
                SG = [TF(0), TF(1)]
                V = [TB(2, 0), TB(2, 1)]
                DACC = [TF(3), TF(4)]
                PEP = [TF(0), TF(5)]

                def build_diag(ui):
                    wc = PB + 56 + ui * 31
                    for k in range(ND, 31):
                        TS(dg[:, k, :], ident[:, :], pcol(wc + k), None, ALU.mult, None, [r_ident, r_par], [r_dgk[k]])

                def chain(i, ks):
                    Vi, r_Vi = V[i]
                    da, r_da = DACC[i]
                    wc = PB + 56 + u[i] * 31
                    for k in ks:
                        srcv = Vi[:, LO - 30 + k:TL - 30 + k]
                        if k == 0:
                            TS(da[:, LO:TL], srcv, pcol(wc + k), None, ALU.mult, None, r_Vi + [r_par], r_da)
                        else:
                            STT(da[:, LO:TL], srcv, pcol(wc + k), da[:, LO:TL], ALU.mult, ALU.add, r_Vi + r_da + [r_par], r_da)

                def conv_pe(i):
                    Vi, r_Vi = V[i]
                    steps = [(dg[:, k, :], Vi, k - 30, [r_dgk[k]] + r_Vi) for k in range(ND, 31)]
                    g = mm_group(steps, R_out)
                    pp, r_pp = PEP[i]
                    ACT(v3(pp, R_out), pv(g, R_out), AF.Identity, g.res + [r_par], r_pp, bias=pcol(PB + 32 + u[i]))

                build_diag(u[0])
                slot = load_main(wl[:, 7 * E + u[0] * 128: 7 * E + u[0] * 128 + 256], 16)
                for i in range(2):
                    g = mm_group(proj_steps(slot, i * 128, xn, r_xn), R_in)
                    ACT(v3(SG[i][0], R_in), pv(g, R_in), AF.Sigmoid, g.res, SG[i][1])
                slot = load_main(wl[:, 6 * E + u[0] * 128: 6 * E + u[0] * 128 + 256], 16)
                for i in range(2):
                    g = mm_group(proj_steps(slot, i * 128, xn, r_xn), R_in)
                    TT(v3(V[i][0], R_in), pv(g, R_in), v3(SG[i][0], R_in), ALU.mult, g.res + SG[i][1], V[i][1])
                    if i == 0:
                        chain(0, range(0, ND // 2))
                chain(0, range(ND // 2, ND))
                conv_pe(0)
                build_diag(u[1])
                conv_pe(1)
                chain(1, range(0, ND))
                for i in range(2):
                    TT(yb[:, u[i], LO:TL], PEP[i][0][:, LO:TL], DACC[i][0][:, LO:TL], ALU.add, PEP[i][1] + DACC[i][1], [r_yb[u[i]]])
            def cz_proj(jj, i):
                g = mm_group(proj_steps(cz_slot[jj], i * 128, xn, r_xn), R_out)
                Z, r_Z = TF(i)
                ACT(v3(Z, R_out), pv(g, R_out), AF.Silu, g.res, r_Z)
                return 2 * jj + i, Z, r_Z

            def cz_norm(ui, Z, r_Z):
                a, r_a = TF(2)
                s, r_s = TF(5)
                TT(a[:, LO:TL], yb[:, ui, LO:TL], mu[:, LO:TL], ALU.subtract, [r_yb[ui]] + r_mu, r_a)
                TT(a[:, LO:TL], a[:, LO:TL], rstd[:, LO:TL], ALU.mult, r_a + r_rstd, r_a)
                ACT(s[:, LO:TL], a[:, LO:TL], AF.Silu, r_a + [r_par], r_s, bias=pcol(PB + 48 + ui), scale=pcol(PB + 40 + ui))
                TT(yb[:, ui, LO:TL], s[:, LO:TL], Z[:, LO:TL], ALU.mult, r_s + r_Z, [r_yb[ui]])

            cz_slot = {0: load_main(wl[:, 8 * E: 8 * E + 256], 16)}
            pre = [cz_proj(0, 0), cz_proj(0, 1)]
            steps = [(ones[:, :], yb[:, k, :], 0, [r_ones, r_yb[k]]) for k in range(8)]
            g1 = mm_group(steps, R_out)
            mu, r_mu = TF(3)
            TS(v3(mu, R_out), pv(g1, R_out), 1.0 / E, None, ALU.mult, None, g1.res, r_mu)
            g2 = next_pg(R_out.nt)
            for k in range(8):
                if k % 2 == 0:
                    sq, r_sq = TB(5, (k // 2) % 2)
                    ACT(sq[:, LO:TL], yb[:, k, LO:TL], AF.Square, [r_yb[k]], r_sq)
                else:
                    sq, r_sq = TB(4, (k // 2) % 2)
                    TT(sq[:, LO:TL], yb[:, k, LO:TL], yb[:, k, LO:TL], ALU.mult, [r_yb[k]], r_sq)
                for tt in range(R_out.nt):
                    lo = LO + tt * R_out.n
                    MM(psa[:, g2.base + tt, 0:R_out.n], ones[:, :], sq[:, lo:lo + R_out.n], (k == 0), (k == 7), [r_ones] + r_sq, g2.res)
            rstd, r_rstd = TF(4)
            TS(v3(rstd, R_out), pv(g2, R_out), 1.0 / E, None, ALU.mult, None, g2.res, r_rstd)
            msq, r_msq = TF(5)
            TT(msq[:, LO:TL], mu[:, LO:TL], mu[:, LO:TL], ALU.mult, r_mu, r_msq)
            TT(rstd[:, LO:TL], rstd[:, LO:TL], msq[:, LO:TL], ALU.subtract, r_rstd + r_msq, r_rstd)
            TS(rstd[:, LO:TL], rstd[:, LO:TL], 0.0, LN_EPS, ALU.max, ALU.add, r_rstd, r_rstd)
            ACT(rstd[:, LO:TL], rstd[:, LO:TL], AF.Ln, r_rstd, r_rstd)
            ACT(rstd[:, LO:TL], rstd[:, LO:TL], AF.Exp, r_rstd, r_rstd, scale=-0.5)
            for o in pre:
                cz_norm(*o)
            for j in range(1, 4):
                cz_slot[j] = load_main(wl[:, 8 * E + 2 * j * 128: 8 * E + 2 * j * 128 + 256], 16)
                for i in range(2):
                    cz_norm(*cz_proj(j, i))
            merge_branch(2)

            for q in range(8):
                slot = load_main(w_o[l][:, q * 256:(q + 1) * 256], 16)
                for i in range(2):
                    oc = q * 2 + i
                    g = mm_group(proj_steps(slot, i * 128, acc, r_acc), R_out)
                    TT(v3(h[:, oc, :], R_out), pv(g, R_out), v3(h[:, oc, :], R_out), ALU.add, g.res + [r_h[oc]], [r_h[oc]])
                    ACT(xn[:, oc, LO:TL], h[:, oc, LO:TL], AF.Copy, [r_h[oc]], [r_xn[oc]])

            DMA("pool", pTs, pT[l].rearrange("(k p) t -> p k t", p=128), [], r_dgk, "p")
            last = (l == depth_run - 1)
            if last:
                if R_out.nt == 2:
                    state["pg"] = 6
                gF = next_pg(R_out.nt)
                state["reserved"] = set(range(gF.base, gF.base + R_out.nt))
            for q in range(8):
                pend = []
                if last and q >= 1:
                    for kk in (2 * q - 2, 2 * q - 1):
                        pend.append((kk,) + rms_sq(kk, R_out))
                slot = load_main(w_pg[l][:, q * 256:(q + 1) * 256], 16)
                ss = load_side(w_pp[l][:, q * 256:(q + 1) * 256])
                sgs = []
                for i in range(2):
                    g = mm_group(proj_steps(slot, i * 128, xn, r_xn), R_out)
                    sg, r_sg = TF(i)
                    ACT(v3(sg, R_out), pv(g, R_out), AF.Sigmoid, g.res, r_sg)
                    sgs.append((sg, r_sg))
                for kk, sq, r_sq in pend:
                    rms_mm(gF, kk, R_out, sq, r_sq)
                for i in range(2):
                    oc = q * 2 + i
                    steps = [(side[:, ss, k, i * 128:(i + 1) * 128], pTs[:, k, :], 0, [r_side[ss]] + r_dgk[0:17]) for k in range(2)]
                    g = mm_group(steps, R_out)
                    sg, r_sg = sgs[i]
                    t, r_t = TF(2 + i)
                    TT(v3(t, R_out), pv(g, R_out), v3(sg, R_out), ALU.mult, g.res + r_sg, r_t)
                    TT(h[:, oc, LO:TL], h[:, oc, LO:TL], t[:, LO:TL], ALU.add, [r_h[oc]] + r_t, [r_h[oc]])

        Rf = R_L[depth_run - 1][1]
        for kk in (14, 15):
            rms_step(gF, kk, Rf)
        state["reserved"] = set()
        rs, r_rs = rms_finish(gF, Rf)
        for k in range(16):
            ob, r_ob = TF(1 + (k % 4))
            STT(ob[:, HALO:TL], h[:, k, HALO:TL], pcol(32 + k), rs[:, HALO:TL], ALU.mult, ALU.mult, [r_h[k], r_par] + r_rs, r_ob)
            DMA("sp", outT[k * 128:(k + 1) * 128, :], ob[:, HALO:TL], r_ob, [r_out[k]], f"out{k % 4}")
        S.op("sp", None, None, r_out, [])

        S.emit({"pe": s_pe, "act": s_act, "dve": s_dve, "pool": s_pool, "sp": s_sp}, dsems)
    return nc


def _pack_params(norm_g, conv_a_w, pool_scale, conv_c_w, conv_c_b, ln_c_g, ln_c_b, final_norm_g):
    par = np.zeros((128, NPAR), dtype=np.float32)
    for l in range(DEPTH):
        par[:, l * 16:(l + 1) * 16] = norm_g[l].reshape(16, 128).T
        B = _par_base(l)
        par[:, B:B + 24] = conv_a_w[l].reshape(3, 8, 128).transpose(2, 1, 0).reshape(128, 24)
        par[:, B + 24:B + 32] = pool_scale[l].reshape(8, 128).T
        par[:, B + 32:B + 40] = conv_c_b[l].reshape(8, 128).T
        par[:, B + 40:B + 48] = ln_c_g[l].reshape(8, 128).T
        par[:, B + 48:B + 56] = ln_c_b[l].reshape(8, 128).T
        par[:, B + 56:B + 304] = conv_c_w[l].reshape(31, 8, 128).transpose(2, 1, 0).reshape(128, 248)
    par[:, 32:48] = final_norm_g.reshape(16, 128).T
    par[:, C_RMS_EPS] = RMS_EPS
    par[:, C_RMS_EPS + 1] = LN_EPS
    return par


def _icnt_table(half):
    t = np.zeros((128, 64), dtype=np.float32)
    for g, w in enumerate((2, 4, 8, 16)):
        for i in range(16):
            pos = half * TOK + i
            t[:, g * 16 + i] = np.float32(1.0) / np.float32(min(pos + 1, w))
    return t


_NC_CACHE = {}


def kernel(x, p, norm_g, w_in, conv_a_w, pool_w, pool_scale, conv_c_w, conv_c_b,
           ln_c_g, ln_c_b, w_branch_out, w_o, w_ple_gate, w_ple_proj, final_norm_g):
    f = lambda a: np.ascontiguousarray(np.asarray(a, dtype=np.float32))
    x, p = f(x), f(p)
    w_in, pool_w, w_branch_out, w_o, w_ple_gate, w_ple_proj = map(f, (w_in, pool_w, w_branch_out, w_o, w_ple_gate, w_ple_proj))
    par = _pack_params(*map(f, (norm_g, conv_a_w, pool_scale, conv_c_w, conv_c_b, ln_c_g, ln_c_b, final_norm_g)))
    ident = np.eye(128, dtype=np.float32)
    if "nc" not in _NC_CACHE:
        _NC_CACHE["nc"] = build_program()
    nc = _NC_CACHE["nc"]
    in_maps = []
    for c in range(NCORES):
        b, half = c // 2, c % 2
        xT = np.zeros((D, TL), dtype=np.float32)
        pT = np.zeros((DEPTH, PLE, TL), dtype=np.float32)
        g0 = half * TOK - HALO
        s0 = max(g0, 0)
        xT[:, s0 - g0:] = x[b, s0:half * TOK + TOK, :].T
        for l in range(DEPTH):
            pT[l][:, s0 - g0:] = p[l, b, s0:half * TOK + TOK, :].T
        in_maps.append({
            "xT": xT, "pT": pT, "par": par, "icnt": _icnt_table(half), "ident": ident,
            "w_in": w_in, "pool_w": pool_w, "w_branch_out": w_branch_out, "w_o": w_o,
            "w_ple_gate": w_ple_gate, "w_ple_proj": w_ple_proj,
        })
    res = run_bass_kernel_spmd(nc, in_maps, core_ids=list(range(NCORES)))
    out = np.empty((BATCH, SEQ, D), dtype=np.float32)
    for c in range(NCORES):
        b, half = c // 2, c % 2
        out[b, half * TOK:(half + 1) * TOK, :] = res.results[c]["outT"].T
    return out
```

```python
from contextlib import ExitStack

import numpy as np
import concourse.bass as bass
import concourse.mybir as mybir
from concourse.bass_utils import run_bass_kernel_spmd

F32 = mybir.dt.float32
BF16 = mybir.dt.bfloat16
AF = mybir.ActivationFunctionType
ALU = mybir.AluOpType

D = 2048
E = 1024
SEQ = 2048
BATCH = 4
DEPTH = 2
PLE = 256
IN_COLS = 15360
HALO = 62
TOK = 1024
TL = TOK + HALO
NCORES = 8
RMS_EPS = 1e-6
LN_EPS = 1e-5
NPAR = 48 + 2 * 304 + 2
C_RMS_EPS = 48 + 2 * 304
ENGS = ("pe", "act", "dve", "pool", "sp")


class Res:
    __slots__ = ("name", "writer", "readers")

    def __init__(self, name):
        self.name = name
        self.writer = None
        self.readers = {}


class Op:
    __slots__ = ("eng", "meth", "kw", "deps", "is_dma", "dsem", "dcount", "signal", "sigval")

    def __init__(self, eng, meth, kw):
        self.eng = eng
        self.meth = meth
        self.kw = kw
        self.deps = []
        self.is_dma = False
        self.dsem = None
        self.dcount = 0
        self.signal = False
        self.sigval = 0


class Sched:
    def __init__(self, nc):
        self.nc = nc
        self.ops = {e: [] for e in ENGS}
        self.dma_counts = {}

    def _add_dep(self, op, dep):
        if dep is None or dep is op:
            return
        if (not dep.is_dma) and (not op.is_dma) and dep.eng == "pe" and op.eng == "pe":
            return
        op.deps.append(dep)
        if not dep.is_dma:
            dep.signal = True

    def op(self, eng, meth, kw=None, reads=(), writes=(), dma_sem=None):
        o = Op(eng, meth, kw)
        if dma_sem is not None:
            o.is_dma = True
            o.dsem = dma_sem
            self.dma_counts[dma_sem] = self.dma_counts.get(dma_sem, 0) + 16
            o.dcount = self.dma_counts[dma_sem]
        for r in reads:
            self._add_dep(o, r.writer)
        for w in writes:
            self._add_dep(o, w.writer)
            for rd in w.readers.values():
                self._add_dep(o, rd)
        for w in writes:
            w.writer = o
            w.readers = {}
        for r in reads:
            r.readers[id(o) if o.is_dma else o.eng] = o
        self.ops[eng].append(o)
        return o

    def emit(self, sems, dma_sems):
        nc = self.nc
        for e in ENGS:
            c = 0
            for o in self.ops[e]:
                if (not o.is_dma) and o.signal:
                    c += 1
                    o.sigval = c
        engobj = {"pe": "tensor", "act": "scalar", "dve": "vector", "pool": "gpsimd", "sp": "sync"}
        with nc.Block() as block:
            for e in ENGS:
                ops = self.ops[e]
                if not ops:
                    continue

                def body(eng, ops=ops, e=e):
                    waited = {}
                    for o in ops:
                        need = {}
                        for d in o.deps:
                            if d.is_dma:
                                key, val = ("d", d.dsem), d.dcount
                            else:
                                key, val = ("e", d.eng), d.sigval
                            if val > need.get(key, 0):
                                need[key] = val
                        todo = []
                        for key, val in need.items():
                            if waited.get(key, 0) >= val:
                                continue
                            waited[key] = val
                            todo.append((key, val))
                        emb = None
                        if o.meth is not None and not o.is_dma:
                            for t in todo:
                                if t[0][0] == "e":
                                    emb = t
                            if emb is not None:
                                todo.remove(emb)
                        for key, val in todo:
                            sem = dma_sems[key[1]] if key[0] == "d" else sems[key[1]]
                            eng.wait_ge(sem, val)
                        if o.meth is None:
                            continue
                        ins = getattr(eng, o.meth)(**o.kw)
                        if emb is not None:
                            ins._wait_ge(sems[emb[0][1]], emb[1])
                        if o.is_dma:
                            ins.then_inc(dma_sems[o.dsem], 16)
                        elif o.signal:
                            ins.then_inc(sems[e], 1)

                getattr(block, engobj[e])(body)


class Rng:
    def __init__(self, lo, n, nt):
        assert lo + n * nt == TL and n <= 512
        self.lo, self.n, self.nt = lo, n, nt


R_L = [
    (Rng(0, 362, 3), Rng(30, 352, 3)),
    (Rng(30, 352, 3), Rng(62, 512, 2)),
]
R_FINAL = Rng(62, 512, 2)


def _par_base(l):
    return 48 + l * 304


def build_program(depth_run=DEPTH):
    nc = bass.Bass("TRN2", target_bir_lowering=False)
    xT = nc.dram_tensor("xT", [D, TL], F32, kind="ExternalInput").ap()
    pT = nc.dram_tensor("pT", [DEPTH, PLE, TL], F32, kind="ExternalInput").ap()
    par_d = nc.dram_tensor("par", [128, NPAR], F32, kind="ExternalInput").ap()
    icnt_d = nc.dram_tensor("icnt", [128, 64], F32, kind="ExternalInput").ap()
    ident_d = nc.dram_tensor("ident", [128, 128], F32, kind="ExternalInput").ap()
    w_in = nc.dram_tensor("w_in", [DEPTH, D, IN_COLS], F32, kind="ExternalInput").ap()
    pool_w = nc.dram_tensor("pool_w", [DEPTH, 4, 256, 256], F32, kind="ExternalInput").ap()
    w_bo = nc.dram_tensor("w_branch_out", [DEPTH, 3, E, D], F32, kind="ExternalInput").ap()
    w_o = nc.dram_tensor("w_o", [DEPTH, D, D], F32, kind="ExternalInput").ap()
    w_pg = nc.dram_tensor("w_ple_gate", [DEPTH, D, D], F32, kind="ExternalInput").ap()
    w_pp = nc.dram_tensor("w_ple_proj", [DEPTH, PLE, D], F32, kind="ExternalInput").ap()
    outT = nc.dram_tensor("outT", [D, TOK], F32, kind="ExternalOutput").ap()

    S = Sched(nc)
    dma_names = ["x0", "x1", "x2", "x3", "par", "ident", "p", "ring0", "ring1", "side0", "side1", "out0", "out1", "out2", "out3"]

    def ACT(out, in_, func, reads, writes, **kw):
        S.op("act", "activation", dict(out=out, in_=in_, func=func, **kw), reads, writes)

    def TT(out, in0, in1, op, reads, writes):
        S.op("dve", "tensor_tensor", dict(out=out, in0=in0, in1=in1, op=op), reads, writes)

    def TS(out, in0, s1, s2, op0, op1, reads, writes):
        kw = dict(out=out, in0=in0, scalar1=s1, scalar2=s2, op0=op0)
        if op1 is not None:
            kw["op1"] = op1
        S.op("dve", "tensor_scalar", kw, reads, writes)

    def STT(out, in0, scalar, in1, op0, op1, reads, writes):
        S.op("dve", "scalar_tensor_tensor", dict(out=out, in0=in0, scalar=scalar, in1=in1, op0=op0, op1=op1), reads, writes)

    def RECIP(out, in_, reads, writes):
        S.op("dve", "reciprocal", dict(out=out, in_=in_), reads, writes)

    def MM(out, lhsT, rhs, start, stop, reads, writes):
        S.op("pe", "matmul", dict(out=out, lhsT=lhsT, rhs=rhs, start=start, stop=stop), reads, writes)

    def DMA(eng, out, in_, reads, writes, sem):
        S.op(eng, "dma_start", dict(out=out, in_=in_), reads, writes, dma_sem=sem)

    with ExitStack() as es:
        ec = es.enter_context
        h = ec(nc.sbuf_tensor("h", [128, 16, TL], F32))
        xn = ec(nc.sbuf_tensor("xn", [128, 16, TL], BF16))
        acc = ec(nc.sbuf_tensor("acc", [128, 16, TL], BF16))
        yb = ec(nc.sbuf_tensor("yb", [128, 8, TL], BF16))
        ring = ec(nc.sbuf_tensor("ring", [128, 2, 16, 256], BF16))
        side = ec(nc.sbuf_tensor("side", [128, 2, 2, 256], BF16))
        dgb = ec(nc.sbuf_tensor("dgb", [128, 31 * 128], BF16))
        par = ec(nc.sbuf_tensor("par_s", [128, NPAR], F32))
        icnt = ec(nc.sbuf_tensor("icnt_s", [128, 64], F32))
        ident = ec(nc.sbuf_tensor("ident_s", [128, 128], BF16))
        ones = ec(nc.sbuf_tensor("ones_s", [128, 128], BF16))
        tmp16 = ec(nc.sbuf_tensor("tmp16", [128, 16], F32))
        Tt = ec(nc.sbuf_tensor("Tt", [128, 6, TL], F32))
        psa = ec(nc.psum_tensor("psa", [128, 8, 512], F32))
        s_pe, s_act, s_dve, s_pool, s_sp = [ec(nc.semaphore(n)) for n in ("s_pe", "s_act", "s_dve", "s_pool", "s_sp")]
        dlist = [ec(nc.semaphore(f"d{i}")) for i in range(len(dma_names))]
        dsems = dict(zip(dma_names, dlist))
        r_bank = [Res(f"bank{i}") for i in range(8)]
        r_h = [Res(f"h{k}") for k in range(16)]
        r_xn = [Res(f"xn{k}") for k in range(16)]
        r_acc = [Res(f"acc{k}") for k in range(16)]
        r_yb = [Res(f"yb{k}") for k in range(8)]
        r_ring = [Res("ring0"), Res("ring1")]
        r_side = [Res("side0"), Res("side1")]
        r_dgk = [Res(f"dg{k}") for k in range(31)]
        r_par = Res("par")
        r_ident = Res("ident")
        r_ones = Res("ones")
        r_tmp16 = Res("tmp16")
        r_Ta = [Res(f"T{i}a") for i in range(6)]
        r_Tb = [Res(f"T{i}b") for i in range(6)]
        r_out = [Res(f"out{k}") for k in range(16)]

        dg = dgb[:, :].rearrange("p (k c) -> p k c", c=128)
        pTs = dgb[:, 0:2 * TL].rearrange("p (k t) -> p k t", t=TL)

        def TF(i):
            return Tt[:, i, :], [r_Ta[i], r_Tb[i]]

        def TB(i, half):
            v = Tt[:, i, :].bitcast(BF16)
            return v[:, half * TL:(half + 1) * TL], [r_Ta[i] if half == 0 else r_Tb[i]]

        def v3(ap2d, R):
            return ap2d[:, R.lo:TL].rearrange("p (j n) -> p j n", n=R.n)

        class PGrp:
            def __init__(self, base, nt):
                self.base, self.nt = base, nt
                self.res = r_bank[base:base + nt]

        def pv(g, R):
            return psa[:, g.base:g.base + R.nt, 0:R.n]

        def pcol(c):
            return par[:, c:c + 1]

        state = {"pg": 0, "ring": 0, "side": 0, "reserved": set()}

        def next_pg(nt):
            b = state["pg"]
            for _ in range(8):
                if b + nt > 8:
                    b = 0
                if not (set(range(b, b + nt)) & state["reserved"]):
                    break
                b += nt
            assert b + nt <= 8 and not (set(range(b, b + nt)) & state["reserved"])
            state["pg"] = b + nt
            return PGrp(b, nt)

        def load_main(src2d, kch):
            s = state["ring"]
            state["ring"] ^= 1
            hold = [r_h[11]] if state.setdefault("n_main", 0) < 2 else []
            state["n_main"] += 1
            DMA("pool", ring[:, s, 0:kch, :], src2d.rearrange("(k p) c -> p k c", p=128), hold, [r_ring[s]], f"ring{s}")
            return s

        def load_main512(src2d):
            s = state["ring"]
            state["ring"] ^= 1
            dst = ring[:, s, :, :].rearrange("p (k a) c -> p k (a c)", a=2)
            DMA("pool", dst, src2d.rearrange("(k p) c -> p k c", p=128), [], [r_ring[s]], f"ring{s}")
            return s

        def load_side(src2d):
            s = state["side"]
            state["side"] ^= 1
            DMA("pool", side[:, s, :, :], src2d.rearrange("(k p) c -> p k c", p=128), [], [r_side[s]], f"side{s}")
            return s

        def mm_group(steps, R):
            g = next_pg(R.nt)
            ns = len(steps)
            for i, (lhsT, rhs2d, shift, reads) in enumerate(steps):
                for tt in range(R.nt):
                    lo = R.lo + tt * R.n + shift
                    MM(psa[:, g.base + tt, 0:R.n], lhsT, rhs2d[:, lo:lo + R.n], (i == 0), (i == ns - 1), reads, g.res)
            return g

        def proj_steps(slot, coff, src, r_src):
            return [(ring[:, slot, k, coff:coff + 128], src[:, k, :], 0, [r_ring[slot], r_src[k]]) for k in range(16)]

        for i4 in range(4):
            DMA("sp", h[:, 4 * i4:4 * i4 + 4, :], xT[512 * i4:512 * i4 + 512, :].rearrange("(k p) t -> p k t", p=128), [],
                r_h[4 * i4:4 * i4 + 4], f"x{i4}")
        DMA("sp", par[:, :], par_d, [], [r_par], "par")
        DMA("sp", icnt[:, :], icnt_d, [], [r_par], "par")
        DMA("pool", ident[:, :], ident_d, [r_h[11]], [r_ident], "ident")
        S.op("dve", "memset", dict(ap=ones[:, :], constant=1.0), [], [r_ones])

        def rms_sq(k, R):
            if k % 2 == 0:
                sq, r_sq = TB(5, (k // 2) % 2)
                ACT(sq[:, R.lo:TL], h[:, k, R.lo:TL], AF.Square, [r_h[k]], r_sq)
            else:
                sq, r_sq = TB(4, (k // 2) % 2)
                TT(sq[:, R.lo:TL], h[:, k, R.lo:TL], h[:, k, R.lo:TL], ALU.mult, [r_h[k]], r_sq)
            return sq, r_sq

        def rms_mm(g, k, R, sq, r_sq):
            for tt in range(R.nt):
                lo = R.lo + tt * R.n
                MM(psa[:, g.base + tt, 0:R.n], ones[:, :], sq[:, lo:lo + R.n], (k == 0), (k == 15), [r_ones] + r_sq, g.res)

        def rms_step(g, k, R):
            sq, r_sq = rms_sq(k, R)
            rms_mm(g, k, R, sq, r_sq)

        def rms_finish(g, R):
            rs, r_rs = TF(0)
            ACT(v3(rs, R), pv(g, R), AF.Ln, g.res + [r_par], r_rs, scale=1.0 / D, bias=pcol(C_RMS_EPS))
            ACT(rs[:, R.lo:TL], rs[:, R.lo:TL], AF.Exp, r_rs, r_rs, scale=-0.5)
            return rs, r_rs

        def rms_stats(R):
            g = next_pg(R.nt)
            for k in range(16):
                rms_step(g, k, R)
            return rms_finish(g, R)

        for l in range(depth_run):
            R_in, R_out = R_L[l]
            PB = _par_base(l)
            wl = w_in[l]
            LO = R_out.lo

            rs, r_rs = rms_stats(R_in)
            for k in range(16):
                STT(xn[:, k, R_in.lo:TL], h[:, k, R_in.lo:TL], pcol(l * 16 + k), rs[:, R_in.lo:TL], ALU.mult, ALU.mult,
                    [r_h[k], r_par] + r_rs, [r_xn[k]])

            def merge_branch(b):
                for q in range(4):
                    sgs = []
                    for hpair in range(2):
                        d0c = q * 4 + hpair * 2
                        c0 = 9216 + b * 2048 + d0c * 128
                        slot = load_main(wl[:, c0:c0 + 256], 16)
                        for jj in range(2):
                            g = mm_group(proj_steps(slot, jj * 128, xn, r_xn), R_out)
                            sg, r_sg = TF(hpair * 2 + jj)
                            ACT(v3(sg, R_out), pv(g, R_out), AF.Sigmoid, g.res, r_sg)
                            sgs.append((sg, r_sg))
                    slot = load_main512(w_bo[l, b][:, q * 512:(q + 1) * 512])
                    wv = ring[:, slot, :, :].rearrange("p (k a) c -> p k (a c)", a=2)
                    for jj in range(4):
                        d = q * 4 + jj
                        steps = [(wv[:, k, jj * 128:(jj + 1) * 128], yb[:, k, :], 0, [r_ring[slot], r_yb[k]]) for k in range(8)]
                        g = mm_group(steps, R_out)
                        sg, r_sg = sgs[jj]
                        if b == 0:
                            TT(v3(acc[:, d, :], R_out), pv(g, R_out), v3(sg, R_out), ALU.mult, g.res + r_sg, [r_acc[d]])
                        else:
                            t, r_t = TF(4 + (jj % 2))
                            TT(v3(t, R_out), pv(g, R_out), v3(sg, R_out), ALU.mult, g.res + r_sg, r_t)
                            TT(acc[:, d, LO:TL], acc[:, d, LO:TL], t[:, LO:TL], ALU.add, [r_acc[d]] + r_t, [r_acc[d]])

            for j in range(4):
                u = [2 * j, 2 * j + 1]
                tA = [TF(0), TF(1)]
                cv = [TF(2), TF(3)]
                slot = load_main(wl[:, 0 * E + u[0] * 128: 0 * E + u[0] * 128 + 256], 16)
                for i in range(2):
                    g = mm_group(proj_steps(slot, i * 128, xn, r_xn), R_in)
                    ACT(v3(tA[i][0], R_in), pv(g, R_in), AF.Copy, g.res, tA[i][1])
                slot = load_main(wl[:, 2 * E + u[0] * 128: 2 * E + u[0] * 128 + 256], 16)
                for i in range(2):
                    g = mm_group(proj_steps(slot, i * 128, xn, r_xn), R_in)
                    cx, r_cx = tA[i]
                    co, r_co = cv[i]
                    TT(v3(cx, R_in), pv(g, R_in), v3(cx, R_in), ALU.mult, g.res + r_cx, r_cx)
                    wc = PB + u[i] * 3
                    TS(co[:, LO:TL], cx[:, LO:TL], pcol(wc + 2), None, ALU.mult, None, r_cx + [r_par], r_co)
                    for sh in (1, 2):
                        STT(co[:, LO:TL], cx[:, LO - sh:TL - sh], pcol(wc + 2 - sh), co[:, LO:TL], ALU.mult, ALU.add,
                            r_cx + r_co + [r_par], r_co)
                slot = load_main(wl[:, 1 * E + u[0] * 128: 1 * E + u[0] * 128 + 256], 16)
                for i in range(2):
                    g = mm_group(proj_steps(slot, i * 128, xn, r_xn), R_out)
                    co, r_co = cv[i]
                    TT(v3(co, R_out), pv(g, R_out), v3(co, R_out), ALU.mult, g.res + r_co, r_co)
                slot = load_main(wl[:, 3 * E + u[0] * 128: 3 * E + u[0] * 128 + 256], 16)
                for i in range(2):
                    g = mm_group(proj_steps(slot, i * 128, xn, r_xn), R_out)
                    sz, r_sz = tA[i]
                    co, r_co = cv[i]
                    ACT(v3(sz, R_out), pv(g, R_out), AF.Silu, g.res, r_sz)
                    TT(yb[:, u[i], LO:TL], co[:, LO:TL], sz[:, LO:TL], ALU.mult, r_co + r_sz, [r_yb[u[i]]])
            merge_branch(0)

            for j in range(4):
                u = [2 * j, 2 * j + 1]
                wlen = 2 ** (j + 1)
                U = [TF(0), TF(1)]
                XY = [TF(2), TF(3)]
                P = [TB(4, 0), TB(4, 1)]
                slot = load_main(wl[:, 4 * E + u[0] * 128: 4 * E + u[0] * 128 + 256], 16)
                for i in range(2):
                    g = mm_group(proj_steps(slot, i * 128, xn, r_xn), R_in)
                    Ui, r_Ui = U[i]
                    ACT(v3(Ui, R_in), pv(g, R_in), AF.Copy, g.res, r_Ui)
                    src, r_src = Ui, r_Ui
                    lo_valid = R_in.lo
                    sh = 1
                    st = 0
                    while sh < wlen:
                        dst, r_dst = XY[st % 2]
                        lo_new = lo_valid + sh
                        TT(dst[:, lo_new:TL], src[:, lo_new:TL], src[:, lo_new - sh:TL - sh], ALU.add, r_src, r_dst)
                        src, r_src = dst, r_dst
                        lo_valid = lo_new
                        sh *= 2
                        st += 1
                    assert lo_valid <= LO
                    Pi, r_Pi = P[i]
                    STT(Pi[:, LO:TL], src[:, LO:TL], 1.0 / wlen, Ui[:, LO:TL], ALU.mult, ALU.subtract, r_src + r_Ui, r_Pi)
                    TT(tmp16[:, :], src[:, HALO:HALO + 16], icnt[:, j * 16:(j + 1) * 16], ALU.mult, r_src + [r_par], [r_tmp16])
                    TT(Pi[:, HALO:HALO + 16], tmp16[:, :], Ui[:, HALO:HALO + 16], ALU.subtract, [r_tmp16] + r_Ui, r_Pi)
                slot = load_main(wl[:, 5 * E + u[0] * 128: 5 * E + u[0] * 128 + 256], 16)
                for i in range(2):
                    g = mm_group(proj_steps(slot, i * 128, xn, r_xn), R_out)
                    ACT(v3(U[i][0], R_out), pv(g, R_out), AF.Silu, g.res, U[i][1])
                ss = load_side(pool_w[l, j])
                for o in range(2):
                    steps = [(side[:, ss, i, o * 128:(o + 1) * 128], P[i][0], 0, [r_side[ss]] + P[i][1]) for i in range(2)]
                    g = mm_group(steps, R_out)
                    STT(v3(yb[:, u[o], :], R_out), pv(g, R_out), pcol(PB + 24 + u[o]), v3(U[o][0], R_out), ALU.mult, ALU.mult,
                        g.res + [r_par] + U[o][1], [r_yb[u[o]]])
            merge_branch(1)

            ND = 12
            for j in range(4):
                u = [2 * j, 2 * j + 1]
                SG = [TF(0), TF(1)]
                V = [TB(2, 0), TB(2, 1)]
                DACC = [TF(3), TF(4)]
                PEP = [TF(0), TF(5)]

                def build_diag(ui):
                    wc = PB + 56 + ui * 31
                    for k in range(ND, 31):
                        ACT(dg[:, k, :], ident[:, :], AF.Copy, [r_ident, r_par], [r_dgk[k]], scale=pcol(wc + k))

                def chain(i, ks):
                    Vi, r_Vi = V[i]
                    da, r_da = DACC[i]
                    wc = PB + 56 + u[i] * 31
                    for k in ks:
                        srcv = Vi[:, LO - 30 + k:TL - 30 + k]
                        if k == 0:
                            TS(da[:, LO:TL], srcv, pcol(wc + k), None, ALU.mult, None, r_Vi + [r_par], r_da)
                        else:
                            STT(da[:, LO:TL], srcv, pcol(wc + k), da[:, LO:TL], ALU.mult, ALU.add, r_Vi + r_da + [r_par], r_da)

                def conv_mm(i):
                    Vi, r_Vi = V[i]
                    steps = [(dg[:, k, :], Vi, k - 30, [r_dgk[k]] + r_Vi) for k in range(ND, 31)]
                    return mm_group(steps, R_out)

                def conv_evac(i, g):
                    pp, r_pp = PEP[i]
                    ACT(v3(pp, R_out), pv(g, R_out), AF.Identity, g.res + [r_par], r_pp, bias=pcol(PB + 32 + u[i]))

                build_diag(u[0])
                slot = load_main(wl[:, 7 * E + u[0] * 128: 7 * E + u[0] * 128 + 256], 16)
                for i in range(2):
                    g = mm_group(proj_steps(slot, i * 128, xn, r_xn), R_in)
                    ACT(v3(SG[i][0], R_in), pv(g, R_in), AF.Sigmoid, g.res, SG[i][1])
                slot = load_main(wl[:, 6 * E + u[0] * 128: 6 * E + u[0] * 128 + 256], 16)
                for i in range(2):
                    g = mm_group(proj_steps(slot, i * 128, xn, r_xn), R_in)
                    TT(v3(V[i][0], R_in), pv(g, R_in), v3(SG[i][0], R_in), ALU.mult, g.res + SG[i][1], V[i][1])
                    if i == 0:
                        chain(0, range(0, ND // 2))
                chain(0, range(ND // 2, ND))
                g0 = conv_mm(0)
                build_diag(u[1])
                conv_evac(0, g0)
                conv_evac(1, conv_mm(1))
                chain(1, range(0, ND))
                for i in range(2):
                    TT(yb[:, u[i], LO:TL], PEP[i][0][:, LO:TL], DACC[i][0][:, LO:TL], ALU.add, PEP[i][1] + DACC[i][1], [r_yb[u[i]]])
            def cz_proj(jj, i):
                g = mm_group(proj_steps(cz_slot[jj], i * 128, xn, r_xn), R_out)
                Z, r_Z = TF(i)
                ACT(v3(Z, R_out), pv(g, R_out), AF.Silu, g.res, r_Z)
                return 2 * jj + i, Z, r_Z

            def cz_norm(ui, Z, r_Z):
                a, r_a = TF(2)
                s, r_s = TF(5)
                TT(a[:, LO:TL], yb[:, ui, LO:TL], mu[:, LO:TL], ALU.subtract, [r_yb[ui]] + r_mu, r_a)
                TT(a[:, LO:TL], a[:, LO:TL], rstd[:, LO:TL], ALU.mult, r_a + r_rstd, r_a)
                ACT(s[:, LO:TL], a[:, LO:TL], AF.Silu, r_a + [r_par], r_s, bias=pcol(PB + 48 + ui), scale=pcol(PB + 40 + ui))
                TT(yb[:, ui, LO:TL], s[:, LO:TL], Z[:, LO:TL], ALU.mult, r_s + r_Z, [r_yb[ui]])

            cz_slot = {0: load_main(wl[:, 8 * E: 8 * E + 256], 16)}
            pre = [cz_proj(0, 0), cz_proj(0, 1)]
            steps = [(ones[:, :], yb[:, k, :], 0, [r_ones, r_yb[k]]) for k in range(8)]
            g1 = mm_group(steps, R_out)
            mu, r_mu = TF(3)
            TS(v3(mu, R_out), pv(g1, R_out), 1.0 / E, None, ALU.mult, None, g1.res, r_mu)
            g2 = next_pg(R_out.nt)
            for k in range(8):
                if k % 2 == 0:
                    sq, r_sq = TB(5, (k // 2) % 2)
                    ACT(sq[:, LO:TL], yb[:, k, LO:TL], AF.Square, [r_yb[k]], r_sq)
                else:
                    sq, r_sq = TB(4, (k // 2) % 2)
                    TT(sq[:, LO:TL], yb[:, k, LO:TL], yb[:, k, LO:TL], ALU.mult, [r_yb[k]], r_sq)
                for tt in range(R_out.nt):
                    lo = LO + tt * R_out.n
                    MM(psa[:, g2.base + tt, 0:R_out.n], ones[:, :], sq[:, lo:lo + R_out.n], (k == 0), (k == 7), [r_ones] + r_sq, g2.res)
            rstd, r_rstd = TF(4)
            TS(v3(rstd, R_out), pv(g2, R_out), 1.0 / E, None, ALU.mult, None, g2.res, r_rstd)
            msq, r_msq = TF(5)
            TT(msq[:, LO:TL], mu[:, LO:TL], mu[:, LO:TL], ALU.mult, r_mu, r_msq)
            TT(rstd[:, LO:TL], rstd[:, LO:TL], msq[:, LO:TL], ALU.subtract, r_rstd + r_msq, r_rstd)
            TS(rstd[:, LO:TL], rstd[:, LO:TL], 0.0, LN_EPS, ALU.max, ALU.add, r_rstd, r_rstd)
            ACT(rstd[:, LO:TL], rstd[:, LO:TL], AF.Ln, r_rstd, r_rstd)
            ACT(rstd[:, LO:TL], rstd[:, LO:TL], AF.Exp, r_rstd, r_rstd, scale=-0.5)
            for o in pre:
                cz_norm(*o)
            for j in range(1, 4):
                cz_slot[j] = load_main(wl[:, 8 * E + 2 * j * 128: 8 * E + 2 * j * 128 + 256], 16)
                for i in range(2):
                    cz_norm(*cz_proj(j, i))
            merge_branch(2)

            for q in range(8):
                slot = load_main(w_o[l][:, q * 256:(q + 1) * 256], 16)
                for i in range(2):
                    oc = q * 2 + i
                    g = mm_group(proj_steps(slot, i * 128, acc, r_acc), R_out)
                    TT(v3(h[:, oc, :], R_out), pv(g, R_out), v3(h[:, oc, :], R_out), ALU.add, g.res + [r_h[oc]], [r_h[oc]])
                    ACT(xn[:, oc, LO:TL], h[:, oc, LO:TL], AF.Copy, [r_h[oc]], [r_xn[oc]])

            DMA("pool", pTs, pT[l].rearrange("(k p) t -> p k t", p=128), [], r_dgk, "p")
            last = (l == depth_run - 1)
            if last:
                if R_out.nt == 2:
                    state["pg"] = 6
                gF = next_pg(R_out.nt)
                state["reserved"] = set(range(gF.base, gF.base + R_out.nt))
            for q in range(8):
                pend = []
                if last and q >= 1:
                    for kk in (2 * q - 2, 2 * q - 1):
                        pend.append((kk,) + rms_sq(kk, R_out))
                slot = load_main(w_pg[l][:, q * 256:(q + 1) * 256], 16)
                ss = load_side(w_pp[l][:, q * 256:(q + 1) * 256])
                sgs = []
                for i in range(2):
                    g = mm_group(proj_steps(slot, i * 128, xn, r_xn), R_out)
                    sg, r_sg = TF(i)
                    ACT(v3(sg, R_out), pv(g, R_out), AF.Sigmoid, g.res, r_sg)
                    sgs.append((sg, r_sg))
                for kk, sq, r_sq in pend:
                    rms_mm(gF, kk, R_out, sq, r_sq)
                for i in range(2):
                    oc = q * 2 + i
                    steps = [(side[:, ss, k, i * 128:(i + 1) * 128], pTs[:, k, :], 0, [r_side[ss]] + r_dgk[0:17]) for k in range(2)]
                    g = mm_group(steps, R_out)
                    sg, r_sg = sgs[i]
                    t, r_t = TF(2 + i)
                    TT(v3(t, R_out), pv(g, R_out), v3(sg, R_out), ALU.mult, g.res + r_sg, r_t)
                    TT(h[:, oc, LO:TL], h[:, oc, LO:TL], t[:, LO:TL], ALU.add, [r_h[oc]] + r_t, [r_h[oc]])

        Rf = R_L[depth_run - 1][1]
        for kk in (14, 15):
            rms_step(gF, kk, Rf)
        state["reserved"] = set()
        rs, r_rs = rms_finish(gF, Rf)
        for k in range(16):
            ob, r_ob = TF(1 + (k % 4))
            STT(ob[:, HALO:TL], h[:, k, HALO:TL], pcol(32 + k), rs[:, HALO:TL], ALU.mult, ALU.mult, [r_h[k], r_par] + r_rs, r_ob)
            DMA("sp", outT[k * 128:(k + 1) * 128, :], ob[:, HALO:TL], r_ob, [r_out[k]], f"out{k % 4}")
        S.op("sp", None, None, r_out, [])

        S.emit({"pe": s_pe, "act": s_act, "dve": s_dve, "pool": s_pool, "sp": s_sp}, dsems)
    return nc


def _pack_params(norm_g, conv_a_w, pool_scale, conv_c_w, conv_c_b, ln_c_g, ln_c_b, final_norm_g):
    par = np.zeros((128, NPAR), dtype=np.float32)
    for l in range(DEPTH):
        par[:, l * 16:(l + 1) * 16] = norm_g[l].reshape(16, 128).T
        B = _par_base(l)
        par[:, B:B + 24] = conv_a_w[l].reshape(3, 8, 128).transpose(2, 1, 0).reshape(128, 24)
        par[:, B + 24:B + 32] = pool_scale[l].reshape(8, 128).T
        par[:, B + 32:B + 40] = conv_c_b[l].reshape(8, 128).T
        par[:, B + 40:B + 48] = ln_c_g[l].reshape(8, 128).T
        par[:, B + 48:B + 56] = ln_c_b[l].reshape(8, 128).T
        par[:, B + 56:B + 304] = conv_c_w[l].reshape(31, 8, 128).transpose(2, 1, 0).reshape(128, 248)
    par[:, 32:48] = final_norm_g.reshape(16, 128).T
    par[:, C_RMS_EPS] = RMS_EPS
    par[:, C_RMS_EPS + 1] = LN_EPS
    return par


def _icnt_table(half):
    t = np.zeros((128, 64), dtype=np.float32)
    for g, w in enumerate((2, 4, 8, 16)):
        for i in range(16):
            pos = half * TOK + i
            t[:, g * 16 + i] = np.float32(1.0) / np.float32(min(pos + 1, w))
    return t


_NC_CACHE = {}


def kernel(x, p, norm_g, w_in, conv_a_w, pool_w, pool_scale, conv_c_w, conv_c_b,
           ln_c_g, ln_c_b, w_branch_out, w_o, w_ple_gate, w_ple_proj, final_norm_g):
    f = lambda a: np.ascontiguousarray(np.asarray(a, dtype=np.float32))
    x, p = f(x), f(p)
    w_in, pool_w, w_branch_out, w_o, w_ple_gate, w_ple_proj = map(f, (w_in, pool_w, w_branch_out, w_o, w_ple_gate, w_ple_proj))
    par = _pack_params(*map(f, (norm_g, conv_a_w, pool_scale, conv_c_w, conv_c_b, ln_c_g, ln_c_b, final_norm_g)))
    ident = np.eye(128, dtype=np.float32)
    if "nc" not in _NC_CACHE:
        _NC_CACHE["nc"] = build_program()
    nc = _NC_CACHE["nc"]
    in_maps = []
    for c in range(NCORES):
        b, half = c // 2, c % 2
        xT = np.zeros((D, TL), dtype=np.float32)
        pT = np.zeros((DEPTH, PLE, TL), dtype=np.float32)
        g0 = half * TOK - HALO
        s0 = max(g0, 0)
        xT[:, s0 - g0:] = x[b, s0:half * TOK + TOK, :].T
        for l in range(DEPTH):
            pT[l][:, s0 - g0:] = p[l, b, s0:half * TOK + TOK, :].T
        in_maps.append({
            "xT": xT, "pT": pT, "par": par, "icnt": _icnt_table(half), "ident": ident,
            "w_in": w_in, "pool_w": pool_w, "w_branch_out": w_branch_out, "w_o": w_o,
            "w_ple_gate": w_ple_gate, "w_ple_proj": w_ple_proj,
        })
    res = run_bass_kernel_spmd(nc, in_maps, core_ids=list(range(NCORES)))
    out = np.empty((BATCH, SEQ, D), dtype=np.float32)
    for c in range(NCORES):
        b, half = c // 2, c % 2
        out[b, half * TOK:(half + 1) * TOK, :] = res.results[c]["outT"].T
    return out
```
